# Optimizing a Trainium2 kernel written in Bass

```python
import jax, jax.numpy as jnp
from jax import lax
import numpy as np

D_MODEL = 1024
BATCH = 2
SEQ = 16384
DEPTH = 4
DEC_BATCH = 32
DEC_SEQ = 2048
PAST_LEN = 128

N_MIXERS = 3
D_FF = 2816
EPS = 1e-6
N_MOD = 9
A_HEADS = 8
A_DK = 64
A_DV = 128
A_CHUNK = 128
A_GATE_CAP = 15.0
A_IN = 2 * A_HEADS * A_DK + 2 * A_HEADS * A_DV + 4 * A_HEADS
ATT_HEADS = 16
ATT_KV_HEADS = 4
ATT_GROUP = ATT_HEADS // ATT_KV_HEADS
HEAD_DIM = 64
ATT_IN = (ATT_HEADS + 2 * ATT_KV_HEADS) * HEAD_DIM
WINDOW = 128
BLOCK = 128
ROPE_THETA = 10000.0
GRID_W = 64

N_A = len(range(0, DEPTH, N_MIXERS))
N_B = len(range(1, DEPTH, N_MIXERS))
N_C = len(range(2, DEPTH, N_MIXERS))

kernel_name = "hybrid_bidir_mlstm_swa_axial_encoder"


def rmsnorm(x, w):
    xf = x.astype(jnp.float32)
    y = xf * lax.rsqrt(jnp.mean(xf * xf, axis=-1, keepdims=True) + EPS)
    return (y * w.astype(jnp.float32)).astype(x.dtype)


def modulate(h, shift, scale):
    return h * (1 + scale) + shift


def swiglu(h, w13, w2):
    gate, up = jnp.split(h @ w13, 2, axis=-1)
    return (jax.nn.silu(gate) * up) @ w2


def rope_tables(pos, dim):
    inv = ROPE_THETA ** (-jnp.arange(0, dim, 2, dtype=jnp.float32) / dim)
    ang = pos.astype(jnp.float32)[:, None] * inv[None, :]
    ang = jnp.concatenate([ang, ang], axis=-1)
    return jnp.cos(ang), jnp.sin(ang)


def apply_rope(x, cos, sin):
    x1, x2 = jnp.split(x, 2, axis=-1)
    rot = jnp.concatenate([-x2, x1], axis=-1)
    return x * cos[:, None].astype(x.dtype) + rot * sin[:, None].astype(x.dtype)


def mlstm_scan(q, k, v, ig, lf):
    B, H, S, DK = q.shape
    DV = v.shape[-1]
    n_ch = S // A_CHUNK

    def to_chunks(a):
        a = a.reshape((B, H, n_ch, A_CHUNK) + a.shape[3:])
        return jnp.moveaxis(a, 2, 0)

    xs = tuple(to_chunks(a) for a in (q, k, v, ig, lf))
    lower = jnp.tril(jnp.ones((A_CHUNK, A_CHUNK), dtype=bool))

    def step(carry, inp):
        C, n, m = carry
        qi, ki, vi, ii, fi = inp
        b = jnp.cumsum(fi, axis=-1)
        log_d = b[..., :, None] - b[..., None, :] + ii[..., None, :]
        log_d = jnp.where(lower, log_d, -jnp.inf)
        inter = b + m[..., None]
        m_t = jnp.maximum(inter, jnp.max(log_d, axis=-1))
        dmat = jnp.exp(log_d - m_t[..., None])
        w_inter = jnp.exp(inter - m_t)
        s = jnp.einsum('bhtd,bhsd->bhts', qi, ki) * dmat
        num = jnp.einsum('bhts,bhsv->bhtv', s, vi) + w_inter[..., None] * jnp.einsum('bhtd,bhdv->bhtv', qi, C)
        den = jnp.sum(s, axis=-1) + w_inter * jnp.einsum('bhtd,bhd->bht', qi, n)
        h = num / jnp.maximum(jnp.abs(den), jnp.exp(-m_t))[..., None]
        b_last = b[..., -1]
        log_w = b_last[..., None] - b + ii
        m_new = jnp.maximum(b_last + m, jnp.max(log_w, axis=-1))
        w = jnp.exp(log_w - m_new[..., None])
        decay = jnp.exp(b_last + m - m_new)
        C = decay[..., None, None] * C + jnp.einsum('bhs,bhsd,bhsv->bhdv', w, ki, vi)
        n = decay[..., None] * n + jnp.einsum('bhs,bhsd->bhd', w, ki)
        return (C, n, m_new), h

    init = (jnp.zeros((B, H, DK, DV), jnp.float32), jnp.zeros((B, H, DK), jnp.float32),
            jnp.full((B, H), -jnp.inf, jnp.float32))
    _, hc = lax.scan(step, init, xs)
    return jnp.moveaxis(hc, 0, 2).reshape(B, H, S, DV)


def mlstm_mixer(h, w_in, b_gate, norm_w, w_out):
    B, S, _ = h.shape
    qk = A_HEADS * A_DK
    vd = A_HEADS * A_DV
    proj = h @ w_in
    q, k, v, o, g = jnp.split(proj, [qk, 2 * qk, 2 * qk + vd, 2 * qk + 2 * vd], axis=-1)

    def heads(a, d):
        return a.reshape(B, S, A_HEADS, d).transpose(0, 2, 1, 3).astype(jnp.float32)

    q = heads(q, A_DK) * (A_DK ** -0.5)
    k = heads(k, A_DK)
    v = heads(v, A_DV)
    g = g.astype(jnp.float32) + b_gate.astype(jnp.float32)
    g = A_GATE_CAP * jnp.tanh(g / A_GATE_CAP)
    g = g.reshape(B, S, 4, A_HEADS).transpose(2, 0, 3, 1)
    ig_f, lf_f = g[0], jax.nn.log_sigmoid(g[1])
    ig_b, lf_b = g[2], jax.nn.log_sigmoid(g[3])
    h_f = mlstm_scan(q, k, v, ig_f, lf_f)
    flip = lambda a: jnp.flip(a, axis=2)
    h_b = flip(mlstm_scan(flip(q), flip(k), flip(v), flip(ig_b), flip(lf_b)))
    hs = (h_f + h_b).transpose(0, 2, 1, 3)
    hs = rmsnorm(hs, norm_w.reshape(A_HEADS, A_DV)).reshape(B, S, vd).astype(h.dtype)
    return (hs * jax.nn.sigmoid(o)) @ w_out


def attn_qkv(h, w_in, q_norm, k_norm):
    B, S, _ = h.shape
    proj = h @ w_in
    qd = ATT_HEADS * HEAD_DIM
    kd = ATT_KV_HEADS * HEAD_DIM
    q = proj[..., :qd].reshape(B, S, ATT_HEADS, HEAD_DIM)
    k = proj[..., qd:qd + kd].reshape(B, S, ATT_KV_HEADS, HEAD_DIM)
    v = proj[..., qd + kd:].reshape(B, S, ATT_KV_HEADS, HEAD_DIM)
    return rmsnorm(q, q_norm), rmsnorm(k, k_norm), v


def swa_mixer(h, w_in, q_norm, k_norm, sink, w_out, cos, sin):
    B, S, _ = h.shape
    nb = S // BLOCK
    q, k, v = attn_qkv(h, w_in, q_norm, k_norm)
    q = apply_rope(q, cos, sin)
    k = apply_rope(k, cos, sin)
    qb = jnp.moveaxis(q.reshape(B, nb, BLOCK, ATT_KV_HEADS, ATT_GROUP, HEAD_DIM), 1, 0)

    def windows(a):
        ap = jnp.pad(a, ((0, 0), (BLOCK, BLOCK), (0, 0), (0, 0))).reshape(B, nb + 2, BLOCK, ATT_KV_HEADS, HEAD_DIM)
        aw = jnp.concatenate([ap[:, :-2], ap[:, 1:-1], ap[:, 2:]], axis=2)
        return jnp.moveaxis(aw, 1, 0)

    kw, vw = windows(k), windows(v)
    qi = jnp.arange(BLOCK)[:, None]
    kj = jnp.arange(3 * BLOCK)[None, :] - BLOCK
    band = jnp.abs(qi - kj) <= WINDOW
    kpos = jnp.arange(nb)[:, None] * BLOCK - BLOCK + jnp.arange(3 * BLOCK)[None, :]
    kvalid = (kpos >= 0) & (kpos < S)
    sink_l = sink.astype(jnp.float32).reshape(ATT_KV_HEADS, ATT_GROUP)[None, :, :, None, None]
    scale = HEAD_DIM ** -0.5

    def block(args):
        qblk, kblk, vblk, valid = args
        s = jnp.einsum('bqkgd,bskd->bkgqs', qblk, kblk).astype(jnp.float32) * scale
        s = jnp.where(band & valid[None, :], s, -jnp.inf)
        sk = jnp.broadcast_to(sink_l, s.shape[:-1] + (1,))
        p = jax.nn.softmax(jnp.concatenate([s, sk], axis=-1), axis=-1)[..., :-1]
        return jnp.einsum('bkgqs,bskd->bqkgd', p.astype(vblk.dtype), vblk)

    o = lax.map(block, (qb, kw, vw, kvalid))
    o = jnp.moveaxis(o, 0, 1).reshape(B, S, ATT_HEADS * HEAD_DIM)
    return o @ w_out


def axial_mixer(h, w_in, q_norm, k_norm, w_out, row_cs, col_cs):
    B, S, _ = h.shape
    nb = S // BLOCK
    half = HEAD_DIM // 2
    q, k, v = attn_qkv(h, w_in, q_norm, k_norm)

    def axial(a):
        return jnp.concatenate([apply_rope(a[..., :half], *row_cs), apply_rope(a[..., half:], *col_cs)], axis=-1)

    q, k = axial(q), axial(k)
    qb = jnp.moveaxis(q.reshape(B, nb, BLOCK, ATT_KV_HEADS, ATT_GROUP, HEAD_DIM), 1, 0)
    scale = HEAD_DIM ** -0.5

    def block(qblk):
        s = jnp.einsum('bqkgd,bskd->bkgqs', qblk, k).astype(jnp.float32) * scale
        p = jax.nn.softmax(s, axis=-1)
        return jnp.einsum('bkgqs,bskd->bqkgd', p.astype(v.dtype), v)

    o = lax.map(block, qb)
    o = jnp.moveaxis(o, 0, 1).reshape(B, S, ATT_HEADS * HEAD_DIM)
    return o @ w_out


def trunk(x, c, ffn_w13, ffn_w2, ada_w, ada_b, norm_w,
          mlstm_w_in, mlstm_b_gate, mlstm_norm_w, mlstm_w_out,
          swa_w_in, swa_q_norm, swa_k_norm, swa_sink, swa_w_out,
          axial_w_in, axial_q_norm, axial_k_norm, axial_w_out):
    B, S, _ = x.shape
    rope_cos, rope_sin = rope_tables(jnp.arange(S), HEAD_DIM)
    rows = S // GRID_W
    row_ids = jnp.repeat(jnp.arange(rows), GRID_W)
    col_ids = jnp.tile(jnp.arange(GRID_W), rows)
    row_cs = rope_tables(row_ids, HEAD_DIM // 2)
    col_cs = rope_tables(col_ids, HEAD_DIM // 2)
    c_act = jax.nn.silu(c)
    for i in range(DEPTH):
        mod = (c_act @ ada_w[i] + ada_b[i])[:, None, :]
        sh1, sc1, g1, sh2, sc2, g2, sh3, sc3, g3 = jnp.split(mod, N_MOD, axis=-1)
        h = modulate(rmsnorm(x, norm_w[i, 0]), sh1, sc1)
        x = x + 0.5 * g1 * swiglu(h, ffn_w13[i, 0], ffn_w2[i, 0])
        h = modulate(rmsnorm(x, norm_w[i, 1]), sh2, sc2)
        kind, j = i % N_MIXERS, i // N_MIXERS
        if kind == 0:
            mix = mlstm_mixer(h, mlstm_w_in[j], mlstm_b_gate[j], mlstm_norm_w[j], mlstm_w_out[j])
        elif kind == 1:
            mix = swa_mixer(h, swa_w_in[j], swa_q_norm[j], swa_k_norm[j], swa_sink[j], swa_w_out[j], rope_cos, rope_sin)
        else:
            mix = axial_mixer(h, axial_w_in[j], axial_q_norm[j], axial_k_norm[j], axial_w_out[j], row_cs, col_cs)
        x = x + g2 * mix
        h = modulate(rmsnorm(x, norm_w[i, 2]), sh3, sc3)
        x = x + 0.5 * g3 * swiglu(h, ffn_w13[i, 1], ffn_w2[i, 1])
    return x


def setup_inputs(seed: int = 0) -> dict:
    key = jax.random.key(seed)
    ks = jax.random.split(key, 24)
    f32 = jnp.float32
    nrm = lambda k, shape, s: jax.random.normal(k, shape, f32) * s
    gate_base = jnp.repeat(jnp.array([0.0, 3.0, 0.0, 3.0], f32), A_HEADS)
    return {
        "x_prompt": nrm(ks[0], (BATCH, SEQ, D_MODEL), 1.0),
        "x_sample": nrm(ks[1], (DEC_BATCH, DEC_SEQ, D_MODEL), 1.0),
        "c_prompt": nrm(ks[2], (BATCH, D_MODEL), 1.0),
        "c_sample": nrm(ks[3], (DEC_BATCH, D_MODEL), 1.0),
        "ffn_w13": nrm(ks[4], (DEPTH, 2, D_MODEL, 2 * D_FF), D_MODEL ** -0.5),
        "ffn_w2": nrm(ks[5], (DEPTH, 2, D_FF, D_MODEL), D_FF ** -0.5),
        "ada_w": nrm(ks[6], (DEPTH, D_MODEL, N_MOD * D_MODEL), 0.5 * D_MODEL ** -0.5),
        "ada_b": nrm(ks[7], (DEPTH, N_MOD * D_MODEL), 0.02),
        "norm_w": 1.0 + nrm(ks[8], (DEPTH, 3, D_MODEL), 0.02),
        "mlstm_w_in": nrm(ks[9], (N_A, D_MODEL, A_IN), D_MODEL ** -0.5),
        "mlstm_b_gate": gate_base[None, :] + nrm(ks[10], (N_A, 4 * A_HEADS), 0.1),
        "mlstm_norm_w": 1.0 + nrm(ks[11], (N_A, A_HEADS * A_DV), 0.02),
        "mlstm_w_out": nrm(ks[12], (N_A, A_HEADS * A_DV, D_MODEL), (A_HEADS * A_DV) ** -0.5),
        "swa_w_in": nrm(ks[13], (N_B, D_MODEL, ATT_IN), D_MODEL ** -0.5),
        "swa_q_norm": 1.0 + nrm(ks[14], (N_B, HEAD_DIM), 0.02),
        "swa_k_norm": 1.0 + nrm(ks[15], (N_B, HEAD_DIM), 0.02),
        "swa_sink": nrm(ks[16], (N_B, ATT_HEADS), 0.5),
        "swa_w_out": nrm(ks[17], (N_B, ATT_HEADS * HEAD_DIM, D_MODEL), (ATT_HEADS * HEAD_DIM) ** -0.5),
        "axial_w_in": nrm(ks[18], (N_C, D_MODEL, ATT_IN), D_MODEL ** -0.5),
        "axial_q_norm": 1.0 + nrm(ks[19], (N_C, HEAD_DIM), 0.02),
        "axial_k_norm": 1.0 + nrm(ks[20], (N_C, HEAD_DIM), 0.02),
        "axial_w_out": nrm(ks[21], (N_C, ATT_HEADS * HEAD_DIM, D_MODEL), (ATT_HEADS * HEAD_DIM) ** -0.5),
    }


def reference(x_prompt, x_sample, c_prompt, c_sample, ffn_w13, ffn_w2, ada_w, ada_b, norm_w,
              mlstm_w_in, mlstm_b_gate, mlstm_norm_w, mlstm_w_out,
              swa_w_in, swa_q_norm, swa_k_norm, swa_sink, swa_w_out,
              axial_w_in, axial_q_norm, axial_k_norm, axial_w_out):
    params = (ffn_w13, ffn_w2, ada_w, ada_b, norm_w,
              mlstm_w_in, mlstm_b_gate, mlstm_norm_w, mlstm_w_out,
              swa_w_in, swa_q_norm, swa_k_norm, swa_sink, swa_w_out,
              axial_w_in, axial_q_norm, axial_k_norm, axial_w_out)
    y_prompt = trunk(x_prompt, c_prompt, *params)
    y_sample = trunk(x_sample, c_sample, *params)
    return (y_prompt, y_sample)
```

```python
import bisect
import os
import contextlib
import numpy as np
import concourse.bass as bass
import concourse.mybir as mybir
from concourse.bass_utils import run_bass_kernel_spmd

F32 = mybir.dt.float32
BF16 = mybir.dt.bfloat16
AF = mybir.ActivationFunctionType
ALU = mybir.AluOpType
AX = mybir.AxisListType

D = 1024
DFF = 2816
NFC = 22
KC = 8
EPS = 1e-6
NEG = -30000.0
NDS = 56


class Eng:
    def __init__(s, name, h, sem):
        s.name, s.h, s.sem = name, h, sem
        s.cnt = 0
        s.idx = 0
        s.last = None
        s.sig_idx = []
        s.sig_cnt = []
        s.seen = {}


class DSem:
    def __init__(s, h, key):
        s.h, s.key, s.count = h, key, 0


class Buf:
    def __init__(s, name):
        s.name = name
        s.w = None
        s.r = {}
        s.dsem = None


class K:
    def __init__(s, nc, es):
        s.nc = nc
        mk = lambda n: es.enter_context(nc.semaphore(n))
        s.pe = Eng("pe", nc.tensor, mk("s_pe"))
        s.act = Eng("act", nc.scalar, mk("s_act"))
        s.dve = Eng("dve", nc.vector, mk("s_dve"))
        s.pool = Eng("pool", nc.gpsimd, mk("s_pool"))
        s.sp = Eng("sp", nc.sync, mk("s_sp"))
        s.engs = [s.pe, s.act, s.dve, s.pool, s.sp]
        s.dsems = [DSem(mk("d%d" % i), "d%d" % i) for i in range(NDS)]
        s.free = list(s.dsems)
        s.pbufs = []
        s.used = []

    def buf(s, name):
        b = Buf(name)
        s.pbufs.append(b)
        return b

    def _wait(s, eng, ev):
        if ev[0] == "e":
            e2, idx = ev[1], ev[2]
            if e2 is eng and eng is s.pe:
                return
            i = bisect.bisect_left(e2.sig_idx, idx)
            if i < len(e2.sig_idx):
                c = e2.sig_cnt[i]
            else:
                assert e2.idx >= idx and e2.last is not None
                e2.last.then_inc(e2.sem, 1)
                e2.cnt += 1
                e2.sig_idx.append(e2.idx)
                e2.sig_cnt.append(e2.cnt)
                c = e2.cnt
            if eng.seen.get(e2.name, 0) >= c:
                return
            eng.h.wait_ge(e2.sem, c)
            eng.seen[e2.name] = c
        else:
            ds, val = ev[1], ev[2]
            if eng.seen.get(ds.key, 0) >= val:
                return
            eng.h.wait_ge(ds.h, val)
            eng.seen[ds.key] = val

    def _deps(s, eng, r, w):
        for b in r:
            if b.w is not None:
                s._wait(eng, b.w)
        for b in w:
            if b.w is not None:
                s._wait(eng, b.w)
            for ev in b.r.values():
                s._wait(eng, ev)

    def op(s, eng, fn, r=(), w=()):
        s._deps(eng, r, w)
        ins = fn()
        eng.idx += 1
        eng.last = ins
        ev = ("e", eng, eng.idx)
        for b in r:
            b.r[eng.name] = ev
        for b in w:
            b.w = ev
            b.r = {}
        return ins

    def dma(s, pairs, r=(), w=(), q=None):
        q = q or s.sp
        s._deps(q, r, w)
        owner = w[0] if len(w) else r[0]
        if owner.dsem is None:
            owner.dsem = s.free.pop()
            s.used.append(owner.dsem)
        ds = owner.dsem
        for (o, i) in pairs:
            q.h.dma_start(out=o, in_=i).then_inc(ds.h, 16)
            ds.count += 16
        ev = ("d", ds, ds.count)
        for b in r:
            b.r["dma_" + ds.key] = ev
        for b in w:
            b.w = ev
            b.r = {}

    def barrier(s):
        for e in s.engs:
            for e2 in s.engs:
                if e2 is not e and e2.idx > 0 and e2 is not s.sp:
                    s._wait(e, ("e", e2, e2.idx))
            for ds in s.used:
                if ds.count > 0:
                    s._wait(e, ("d", ds, ds.count))
        s.free = list(s.dsems)
        s.used = []
        s.pbufs = []


def bc(ap, shape):
    return ap.to_broadcast(list(shape))


def build(T, NL=4, stop=None):
    NT = T // 128
    SLOT = T // 8
    BPG = SLOT // 128
    GT = 256
    NG = T // GT
    TPG = GT // 128
    nc = bass.Bass("TRN2", target_bir_lowering=False)
    dt_in = lambda name, shape, dt=F32: nc.dram_tensor(name, list(shape), dt, kind="ExternalInput").ap()
    dt_sc = lambda name, shape, dt=F32: nc.dram_tensor(name, list(shape), dt, kind="Internal").ap()
    x_in = dt_in("x", [T, D])
    c8 = dt_in("c8", [8, D])
    ffn_w13 = dt_in("ffn_w13", [4, 2, D, 2 * DFF])
    ffn_w2 = dt_in("ffn_w2", [4, 2, DFF, D])
    ada_w = dt_in("ada_w", [4, D, 9 * D])
    ada_b = dt_in("ada_b", [4, 9 * D])
    norm_w = dt_in("norm_w", [4, 3, D])
    ml_w_in = dt_in("mlstm_w_in", [2, D, 3104])
    ml_bg = dt_in("mlstm_b_gate", [2, 32])
    ml_nw = dt_in("mlstm_norm_w", [2, D])
    ml_w_out = dt_in("mlstm_w_out", [2, D, D])
    at_w_in = [dt_in("swa_w_in", [1, D, 1536]), dt_in("axial_w_in", [1, D, 1536])]
    at_qn = [dt_in("swa_q_norm", [1, 64]), dt_in("axial_q_norm", [1, 64])]
    at_kn = [dt_in("swa_k_norm", [1, 64]), dt_in("axial_k_norm", [1, 64])]
    swa_sink = dt_in("swa_sink", [1, 16])
    at_w_out = [dt_in("swa_w_out", [1, D, D]), dt_in("axial_w_out", [1, D, D])]
    ropec = [dt_in("rope_swa_c", [T, 64]), dt_in("rope_ax_c", [T, 64])]
    ropes = [dt_in("rope_swa_s", [T, 64]), dt_in("rope_ax_s", [T, 64])]
    keepf_d = dt_in("keepf", [128, NT])
    keepb_d = dt_in("keepb", [128, NT])
    swab_d = dt_in("swab", [128, NT * 2])
    amask_d = dt_in("amask", [128, 64])
    y_out = nc.dram_tensor("y", [T, D], F32, kind="ExternalOutput").ap()
    xs = dt_sc("xs", [T, D])
    MR = dt_sc("MR", [NL, 9, 8, 128, D])
    QTK = dt_sc("QTK", [10, 128, T], BF16)
    Vd = dt_sc("Vd", [4, 128, NT, 64], BF16)
    OTd = dt_sc("OTd", [16, 64, T], BF16)
    MQK = dt_sc("MQK", [8, 128, T], BF16)
    Ktm = dt_sc("Ktm", [T, 512])
    Vm = dt_sc("Vm", [T, D], BF16)
    SOd = dt_sc("SOd", [T, D])
    GTd = dt_sc("GTd", [T, 32])
    Hf = dt_sc("Hf", [T, D])
    HGd = dt_sc("HGd", [T, D], BF16)

    top = contextlib.ExitStack()
    with top:
        k = K(nc, top)
        pe, act, dve, pool = k.pe, k.act, k.dve, k.pool
        uid = [0]

        def TT(es, name, shape, dt=F32):
            uid[0] += 1
            return es.enter_context(nc.sbuf_tensor("%s_u%d" % (name, uid[0]), list(shape), dt))

        def PP(es, name, shape, dt=F32):
            uid[0] += 1
            return es.enter_context(nc.psum_tensor("%s_u%d" % (name, uid[0]), list(shape), dt))

        identb = TT(top, "identb", [128, 128], BF16)
        identf = TT(top, "identf", [128, 128])
        TRIi = TT(top, "TRIi", [128, 128])
        TRIr = TT(top, "TRIr", [128, 128])
        ONESf = TT(top, "ONESf", [128, 128])
        Sel = TT(top, "Sel", [8, 8, 128])
        E65 = TT(top, "E65", [65, 64])
        nhalf = TT(top, "nhalf", [128, 32])
        cb = k.buf("consts")

        def mkmask(t, pattern, cmul, cmp, base=0):
            k.op(pool, lambda: nc.gpsimd.memset(t, 1.0), w=[cb])
            k.op(pool, lambda: nc.gpsimd.affine_select(out=t, in_=t, pattern=pattern, compare_op=cmp, fill=0.0,
                                                       base=base, channel_multiplier=cmul), w=[cb])
        mkmask(identf[:], [[-1, 128]], 1, ALU.is_equal)
        mkmask(TRIi[:], [[1, 128]], -1, ALU.is_ge)
        mkmask(TRIr[:], [[-1, 128]], 1, ALU.is_ge)
        mkmask(Sel[:], [[-1, 8], [0, 128]], 1, ALU.is_equal)
        mkmask(E65[:], [[0, 64]], 1, ALU.is_equal, base=-64)
        k.op(pool, lambda: nc.gpsimd.memset(ONESf[:], 1.0), w=[cb])
        k.op(pool, lambda: nc.gpsimd.memset(nhalf[:], -0.5), w=[cb])
        k.op(dve, lambda: nc.vector.tensor_copy(out=identb[:], in_=identf[:]), r=[cb], w=[cb])
        k.barrier()

        state = {"src": x_in, "nph": 0}

        def phase_done():
            k.barrier()
            state["nph"] += 1
            return stop is not None and state["nph"] >= stop

        def load_weight(es, dst, dstbuf, pieces):
            nmax = max(p[1].shape[-1] for p in pieces)
            stg = [TT(es, "wstg%d" % i, [128, nmax]) for i in range(3)]
            sb = [k.buf("wstg%d" % i) for i in range(3)]
            cv = [dve, pool, act]
            for i, (d_ap, s_ap) in enumerate(pieces):
                P, n = s_ap.shape[0], s_ap.shape[-1]
                j = i % 3
                k.dma([(stg[j][0:P, 0:n], s_ap)], w=[sb[j]])
                e = cv[i % 3]
                if e is act:
                    k.op(act, lambda: nc.scalar.copy(out=d_ap, in_=stg[j][0:P, 0:n]), r=[sb[j]], w=[dstbuf])
                else:
                    k.op(e, lambda: e.h.tensor_copy(out=d_ap, in_=stg[j][0:P, 0:n]), r=[sb[j]], w=[dstbuf])

        class NormCtx:
            def __init__(s, es, nb=2):
                s.hb = [TT(es, "n_hb%d" % i, [128, D], BF16) for i in range(nb)]
                s.hbb = [k.buf("n_hb%d" % i) for i in range(nb)]
                s.sm = [TT(es, "n_sm%d" % i, [128, 4]) for i in range(nb)]
                s.smb = [k.buf("n_sm%d" % i) for i in range(nb)]
                s.pT = [PP(es, "n_pT%d" % i, [128, 8, 128], BF16) for i in range(2)]
                s.pTb = [k.buf("n_pT%d" % i) for i in range(2)]
                s.n = 0

            def part1(s, xa, xab, A, Ab, B, Bb):
                i = s.n % len(s.hb)
                s.cur = i
                hb, hbb, sm, smb = s.hb[i], s.hbb[i], s.sm[i], s.smb[i]
                k.op(act, lambda: nc.scalar.activation(out=hb[:], in_=xa, func=AF.Square, accum_out=sm[:, 0:1]),
                     r=[xab], w=[hbb, smb])
                k.op(dve, lambda: nc.vector.tensor_scalar(out=sm[:, 1:2], in0=sm[:, 0:1], scalar1=1.0 / D, scalar2=EPS,
                                                          op0=ALU.mult, op1=ALU.add), r=[smb], w=[smb])
                k.op(pool, lambda: nc.gpsimd.tensor_tensor(out=sm[:, 2:3], in0=sm[:, 1:2], in1=nhalf[:, 0:1], op=ALU.pow),
                     r=[smb], w=[smb])
                k.op(dve, lambda: nc.vector.scalar_tensor_tensor(out=xa, in0=xa, scalar=sm[:, 2:3], in1=A,
                                                                 op0=ALU.mult, op1=ALU.mult), r=[smb, Ab, xab], w=[xab])
                k.op(pool, lambda: nc.gpsimd.tensor_tensor(out=hb[:], in0=xa, in1=B, op=ALU.add), r=[xab, Bb], w=[hbb])

            def part2(s, hT, hTb):
                i = s.cur
                j = s.n % 2
                s.n += 1
                for kc in range(KC):
                    k.op(pe, lambda: nc.tensor.transpose(s.pT[j][:, kc, :], s.hb[i][:, kc * 128:(kc + 1) * 128], identb[:]),
                         r=[s.hbb[i]], w=[s.pTb[j]])
                k.op(act, lambda: nc.scalar.copy(out=hT, in_=s.pT[j][:]), r=[s.pTb[j]], w=[hTb])

        class ModTiles:
            def __init__(s, es, l, ms, pfx):
                s.l, s.ms = l, ms
                s.t = {m: TT(es, "%s_mod%d" % (pfx, m), [128, D]) for m in ms}
                s.b = {m: k.buf("%s_mod%d" % (pfx, m)) for m in ms}
                s.slot = -1

            def need(s, slot):
                if slot != s.slot:
                    s.slot = slot
                    for m in s.ms:
                        k.dma([(s.t[m][:], MR[s.l, m, slot])], w=[s.b[m]])

        class Epilogue:
            def __init__(s, es, nb=2):
                s.xb = [TT(es, "e_xb%d" % i, [128, D]) for i in range(nb)]
                s.xbb = [k.buf("e_xb%d" % i) for i in range(nb)]
                s.n = 0

            def load(s, src, tok0):
                i = s.n % len(s.xb)
                k.dma([(s.xb[i][:], src[tok0:tok0 + 128, :])], w=[s.xbb[i]])
                return i

            def apply(s, i, py, pyb, G, Gb, dst, tok0):
                for h in range(2):
                    k.op(dve, lambda: nc.vector.tensor_tensor(out=py[h][:], in0=py[h][:], in1=G[:, h * 512:(h + 1) * 512],
                                                              op=ALU.mult), r=[Gb], w=[pyb[h]])
                    k.op(dve, lambda: nc.vector.tensor_tensor(out=s.xb[i][:, h * 512:(h + 1) * 512],
                                                              in0=s.xb[i][:, h * 512:(h + 1) * 512], in1=py[h][:], op=ALU.add),
                         r=[pyb[h]], w=[s.xbb[i]])
                k.dma([(dst[tok0:tok0 + 128, :], s.xb[i][:])], r=[s.xbb[i]])
                s.n += 1

        def phase_mod():
            es = contextlib.ExitStack()
            with es:
                c8t = TT(es, "c8t", [8, D]); c8b = k.buf("c8t")
                c8s = TT(es, "c8s", [8, D], BF16)
                csT = TT(es, "csT", [128, KC, 8], BF16); csTb = k.buf("csT")
                csrep = TT(es, "csrep", [128, 8, KC, 128], BF16); csrb = k.buf("csrep")
                pcs = PP(es, "pcs", [128, KC, 8], BF16); pcsb = k.buf("pcs")
                k.dma([(c8t[:], c8[:, :])], w=[c8b])
                k.op(act, lambda: nc.scalar.activation(out=c8s[:], in_=c8t[:], func=AF.Silu), r=[c8b], w=[c8b])
                for kc in range(KC):
                    k.op(pe, lambda: nc.tensor.transpose(pcs[:, kc, :], c8s[0:8, kc * 128:(kc + 1) * 128], identb[0:8, 0:8]),
                         r=[c8b], w=[pcsb])
                k.op(dve, lambda: nc.vector.tensor_copy(out=csT[:], in_=pcs[:]), r=[pcsb], w=[csTb])
                for sl in range(8):
                    k.op(dve, lambda: nc.vector.tensor_copy(out=csrep[:, sl], in_=bc(csT[:, :, sl:sl + 1], [128, KC, 128])),
                         r=[csTb], w=[csrb])
                stg = [TT(es, "m_stg%d" % i, [128, KC, 512]) for i in range(2)]
                stgb = [k.buf("m_stg%d" % i) for i in range(2)]
                wst = [TT(es, "m_wst%d" % i, [128, KC, 512], BF16) for i in range(2)]
                wstb = [k.buf("m_wst%d" % i) for i in range(2)]
                adb = [TT(es, "m_adb%d" % i, [128, 512]) for i in range(2)]
                adbb = [k.buf("m_adb%d" % i) for i in range(2)]
                nwr = TT(es, "m_nwr", [128, 3, D]); nwrb = k.buf("m_nwr")
                mo = [TT(es, "m_mo%d" % i, [128, 512]) for i in range(3)]
                mob = [k.buf("m_mo%d" % i) for i in range(3)]
                pm = [PP(es, "m_pm%d" % i, [128, 512]) for i in range(3)]
                pmb = [k.buf("m_pm%d" % i) for i in range(3)]
                n = 0
                cgi = 0
                for l in range(NL):
                    k.dma([(nwr[:, j, :], norm_w[l, j:j + 1, :].partition_broadcast(128)) for j in range(3)], w=[nwrb])
                    for m in range(9):
                        for half in range(2):
                            c0 = m * D + half * 512
                            j = cgi % 2
                            cgi += 1
                            k.dma([(stg[j][:], ada_w[l, :, c0:c0 + 512].rearrange("(kc p) n -> p kc n", p=128))], w=[stgb[j]])
                            k.dma([(adb[j][:], ada_b[l:l + 1, c0:c0 + 512].partition_broadcast(128))], w=[adbb[j]])
                            k.op(pool, lambda: nc.gpsimd.tensor_copy(out=wst[j][:], in_=stg[j][:]), r=[stgb[j]], w=[wstb[j]])
                            for sl in range(8):
                                q = n % 3
                                n += 1
                                for kc in range(KC):
                                    k.op(pe, lambda: nc.tensor.matmul(pm[q][:], csrep[:, sl, kc, :], wst[j][:, kc, :],
                                                                      start=(kc == 0), stop=(kc == KC - 1)),
                                         r=[csrb, wstb[j]], w=[pmb[q]])
                                k.op(dve, lambda: nc.vector.tensor_tensor(out=mo[q][:], in0=pm[q][:], in1=adb[j][:], op=ALU.add),
                                     r=[pmb[q], adbb[j]], w=[mob[q]])
                                if m in (1, 4, 7):
                                    k.op(dve, lambda: nc.vector.scalar_tensor_tensor(
                                        out=mo[q][:], in0=mo[q][:], scalar=1.0, in1=nwr[:, m // 3, half * 512:(half + 1) * 512],
                                        op0=ALU.add, op1=ALU.mult), r=[nwrb], w=[mob[q]])
                                elif m in (2, 8):
                                    k.op(dve, lambda: nc.vector.tensor_scalar(out=mo[q][:], in0=mo[q][:], scalar1=0.5, scalar2=None,
                                                                              op0=ALU.mult), w=[mob[q]])
                                k.dma([(MR[l, m, sl, :, half * 512:(half + 1) * 512], mo[q][:])], r=[mob[q]])
            return phase_done()

        def phase_ffn(l, which, dst):
            src = state["src"]
            mi = 0 if which == 0 else 6
            es = contextlib.ExitStack()
            with es:
                W13 = TT(es, "W13", [128, KC, 2 * DFF], BF16); W13b = k.buf("W13")
                W2 = TT(es, "W2", [128, NFC, D], BF16); W2b = k.buf("W2")
                es2 = contextlib.ExitStack()
                with es2:
                    pieces = []
                    for kc in range(KC):
                        for c in range(4):
                            pieces.append((W13[:, kc, c * 1408:(c + 1) * 1408],
                                           ffn_w13[l, which, kc * 128:(kc + 1) * 128, c * 1408:(c + 1) * 1408]))
                    for fc in range(NFC):
                        pieces.append((W2[:, fc, :], ffn_w2[l, which, fc * 128:(fc + 1) * 128, :]))
                    wb = k.buf("Wall")
                    load_weight(es2, None, wb, pieces)
                    k.barrier()
                xa = [TT(es, "f_xa%d" % i, [128, D]) for i in range(2)]
                xab = [k.buf("f_xa%d" % i) for i in range(2)]
                nctx = NormCtx(es)
                mods = ModTiles(es, l, [mi, mi + 1], "f")
                modg = ModTiles(es, l, [mi + 2], "fg")
                hT = [TT(es, "f_hT%d" % i, [128, KC, GT], BF16) for i in range(2)]
                hTb = [k.buf("f_hT%d" % i) for i in range(2)]
                sg = [TT(es, "f_sg%d" % i, [128, GT]) for i in range(2)]
                sgb = [k.buf("f_sg%d" % i) for i in range(2)]
                uT = TT(es, "f_uT", [128, NFC, GT], BF16); uTb = k.buf("f_uT")
                ep = Epilogue(es)
                pg = [PP(es, "f_pg%d" % i, [128, GT]) for i in range(2)]
                pgb = [k.buf("f_pg%d" % i) for i in range(2)]
                pu = [PP(es, "f_pu%d" % i, [128, GT]) for i in range(2)]
                pub = [k.buf("f_pu%d" % i) for i in range(2)]
                py = [PP(es, "f_py%d" % i, [128, 512]) for i in range(2)]
                pyb = [k.buf("f_py%d" % i) for i in range(2)]
                cnt = {"xa": 0}

                def norm1(g):
                    mods.need((g * GT) // SLOT)
                    pend = []
                    for tt in range(TPG):
                        i = cnt["xa"] % 2
                        cnt["xa"] += 1
                        tok0 = g * GT + tt * 128
                        k.dma([(xa[i][:], src[tok0:tok0 + 128, :])], w=[xab[i]])
                        nctx.part1(xa[i][:], xab[i], mods.t[mi + 1][:], mods.b[mi + 1], mods.t[mi][:], mods.b[mi])
                        nctx.part2(hT[g % 2][:, :, tt * 128:(tt + 1) * 128], hTb[g % 2])

                norm1(0)
                for g in range(NG):
                    h_ = hT[g % 2]
                    for fc in range(NFC):
                        q = fc % 2
                        for kc in range(KC):
                            k.op(pe, lambda: nc.tensor.matmul(pg[q][:], W13[:, kc, fc * 128:(fc + 1) * 128], h_[:, kc, :],
                                                              start=(kc == 0), stop=(kc == KC - 1)), r=[hTb[g % 2]], w=[pgb[q]])
                        for kc in range(KC):
                            k.op(pe, lambda: nc.tensor.matmul(pu[q][:], W13[:, kc, DFF + fc * 128:DFF + (fc + 1) * 128], h_[:, kc, :],
                                                              start=(kc == 0), stop=(kc == KC - 1)), r=[hTb[g % 2]], w=[pub[q]])
                        k.op(act, lambda: nc.scalar.activation(out=sg[q][:], in_=pg[q][:], func=AF.Silu), r=[pgb[q]], w=[sgb[q]])
                        k.op(dve, lambda: nc.vector.tensor_tensor(out=uT[:, fc, :], in0=sg[q][:], in1=pu[q][:], op=ALU.mult),
                             r=[sgb[q], pub[q]], w=[uTb])
                        if fc == 10 and g + 1 < NG:
                            norm1(g + 1)
                    modg.need((g * GT) // SLOT)
                    for tt in range(TPG):
                        tok0 = g * GT + tt * 128
                        xi = ep.load(src, tok0)
                        for h in range(2):
                            for fc in range(NFC):
                                k.op(pe, lambda: nc.tensor.matmul(py[h][:], uT[:, fc, tt * 128:(tt + 1) * 128],
                                                                  W2[:, fc, h * 512:(h + 1) * 512],
                                                                  start=(fc == 0), stop=(fc == NFC - 1)), r=[uTb], w=[pyb[h]])
                        ep.apply(xi, py, pyb, modg.t[mi + 2], modg.b[mi + 2], dst, tok0)
            state["src"] = dst
            return phase_done()

        def phase_att_in(l, kind):
            src = state["src"]
            es = contextlib.ExitStack()
            with es:
                Win = TT(es, "a_Win", [128, KC, 1536], BF16); Winb = k.buf("a_Win")
                es2 = contextlib.ExitStack()
                with es2:
                    pieces = [(Win[:, kc, :], at_w_in[kind][0, kc * 128:(kc + 1) * 128, :]) for kc in range(KC)]
                    load_weight(es2, None, Winb, pieces)
                    k.barrier()
                nwr = TT(es, "a_nwr", [128, 20, 64]); nwrb = k.buf("a_nwr")
                nws = TT(es, "a_nws", [128, 2, 64])
                k.dma([(nws[:, 0, :], at_qn[kind][0:1, :].partition_broadcast(128)),
                       (nws[:, 1, :], at_kn[kind][0:1, :].partition_broadcast(128))], w=[nwrb])
                k.op(dve, lambda: nc.vector.tensor_scalar(out=nwr[:, 0:16, :], in0=bc(nws[:, 0:1, :], [128, 16, 64]), scalar1=0.125,
                                                          scalar2=None, op0=ALU.mult), r=[nwrb], w=[nwrb])
                k.op(dve, lambda: nc.vector.tensor_copy(out=nwr[:, 16:20, :], in_=bc(nws[:, 1:2, :], [128, 4, 64])), r=[nwrb], w=[nwrb])
                xa = [TT(es, "a_xa%d" % i, [128, D]) for i in range(2)]
                xab = [k.buf("a_xa%d" % i) for i in range(2)]
                nctx = NormCtx(es)
                mods = ModTiles(es, l, [3, 4], "a")
                hT = [TT(es, "a_hT%d" % i, [128, KC, 128], BF16) for i in range(2)]
                hTb = [k.buf("a_hT%d" % i) for i in range(2)]
                pq = [PP(es, "a_pq%d" % i, [128, 512]) for i in range(3)]
                pqb = [k.buf("a_pq%d" % i) for i in range(3)]
                ptq = PP(es, "a_ptq", [128, 8, 128], BF16); ptqb = k.buf("a_ptq")
                ptk = PP(es, "a_ptk", [128, 2, 128], BF16); ptkb = k.buf("a_ptk")
                NB_ = 2
                qk = [TT(es, "a_qk%d" % i, [128, 20, 64]) for i in range(NB_)]
                qkb = [k.buf("a_qk%d" % i) for i in range(NB_)]
                sq = [TT(es, "a_sq%d" % i, [128, 20, 64]) for i in range(NB_)]
                sqb = [k.buf("a_sq%d" % i) for i in range(NB_)]
                t2 = [TT(es, "a_t2%d" % i, [128, 20, 64]) for i in range(NB_)]
                t2b = [k.buf("a_t2%d" % i) for i in range(NB_)]
                qr = [TT(es, "a_qr%d" % i, [128, 1280], BF16) for i in range(NB_)]
                qrb = [k.buf("a_qr%d" % i) for i in range(NB_)]
                vb = [TT(es, "a_vb%d" % i, [128, 4, 64], BF16) for i in range(NB_)]
                vbb = [k.buf("a_vb%d" % i) for i in range(NB_)]
                sm = [TT(es, "a_sm%d" % i, [128, 3, 20]) for i in range(NB_)]
                smb = [k.buf("a_sm%d" % i) for i in range(NB_)]
                cs = [TT(es, "a_cs%d" % i, [128, 2, 64]) for i in range(NB_)]
                csb = [k.buf("a_cs%d" % i) for i in range(NB_)]
                qT = [TT(es, "a_qT%d" % i, [128, 10, 128], BF16) for i in range(NB_)]
                qTb = [k.buf("a_qT%d" % i) for i in range(NB_)]
                hbk = 32 if kind == 0 else 16
                nbk = 64 // (2 * hbk)
                for t in range(NT):
                    i = t % 2
                    tok0 = t * 128
                    mods.need(tok0 // SLOT)
                    k.dma([(xa[i][:], src[tok0:tok0 + 128, :])], w=[xab[i]])
                    k.dma([(cs[i][:, 0, :], ropec[kind][tok0:tok0 + 128, :]), (cs[i][:, 1, :], ropes[kind][tok0:tok0 + 128, :])], w=[csb[i]])
                    nctx.part1(xa[i][:], xab[i], mods.t[4][:], mods.b[4], mods.t[3][:], mods.b[3])
                    nctx.part2(hT[i][:], hTb[i])
                    for n in range(3):
                        for kc in range(KC):
                            k.op(pe, lambda: nc.tensor.matmul(pq[n][:], hT[i][:, kc, :], Win[:, kc, n * 512:(n + 1) * 512],
                                                              start=(kc == 0), stop=(kc == KC - 1)), r=[hTb[i], Winb], w=[pqb[n]])
                    qkf = qk[i][:].rearrange("p h d -> p (h d)")
                    k.op(act, lambda: nc.scalar.copy(out=qkf[:, 0:512], in_=pq[0][:]), r=[pqb[0]], w=[qkb[i]])
                    k.op(act, lambda: nc.scalar.copy(out=qkf[:, 512:1024], in_=pq[1][:]), r=[pqb[1]], w=[qkb[i]])
                    k.op(act, lambda: nc.scalar.copy(out=qkf[:, 1024:1280], in_=pq[2][:, 0:256]), r=[pqb[2]], w=[qkb[i]])
                    k.op(act, lambda: nc.scalar.copy(out=vb[i][:].rearrange("p h d -> p (h d)"), in_=pq[2][:, 256:512]),
                         r=[pqb[2]], w=[vbb[i]])
                    k.dma([(Vd[:, :, t, :].rearrange("g p d -> p g d"), vb[i][:])], r=[vbb[i]])
                    k.op(act, lambda: nc.scalar.activation(out=sq[i][:], in_=qk[i][:], func=AF.Square), r=[qkb[i]], w=[sqb[i]])
                    k.op(dve, lambda: nc.vector.tensor_reduce(out=sm[i][:, 0, :], in_=sq[i][:], axis=AX.X, op=ALU.add), r=[sqb[i]], w=[smb[i]])
                    k.op(dve, lambda: nc.vector.tensor_scalar(out=sm[i][:, 1, :], in0=sm[i][:, 0, :], scalar1=1.0 / 64, scalar2=EPS,
                                                              op0=ALU.mult, op1=ALU.add), r=[smb[i]], w=[smb[i]])
                    k.op(pool, lambda: nc.gpsimd.tensor_tensor(out=sm[i][:, 2, :], in0=sm[i][:, 1, :], in1=nhalf[:, 0:20], op=ALU.pow),
                         r=[smb[i]], w=[smb[i]])
                    k.op(dve, lambda: nc.vector.tensor_tensor(out=qk[i][:], in0=qk[i][:], in1=bc(sm[i][:, 2, :].unsqueeze(2), [128, 20, 64]),
                                                              op=ALU.mult), r=[smb[i]], w=[qkb[i]])
                    k.op(pool, lambda: nc.gpsimd.tensor_tensor(out=qk[i][:], in0=qk[i][:], in1=nwr[:], op=ALU.mult), r=[nwrb], w=[qkb[i]])
                    k.op(dve, lambda: nc.vector.tensor_tensor(out=sq[i][:], in0=qk[i][:], in1=bc(cs[i][:, 0:1, :], [128, 20, 64]), op=ALU.mult),
                         r=[qkb[i], csb[i]], w=[sqb[i]])
                    q5 = qk[i][:].rearrange("p h (b two e) -> p h b two e", two=2, e=hbk)
                    t5 = t2[i][:].rearrange("p h (b two e) -> p h b two e", two=2, e=hbk)
                    s5 = cs[i][:, 1:2, :].rearrange("p o (b two e) -> p o b two e", two=2, e=hbk)
                    for half in range(2):
                        k.op(pool, lambda: nc.gpsimd.tensor_tensor(out=t5[:, :, :, half, :], in0=q5[:, :, :, 1 - half, :],
                                                                   in1=bc(s5[:, :, :, half, :], [128, 20, nbk, hbk]), op=ALU.mult),
                             r=[qkb[i], csb[i]], w=[t2b[i]])
                    k.op(dve, lambda: nc.vector.tensor_tensor(out=qr[i][:], in0=sq[i][:].rearrange("p h d -> p (h d)"),
                                                              in1=t2[i][:].rearrange("p h d -> p (h d)"), op=ALU.add),
                         r=[sqb[i], t2b[i]], w=[qrb[i]])
                    for j in range(8):
                        k.op(pe, lambda: nc.tensor.transpose(ptq[:, j, :], qr[i][:, j * 128:(j + 1) * 128], identb[:]), r=[qrb[i]], w=[ptqb])
                    for j in range(2):
                        k.op(pe, lambda: nc.tensor.transpose(ptk[:, j, :], qr[i][:, 1024 + j * 128:1024 + (j + 1) * 128], identb[:]),
                             r=[qrb[i]], w=[ptkb])
                    k.op(act, lambda: nc.scalar.copy(out=qT[i][:, 0:8, :], in_=ptq[:]), r=[ptqb], w=[qTb[i]])
                    k.op(act, lambda: nc.scalar.copy(out=qT[i][:, 8:10, :], in_=ptk[:]), r=[ptkb], w=[qTb[i]])
                    k.dma([(QTK[:, :, tok0:tok0 + 128].rearrange("j p t -> p j t"), qT[i][:])], r=[qTb[i]])
            return phase_done()

        def phase_att(l, kind):
            es = contextlib.ExitStack()
            with es:
                KT = [TT(es, "t_KT%d" % i, [64, T], BF16) for i in range(2)]
                KTb = [k.buf("t_KT%d" % i) for i in range(2)]
                V = [TT(es, "t_V%d" % i, [128, NT, 65], BF16) for i in range(2)]
                Vb = [k.buf("t_V%d" % i) for i in range(2)]
                am = TT(es, "t_am", [128, 64]); amb = k.buf("t_am")
                sw = TT(es, "t_sw", [128, NT * 2])
                k.dma([(am[:], amask_d[:, :]), (sw[:], swab_d[:, :])], w=[amb])
                es_s = TT(es, "t_ess", [65, 16])
                esx = TT(es, "t_esx", [65, 16, 128]); esb = k.buf("t_esx")
                if kind == 0:
                    k.dma([(es_s[64:65, :], swa_sink[0:1, :])], w=[esb])
                    k.op(act, lambda: nc.scalar.activation(out=es_s[64:65, :], in_=es_s[64:65, :], func=AF.Exp), r=[esb], w=[esb])
                    k.op(dve, lambda: nc.vector.tensor_copy(out=esx[64:65, :, :], in_=bc(es_s[64:65, :].unsqueeze(2), [1, 16, 128])),
                         r=[esb], w=[esb])
                for i in range(2):
                    k.op(pool, lambda: nc.gpsimd.memset(V[i][:, :, 64:65], 1.0), w=[Vb[i]])
                NQ = 3
                qt = [TT(es, "t_qt%d" % i, [64, 4, 128], BF16) for i in range(NQ)]
                qtb = [k.buf("t_qt%d" % i) for i in range(NQ)]
                NP = 3
                pt = [TT(es, "t_pt%d" % i, [128, 4, 128], BF16) for i in range(NP)]
                ptb = [k.buf("t_pt%d" % i) for i in range(NP)]
                R = [TT(es, "t_R%d" % i, [65, 512]) for i in range(2)]
                Rb = [k.buf("t_R%d" % i) for i in range(2)]
                rec = [TT(es, "t_rec%d" % i, [64, 512]) for i in range(2)]
                recb = [k.buf("t_rec%d" % i) for i in range(2)]
                ot = [TT(es, "t_ot%d" % i, [64, 4, 128], BF16) for i in range(2)]
                otb = [k.buf("t_ot%d" % i) for i in range(2)]
                NS = 4
                ps = [PP(es, "t_ps%d" % i, [128, 512]) for i in range(NS)]
                psb = [k.buf("t_ps%d" % i) for i in range(NS)]
                po = [PP(es, "t_po%d" % i, [65, 512]) for i in range(2)]
                pob = [k.buf("t_po%d" % i) for i in range(2)]
                pbt = [PP(es, "t_pb%d" % i, [64, 512]) for i in range(2)]
                pbb = [k.buf("t_pb%d" % i) for i in range(2)]
                cn = {"s": 0, "q": 0, "o": 0}

                def load_kv(g):
                    i = g % 2
                    CK = min(T, 2048)
                    k.dma([(KT[i][:, c0:c0 + CK], QTK[8 + g // 2, (g % 2) * 64:(g % 2) * 64 + 64, c0:c0 + CK]) for c0 in range(0, T, CK)], w=[KTb[i]])
                    VB = min(NT, 8)
                    k.dma([(V[i][:, b0:b0 + VB, 0:64], Vd[g, :, b0:b0 + VB, :]) for b0 in range(0, NT, VB)], w=[Vb[i]])

                load_kv(0)
                for g in range(4):
                    gi = g % 2
                    if g + 1 < 4:
                        load_kv(g + 1)
                    for i in range(NT):
                        qi = cn["q"] % NQ
                        cn["q"] += 1
                        k.dma([(qt[qi][:], QTK[2 * g:2 * g + 2, :, i * 128:(i + 1) * 128].rearrange("j (hh d) t -> d (j hh) t", hh=2))],
                              w=[qtb[qi]])
                        if kind == 0:
                            js = [j for j in (i - 1, i, i + 1) if 0 <= j < NT]
                        else:
                            js = list(range(NT))
                        oi = cn["o"] % 2
                        cn["o"] += 1
                        for jn, j in enumerate(js):
                            si = cn["s"] % NS
                            pi = cn["s"] % NP
                            cn["s"] += 1
                            k.op(pe, lambda: nc.tensor.matmul(ps[si][:], KT[gi][:, j * 128:(j + 1) * 128],
                                                              qt[qi][:].rearrange("d h t -> d (h t)"), start=True, stop=True),
                                 r=[KTb[gi], qtb[qi]], w=[psb[si]])
                            if kind == 1:
                                bias = am[:, (i // BPG) * 8 + (j // BPG):(i // BPG) * 8 + (j // BPG) + 1]
                            elif j == i:
                                bias = 0.0
                            elif j < i:
                                bias = sw[:, 2 * i:2 * i + 1]
                            else:
                                bias = sw[:, 2 * i + 1:2 * i + 2]
                            ptf = pt[pi][:].rearrange("k h t -> k (h t)")
                            k.op(act, lambda: nc.scalar.activation(out=ptf, in_=ps[si][:], func=AF.Exp, bias=bias), r=[psb[si], amb], w=[ptb[pi]])
                            if kind == 0 and j != i:
                                tri = TRIr if j < i else TRIi
                                k.op(dve, lambda: nc.vector.tensor_tensor(out=pt[pi][:], in0=pt[pi][:], in1=bc(tri[:].unsqueeze(1), [128, 4, 128]),
                                                                          op=ALU.mult), r=[cb], w=[ptb[pi]])
                            k.op(pe, lambda: nc.tensor.matmul(po[oi][:], V[gi][:, j, :], ptf, start=(jn == 0), stop=(jn == len(js) - 1)),
                                 r=[Vb[gi], ptb[pi]], w=[pob[oi]])
                        k.op(dve, lambda: nc.vector.tensor_copy(out=R[oi][:], in_=po[oi][:]), r=[pob[oi]], w=[Rb[oi]])
                        if kind == 0:
                            k.op(dve, lambda: nc.vector.tensor_tensor(out=R[oi][64:65, :], in0=R[oi][64:65, :],
                                                                      in1=esx[64:65, 4 * g:4 * g + 4, :].rearrange("p h t -> p (h t)"), op=ALU.add),
                                 r=[esb], w=[Rb[oi]])
                        k.op(pe, lambda: nc.tensor.matmul(pbt[oi][:], E65[:], R[oi][:], start=True, stop=True), r=[Rb[oi], cb], w=[pbb[oi]])
                        k.op(dve, lambda: nc.vector.reciprocal(out=rec[oi][:], in_=pbt[oi][:]), r=[pbb[oi]], w=[recb[oi]])
                        k.op(pool, lambda: nc.gpsimd.tensor_tensor(out=ot[oi][:].rearrange("d h t -> d (h t)"), in0=R[oi][0:64, :], in1=rec[oi][:],
                                                                   op=ALU.mult), r=[Rb[oi], recb[oi]], w=[otb[oi]])
                        k.dma([(OTd[4 * g:4 * g + 4, :, i * 128:(i + 1) * 128].rearrange("h d t -> d h t"), ot[oi][:])], r=[otb[oi]])
            return phase_done()

        def phase_att_out(l, kind, dst):
            src = state["src"]
            es = contextlib.ExitStack()
            with es:
                Wo = TT(es, "o_Wo", [64, 16, D], BF16); Wob = k.buf("o_Wo")
                es2 = contextlib.ExitStack()
                with es2:
                    pieces = [(Wo[:, h, :], at_w_out[kind][0, h * 64:(h + 1) * 64, :]) for h in range(16)]
                    load_weight(es2, None, Wob, pieces)
                    k.barrier()
                mods = ModTiles(es, l, [5], "o")
                ep = Epilogue(es)
                oT = [TT(es, "o_oT%d" % i, [64, 16, 128], BF16) for i in range(3)]
                oTb = [k.buf("o_oT%d" % i) for i in range(3)]
                py = [[PP(es, "o_py%d_%d" % (i, h), [128, 512]) for h in range(2)] for i in range(2)]
                pyb = [[k.buf("o_py%d_%d" % (i, h)) for h in range(2)] for i in range(2)]
                for t in range(NT):
                    tok0 = t * 128
                    i3 = t % 3
                    i2 = t % 2
                    mods.need(tok0 // SLOT)
                    k.dma([(oT[i3][:], OTd[:, :, tok0:tok0 + 128].rearrange("h d t -> d h t"))], w=[oTb[i3]])
                    xi = ep.load(src, tok0)
                    for h in range(2):
                        for hd in range(16):
                            k.op(pe, lambda: nc.tensor.matmul(py[i2][h][:], oT[i3][:, hd, :], Wo[:, hd, h * 512:(h + 1) * 512],
                                                              start=(hd == 0), stop=(hd == 15)), r=[oTb[i3], Wob], w=[pyb[i2][h]])
                    ep.apply(xi, py[i2], pyb[i2], mods.t[5], mods.b[5], dst, tok0)
            state["src"] = dst
            return phase_done()

        def phase_ml_in(l, j):
            src = state["src"]
            es = contextlib.ExitStack()
            with es:
                Win = TT(es, "m_Win", [128, KC, 3104], BF16); Winb = k.buf("m_Win")
                es2 = contextlib.ExitStack()
                with es2:
                    pieces = []
                    for kc in range(KC):
                        pieces.append((Win[:, kc, 0:1552], ml_w_in[j, kc * 128:(kc + 1) * 128, 0:1552]))
                        pieces.append((Win[:, kc, 1552:3104], ml_w_in[j, kc * 128:(kc + 1) * 128, 1552:3104]))
                    load_weight(es2, None, Winb, pieces)
                    k.barrier()
                bg = TT(es, "m_bg", [128, 32]); bgb = k.buf("m_bg")
                k.dma([(bg[:], ml_bg[j:j + 1, :].partition_broadcast(128))], w=[bgb])
                xa = [TT(es, "mi_xa%d" % i, [128, D]) for i in range(2)]
                xab = [k.buf("mi_xa%d" % i) for i in range(2)]
                nctx = NormCtx(es)
                mods = ModTiles(es, l, [3, 4], "mi")
                hT = [TT(es, "mi_hT%d" % i, [128, KC, 128], BF16) for i in range(2)]
                hTb = [k.buf("mi_hT%d" % i) for i in range(2)]
                pq = [PP(es, "mi_pq%d" % i, [128, 512]) for i in range(4)]
                pqb = [k.buf("mi_pq%d" % i) for i in range(4)]
                ptr = PP(es, "mi_ptr", [128, 8, 128], BF16); ptrb = k.buf("mi_ptr")
                qkb_ = [TT(es, "mi_qkb%d" % i, [128, 1024], BF16) for i in range(2)]
                qkbb = [k.buf("mi_qkb%d" % i) for i in range(2)]
                kf = [TT(es, "mi_kf%d" % i, [128, 512]) for i in range(2)]
                kfb = [k.buf("mi_kf%d" % i) for i in range(2)]
                vt = [TT(es, "mi_vt%d" % i, [128, D], BF16) for i in range(2)]
                vtb = [k.buf("mi_vt%d" % i) for i in range(2)]
                so = [TT(es, "mi_so%d" % i, [128, D]) for i in range(2)]
                sob = [k.buf("mi_so%d" % i) for i in range(2)]
                gg = [TT(es, "mi_gg%d" % i, [128, 4, 32]) for i in range(2)]
                ggb = [k.buf("mi_gg%d" % i) for i in range(2)]
                qT = [TT(es, "mi_qT%d" % i, [128, 8, 128], BF16) for i in range(2)]
                qTb = [k.buf("mi_qT%d" % i) for i in range(2)]
                cn = 0
                for t in range(NT):
                    i = t % 2
                    tok0 = t * 128
                    mods.need(tok0 // SLOT)
                    k.dma([(xa[i][:], src[tok0:tok0 + 128, :])], w=[xab[i]])
                    nctx.part1(xa[i][:], xab[i], mods.t[4][:], mods.b[4], mods.t[3][:], mods.b[3])
                    nctx.part2(hT[i][:], hTb[i])
                    for n in range(7):
                        q = cn % 4
                        cn += 1
                        w_ = 512 if n < 6 else 32
                        for kc in range(KC):
                            k.op(pe, lambda: nc.tensor.matmul(pq[q][:, 0:w_], hT[i][:, kc, :], Win[:, kc, n * 512:n * 512 + w_],
                                                              start=(kc == 0), stop=(kc == KC - 1)), r=[hTb[i], Winb], w=[pqb[q]])
                        SK = os.environ.get("ML_SKIP", "")
                        if n == 0 and "q" in SK: pass
                        elif n == 1 and "k" in SK: pass
                        elif n in (2, 3) and "v" in SK: pass
                        elif n in (4, 5) and "o" in SK: pass
                        elif n == 0:
                            k.op(act, lambda: nc.scalar.activation(out=qkb_[i][:, 0:512], in_=pq[q][:], func=AF.Copy, scale=0.125),
                                 r=[pqb[q]], w=[qkbb[i]])
                        elif n == 1:
                            k.op(dve, lambda: nc.vector.tensor_copy(out=kf[i][:], in_=pq[q][:]), r=[pqb[q]], w=[kfb[i]])
                            k.op(act, lambda: nc.scalar.copy(out=qkb_[i][:, 512:1024], in_=kf[i][:]), r=[kfb[i]], w=[qkbb[i]])
                            k.dma([(Ktm[tok0:tok0 + 128, :], kf[i][:])], r=[kfb[i]])
                        elif n in (2, 3):
                            k.op(dve, lambda: nc.vector.tensor_copy(out=vt[i][:, (n - 2) * 512:(n - 1) * 512], in_=pq[q][:]), r=[pqb[q]], w=[vtb[i]])
                            if n == 3:
                                k.dma([(Vm[tok0:tok0 + 128, :], vt[i][:])], r=[vtb[i]])
                        elif n in (4, 5):
                            k.op(act, lambda: nc.scalar.activation(out=so[i][:, (n - 4) * 512:(n - 3) * 512], in_=pq[q][:], func=AF.Sigmoid),
                                 r=[pqb[q]], w=[sob[i]])
                            if n == 5:
                                k.dma([(SOd[tok0:tok0 + 128, :], so[i][:])], r=[sob[i]])
                        elif "g" not in SK:
                            G = gg[i]
                            k.op(dve, lambda: nc.vector.tensor_tensor(out=G[:, 0, :], in0=pq[q][:, 0:32], in1=bg[:], op=ALU.add),
                                 r=[pqb[q], bgb], w=[ggb[i]])
                            k.op(act, lambda: nc.scalar.activation(out=G[:, 1, :], in_=G[:, 0, :], func=AF.Tanh, scale=1.0 / 15.0), r=[ggb[i]], w=[ggb[i]])
                            k.op(dve, lambda: nc.vector.tensor_scalar(out=G[:, 0, :], in0=G[:, 1, :], scalar1=15.0, scalar2=None, op0=ALU.mult),
                                 r=[ggb[i]], w=[ggb[i]])
                            k.op(act, lambda: nc.scalar.activation(out=G[:, 1, :], in_=G[:, 0, :], func=AF.Exp, scale=-1.0), r=[ggb[i]], w=[ggb[i]])
                            k.op(act, lambda: nc.scalar.activation(out=G[:, 2, :], in_=G[:, 1, :], func=AF.Ln, bias=1.0), r=[ggb[i]], w=[ggb[i]])
                            g4 = G[:, 0, :].rearrange("p (a b h) -> p a b h", a=2, b=2)
                            l4 = G[:, 2, :].rearrange("p (a b h) -> p a b h", a=2, b=2)
                            k.op(dve, lambda: nc.vector.tensor_scalar(out=g4[:, :, 1, :], in0=l4[:, :, 1, :], scalar1=-1.0, scalar2=None, op0=ALU.mult),
                                 r=[ggb[i]], w=[ggb[i]])
                            k.dma([(GTd[tok0:tok0 + 128, :], G[:, 0, :])], r=[ggb[i]])
                    if "t" in SK:
                        continue
                    for jj in range(8):
                        k.op(pe, lambda: nc.tensor.transpose(ptr[:, jj, :], qkb_[i][:, jj * 128:(jj + 1) * 128], identb[:]), r=[qkbb[i]], w=[ptrb])
                    k.op(act, lambda: nc.scalar.copy(out=qT[i][:], in_=ptr[:]), r=[ptrb], w=[qTb[i]])
                    k.dma([(MQK[:, :, tok0:tok0 + 128].rearrange("j p t -> p j t"), qT[i][:])], r=[qTb[i]])
            return phase_done()

        def phase_ml_scan(l, j, direction):
            fwd = direction == 0
            TRI = TRIi if fwd else TRIr
            es = contextlib.ExitStack()
            with es:
                keep = TT(es, "s_keep", [128, NT]); keepb_ = k.buf("s_keep")
                k.dma([(keep[:], (keepf_d if fwd else keepb_d)[:, :])], w=[keepb_])
                nwr = TT(es, "s_nwr", [128, D]); nwrb = k.buf("s_nwr")
                k.dma([(nwr[:], ml_nw[j:j + 1, :].partition_broadcast(128))], w=[nwrb])
                C = TT(es, "s_C", [64, 8, 129]); Cb_ = k.buf("s_C")
                Cbf = TT(es, "s_Cbf", [64, 8, 129], BF16); Cbfb = k.buf("s_Cbf")
                k.op(dve, lambda: nc.vector.memset(C[:], 0.0), w=[Cb_])
                k.op(pool, lambda: nc.gpsimd.memset(Cbf[:], 0.0), w=[Cbfb])
                NB_ = 2
                mk = lambda nm, shape, dt=F32: ([TT(es, "s_%s%d" % (nm, i), shape, dt) for i in range(NB_)],
                                                [k.buf("s_%s%d" % (nm, i)) for i in range(NB_)])
                QT, QTb = mk("QT", [64, 8, 128], BF16)
                KTt, KTb = mk("KT", [64, 8, 128], BF16)
                Kt, Ktb = mk("Kt", [128, 8, 64])
                Va, Vab = mk("Va", [128, 8, 129], BF16)
                Gt, Gtb = mk("Gt", [128, 32])
                for i in range(NB_):
                    k.op(pool, lambda: nc.gpsimd.memset(Va[i][:, :, 128:129], 1.0), w=[Vab[i]])
                gs, gsb = mk("gs", [128, 6, 8])
                bT, bTb = mk("bT", [8, 128])
                arg, argb = mk("arg", [128, 8, 128])
                eb, ebb = mk("eb", [64, 8, 128])
                QTs, QTsb = mk("QTs", [64, 8, 128], BF16)
                ST, STb = mk("ST", [128, 8, 128], BF16)
                Kw, Kwb = mk("Kw", [128, 8, 64], BF16)
                hd, hdb = mk("hd", [128, 8, 128])
                dn, dnb = mk("dn", [128, 2, 8])
                hfl, hflb = mk("hfl", [128, D])
                sot, sotb = mk("sot", [128, D])
                sq2, sq2b = mk("sq2", [128, 8, 128])
                sm2, sm2b = mk("sm2", [128, 3, 8])
                hg, hgb = mk("hg", [128, D], BF16)
                psm = PP(es, "s_psm", [128, 512]); psmb = k.buf("s_psm")
                pbc = [PP(es, "s_pbc%d" % i, [128, 4, 128]) for i in range(2)]
                pbcb = [k.buf("s_pbc%d" % i) for i in range(2)]
                pss = [PP(es, "s_pss%d" % i, [128, 4, 128]) for i in range(2)]
                pssb = [k.buf("s_pss%d" % i) for i in range(2)]
                hsplit = [(0, 3), (3, 6), (6, 8)]
                pov = [PP(es, "s_po%d" % i, [128, 3, 129]) for i in range(3)]
                pob = [k.buf("s_po%d" % i) for i in range(3)]
                order = list(range(NT)) if fwd else list(range(NT - 1, -1, -1))
                gb = 0 if fwd else 16
                for n_, c in enumerate(order):
                    i = n_ % 2
                    tok0 = c * 128
                    k.dma([(QT[i][:], MQK[0:4, :, tok0:tok0 + 128].rearrange("j (hh d) t -> d (j hh) t", hh=2))], w=[QTb[i]])
                    k.dma([(KTt[i][:], MQK[4:8, :, tok0:tok0 + 128].rearrange("j (hh d) t -> d (j hh) t", hh=2))], w=[KTb[i]])
                    k.dma([(Kt[i][:].rearrange("p h d -> p (h d)"), Ktm[tok0:tok0 + 128, :])], w=[Ktb[i]])
                    k.dma([(Va[i][:, :, 0:128], Vm[tok0:tok0 + 128, :].rearrange("p (h d) -> p h d", h=8))], w=[Vab[i]])
                    k.dma([(Gt[i][:], GTd[tok0:tok0 + 128, :])], w=[Gtb[i]])
                    if not fwd:
                        k.dma([(hfl[i][:], Hf[tok0:tok0 + 128, :])], w=[hflb[i]])
                        k.dma([(sot[i][:], SOd[tok0:tok0 + 128, :])], w=[sotb[i]])
                    ig = Gt[i][:, gb:gb + 8]
                    lf = Gt[i][:, gb + 8:gb + 16]
                    S_ = gs[i]
                    k.op(pe, lambda: nc.tensor.matmul(psm[:, 0:8], TRI[:], lf, start=True, stop=True), r=[Gtb[i], cb], w=[psmb])
                    k.op(pe, lambda: nc.tensor.matmul(psm[:, 8:16], ONESf[:], lf, start=True, stop=True), r=[Gtb[i], cb], w=[psmb])
                    k.op(pe, lambda: nc.tensor.matmul(psm[0:8, 128:256], lf, TRI[:], start=True, stop=True), r=[Gtb[i], cb], w=[psmb])
                    k.op(dve, lambda: nc.vector.tensor_copy(out=S_[:, 0, :], in_=psm[:, 0:8]), r=[psmb], w=[gsb[i]])
                    k.op(dve, lambda: nc.vector.tensor_copy(out=S_[:, 4, :], in_=psm[:, 8:16]), r=[psmb], w=[gsb[i]])
                    k.op(dve, lambda: nc.vector.tensor_copy(out=bT[i][:], in_=psm[0:8, 128:256]), r=[psmb], w=[bTb[i]])
                    k.op(dve, lambda: nc.vector.tensor_tensor(out=S_[:, 1, :], in0=ig, in1=S_[:, 0, :], op=ALU.subtract), r=[Gtb[i]], w=[gsb[i]])
                    k.op(dve, lambda: nc.vector.tensor_tensor(out=S_[:, 2, :], in0=S_[:, 1, :], in1=S_[:, 4, :], op=ALU.add), w=[gsb[i]])
                    k.op(act, lambda: nc.scalar.activation(out=S_[:, 2, :], in_=S_[:, 2, :], func=AF.Exp), w=[gsb[i]])
                    k.op(act, lambda: nc.scalar.activation(out=S_[:, 3, :], in_=S_[:, 4, :], func=AF.Exp), w=[gsb[i]])
                    for h in range(8):
                        k.op(pe, lambda: nc.tensor.matmul(pbc[h // 4][:, h % 4, :], Sel[0:8, h, :], bT[i][:], start=True, stop=True),
                             r=[bTb[i], cb], w=[pbcb[h // 4]])
                    for hh in range(2):
                        k.op(dve, lambda: nc.vector.tensor_tensor(out=arg[i][:, 4 * hh:4 * hh + 4, :], in0=pbc[hh][:],
                                                                  in1=bc(S_[:, 1, 4 * hh:4 * hh + 4].unsqueeze(2), [128, 4, 128]), op=ALU.add),
                             r=[pbcb[hh], gsb[i]], w=[argb[i]])
                        k.op(dve, lambda: nc.vector.tensor_copy(out=eb[i][:, 4 * hh:4 * hh + 4, :], in_=pbc[hh][0:64]),
                             r=[pbcb[hh]], w=[ebb[i]])
                        k.op(act, lambda: nc.scalar.activation(out=eb[i][:, 4 * hh:4 * hh + 4, :], in_=eb[i][:, 4 * hh:4 * hh + 4, :], func=AF.Exp),
                             w=[ebb[i]])
                    k.op(pool, lambda: nc.gpsimd.tensor_scalar(out=arg[i][:], in0=arg[i][:], scalar1=16.0, scalar2=None, op0=ALU.min), w=[argb[i]])
                    k.op(act, lambda: nc.scalar.activation(out=arg[i][:], in_=arg[i][:], func=AF.Exp), w=[argb[i]])
                    k.op(pool, lambda: nc.gpsimd.tensor_tensor(out=arg[i][:], in0=arg[i][:], in1=bc(TRI[:].unsqueeze(1), [128, 8, 128]), op=ALU.mult),
                         r=[cb], w=[argb[i]])
                    k.op(dve, lambda: nc.vector.tensor_tensor(out=QTs[i][:], in0=QT[i][:], in1=eb[i][:], op=ALU.mult), r=[QTb[i], ebb[i]], w=[QTsb[i]])
                    for h in range(8):
                        k.op(pe, lambda: nc.tensor.matmul(pss[h // 4][:, h % 4, :], KTt[i][:, h, :], QT[i][:, h, :], start=True, stop=True),
                             r=[KTb[i], QTb[i]], w=[pssb[h // 4]])
                    for hh in range(2):
                        k.op(dve, lambda: nc.vector.tensor_tensor(out=ST[i][:, 4 * hh:4 * hh + 4, :], in0=pss[hh][:], in1=arg[i][:, 4 * hh:4 * hh + 4, :],
                                                                  op=ALU.mult), r=[pssb[hh], argb[i]], w=[STb[i]])
                    k.op(pool, lambda: nc.gpsimd.tensor_tensor(out=Kw[i][:], in0=Kt[i][:], in1=bc(S_[:, 2, :].unsqueeze(2), [128, 8, 64]), op=ALU.mult),
                         r=[Ktb[i], gsb[i]], w=[Kwb[i]])
                    for h in range(8):
                        pi_ = 0 if h < 3 else (1 if h < 6 else 2)
                        hl = h - hsplit[pi_][0]
                        k.op(pe, lambda: nc.tensor.matmul(pov[pi_][:, hl, :], ST[i][:, h, :], Va[i][:, h, :], start=True, stop=False),
                             r=[STb[i], Vab[i]], w=[pob[pi_]])
                        k.op(pe, lambda: nc.tensor.matmul(pov[pi_][:, hl, :], QTs[i][:, h, :], Cbf[:, h, :], start=False, stop=True),
                             r=[QTsb[i], Cbfb], w=[pob[pi_]])
                    for pi_, (h0, h1) in enumerate(hsplit):
                        nh = h1 - h0
                        k.op(dve, lambda: nc.vector.tensor_copy(out=dn[i][:, 1, h0:h1], in_=pov[pi_][:, 0:nh, 128]), r=[pob[pi_]], w=[dnb[i]])
                    k.op(dve, lambda: nc.vector.tensor_tensor(out=dn[i][:, 0, :], in0=dn[i][:, 1, :], in1=dn[i][:, 1, :], op=ALU.mult), w=[dnb[i]])
                    k.op(dve, lambda: nc.vector.tensor_scalar(out=dn[i][:, 0, :], in0=dn[i][:, 0, :], scalar1=1.0, scalar2=None, op0=ALU.max), w=[dnb[i]])
                    k.op(pool, lambda: nc.gpsimd.tensor_tensor(out=dn[i][:, 1, :], in0=dn[i][:, 0, :], in1=nhalf[:, 0:8], op=ALU.pow), w=[dnb[i]])
                    for pi_, (h0, h1) in enumerate(hsplit):
                        nh = h1 - h0
                        k.op(dve, lambda: nc.vector.tensor_tensor(out=hd[i][:, h0:h1, :], in0=pov[pi_][:, 0:nh, 0:128],
                                                                  in1=bc(dn[i][:, 1, h0:h1].unsqueeze(2), [128, nh, 128]), op=ALU.mult),
                             r=[pob[pi_], dnb[i]], w=[hdb[i]])
                    hdf = hd[i][:].rearrange("p h d -> p (h d)")
                    if fwd:
                        k.dma([(Hf[tok0:tok0 + 128, :], hdf)], r=[hdb[i]])
                    else:
                        k.op(pool, lambda: nc.gpsimd.tensor_tensor(out=hdf, in0=hdf, in1=hfl[i][:], op=ALU.add), r=[hflb[i]], w=[hdb[i]])
                        k.op(act, lambda: nc.scalar.activation(out=sq2[i][:], in_=hd[i][:], func=AF.Square), r=[hdb[i]], w=[sq2b[i]])
                        k.op(dve, lambda: nc.vector.tensor_reduce(out=sm2[i][:, 0, :], in_=sq2[i][:], axis=AX.X, op=ALU.add), r=[sq2b[i]], w=[sm2b[i]])
                        k.op(dve, lambda: nc.vector.tensor_scalar(out=sm2[i][:, 1, :], in0=sm2[i][:, 0, :], scalar1=1.0 / 128, scalar2=EPS,
                                                                  op0=ALU.mult, op1=ALU.add), w=[sm2b[i]])
                        k.op(pool, lambda: nc.gpsimd.tensor_tensor(out=sm2[i][:, 2, :], in0=sm2[i][:, 1, :], in1=nhalf[:, 0:8], op=ALU.pow), w=[sm2b[i]])
                        k.op(dve, lambda: nc.vector.tensor_tensor(out=hd[i][:], in0=hd[i][:], in1=bc(sm2[i][:, 2, :].unsqueeze(2), [128, 8, 128]),
                                                                  op=ALU.mult), r=[sm2b[i]], w=[hdb[i]])
                        k.op(pool, lambda: nc.gpsimd.tensor_tensor(out=hdf, in0=hdf, in1=nwr[:], op=ALU.mult), r=[nwrb], w=[hdb[i]])
                        k.op(dve, lambda: nc.vector.tensor_tensor(out=hg[i][:], in0=hdf, in1=sot[i][:], op=ALU.mult), r=[hdb[i], sotb[i]], w=[hgb[i]])
                        k.dma([(HGd[tok0:tok0 + 128, :], hg[i][:])], r=[hgb[i]])
                    for h in range(8):
                        pi_ = 0 if h < 3 else (1 if h < 6 else 2)
                        hl = h - hsplit[pi_][0]
                        k.op(pe, lambda: nc.tensor.matmul(pov[pi_][0:64, hl, :], Kw[i][:, h, :], Va[i][:, h, :], start=True, stop=True),
                             r=[Kwb[i], Vab[i]], w=[pob[pi_]])
                    k.op(dve, lambda: nc.vector.tensor_tensor(out=C[:], in0=C[:], in1=bc(S_[0:64, 3, :].unsqueeze(2), [64, 8, 129]), op=ALU.mult),
                         r=[gsb[i]], w=[Cb_])
                    for pi_, (h0, h1) in enumerate(hsplit):
                        nh = h1 - h0
                        k.op(dve, lambda: nc.vector.tensor_tensor(out=C[:, h0:h1, :], in0=C[:, h0:h1, :], in1=pov[pi_][0:64, 0:nh, :], op=ALU.add),
                             r=[pob[pi_]], w=[Cb_])
                    k.op(dve, lambda: nc.vector.tensor_scalar(out=C[:], in0=C[:], scalar1=keep[0:64, c:c + 1], scalar2=None, op0=ALU.mult),
                         r=[keepb_], w=[Cb_])
                    k.op(pool, lambda: nc.gpsimd.tensor_copy(out=Cbf[:], in_=C[:]), r=[Cb_], w=[Cbfb])
            return phase_done()

        def phase_ml_out(l, j, dst):
            src = state["src"]
            es = contextlib.ExitStack()
            with es:
                Wo = TT(es, "mo_Wo", [128, KC, D], BF16); Wob = k.buf("mo_Wo")
                es2 = contextlib.ExitStack()
                with es2:
                    pieces = [(Wo[:, kc, :], ml_w_out[j, kc * 128:(kc + 1) * 128, :]) for kc in range(KC)]
                    load_weight(es2, None, Wob, pieces)
                    k.barrier()
                mods = ModTiles(es, l, [5], "mo")
                ep = Epilogue(es)
                hgt = [TT(es, "mo_hg%d" % i, [128, D], BF16) for i in range(2)]
                hgtb = [k.buf("mo_hg%d" % i) for i in range(2)]
                hT = [TT(es, "mo_hT%d" % i, [128, KC, 128], BF16) for i in range(2)]
                hTb = [k.buf("mo_hT%d" % i) for i in range(2)]
                pT = [PP(es, "mo_pT%d" % i, [128, 8, 128], BF16) for i in range(2)]
                pTb = [k.buf("mo_pT%d" % i) for i in range(2)]
                py = [[PP(es, "mo_py%d_%d" % (i, h), [128, 512]) for h in range(2)] for i in range(2)]
                pyb = [[k.buf("mo_py%d_%d" % (i, h)) for h in range(2)] for i in range(2)]
                for t in range(NT):
                    tok0 = t * 128
                    i = t % 2
                    mods.need(tok0 // SLOT)
                    k.dma([(hgt[i][:], HGd[tok0:tok0 + 128, :])], w=[hgtb[i]])
                    xi = ep.load(src, tok0)
                    for kc in range(KC):
                        k.op(pe, lambda: nc.tensor.transpose(pT[i][:, kc, :], hgt[i][:, kc * 128:(kc + 1) * 128], identb[:]), r=[hgtb[i]], w=[pTb[i]])
                    k.op(act, lambda: nc.scalar.copy(out=hT[i][:], in_=pT[i][:]), r=[pTb[i]], w=[hTb[i]])
                    for h in range(2):
                        for kc in range(KC):
                            k.op(pe, lambda: nc.tensor.matmul(py[i][h][:], hT[i][:, kc, :], Wo[:, kc, h * 512:(h + 1) * 512],
                                                              start=(kc == 0), stop=(kc == KC - 1)), r=[hTb[i], Wob], w=[pyb[i][h]])
                    ep.apply(xi, py[i], pyb[i], mods.t[5], mods.b[5], dst, tok0)
            state["src"] = dst
            return phase_done()

        def program():
            if phase_mod():
                return
            for l in range(NL):
                last = (l == NL - 1)
                if phase_ffn(l, 0, xs):
                    return
                kind, j = l % 3, l // 3
                if kind == 0:
                    if phase_ml_in(l, j): return
                    if phase_ml_scan(l, j, 0): return
                    if phase_ml_scan(l, j, 1): return
                    if phase_ml_out(l, j, xs): return
                else:
                    if phase_att_in(l, kind - 1): return
                    if phase_att(l, kind - 1): return
                    if phase_att_out(l, kind - 1, xs): return
                if phase_ffn(l, 1, y_out if last else xs):
                    return
        program()
        if state["src"] is not y_out:
            es = contextlib.ExitStack()
            with es:
                tb = [TT(es, "cp%d" % i, [128, D]) for i in range(2)]
                tbb = [k.buf("cp%d" % i) for i in range(2)]
                for t in range(NT):
                    k.dma([(tb[t % 2][:], state["src"][t * 128:(t + 1) * 128, :])], w=[tbb[t % 2]])
                    k.dma([(y_out[t * 128:(t + 1) * 128, :], tb[t % 2][:])], r=[tbb[t % 2]])
            k.barrier()
    return nc


def rope_np(pos, dim):
    inv = (np.float32(10000.0) ** (-np.arange(0, dim, 2, dtype=np.float32) / np.float32(dim))).astype(np.float32)
    ang = pos.astype(np.float32)[:, None] * inv[None, :]
    ang = np.concatenate([ang, ang], axis=-1)
    return np.cos(ang).astype(np.float32), np.sin(ang).astype(np.float32)


def core_tables(T, S):
    NT = T // 128
    bps = S // 128
    pos = np.arange(T) % S
    c, s = rope_np(pos, 64)
    s_sw = np.concatenate([-s[:, :32], s[:, 32:]], axis=1)
    rc, rs = rope_np(pos // 64, 32)
    cc, cs_ = rope_np(pos % 64, 32)
    c_ax = np.concatenate([rc, cc], axis=1)
    s_ax = np.concatenate([-rs[:, :16], rs[:, 16:], -cs_[:, :16], cs_[:, 16:]], axis=1)
    blk = np.arange(NT)
    keepf = ((blk + 1) % bps != 0).astype(np.float32)
    keepb = (blk % bps != 0).astype(np.float32)
    swab = np.zeros((NT, 2), np.float32)
    swab[blk % bps == 0, 0] = NEG
    swab[(blk + 1) % bps == 0, 1] = NEG
    slot_seq = (np.arange(8) * (T // 8)) // S
    amask = np.where(slot_seq[:, None] == slot_seq[None, :], 0.0, NEG).astype(np.float32)
    rep = lambda a: np.ascontiguousarray(np.broadcast_to(a.reshape(1, -1), (128, a.size))).astype(np.float32)
    return {
        "rope_swa_c": c, "rope_swa_s": s_sw.astype(np.float32), "rope_ax_c": c_ax.astype(np.float32), "rope_ax_s": s_ax.astype(np.float32),
        "keepf": rep(keepf), "keepb": rep(keepb), "swab": rep(swab), "amask": rep(amask),
    }


WNAMES = ["ffn_w13", "ffn_w2", "ada_w", "ada_b", "norm_w", "mlstm_w_in", "mlstm_b_gate", "mlstm_norm_w", "mlstm_w_out",
          "swa_w_in", "swa_q_norm", "swa_k_norm", "swa_sink", "swa_w_out", "axial_w_in", "axial_q_norm", "axial_k_norm", "axial_w_out"]


def run_streams(streams, weights, T, NL=4, stop=None, ncores=8):
    nc = build(T, NL, stop)
    w = {n: np.ascontiguousarray(np.asarray(weights[n], dtype=np.float32)) for n in WNAMES}
    in_maps = []
    for c in range(ncores):
        x, c8, S = streams[c] if c < len(streams) else streams[-1]
        m = {"x": np.ascontiguousarray(x, dtype=np.float32), "c8": np.ascontiguousarray(c8, dtype=np.float32)}
        m.update(core_tables(T, S))
        m.update(w)
        in_maps.append(m)
    res = run_bass_kernel_spmd(nc, in_maps, core_ids=list(range(ncores)))
    return [res.results[c]["y"] for c in range(len(streams))]


def kernel(x_prompt, x_sample, c_prompt, c_sample, **weights):
    x_prompt = np.asarray(x_prompt, dtype=np.float32)
    x_sample = np.asarray(x_sample, dtype=np.float32)
    c_prompt = np.asarray(c_prompt, dtype=np.float32)
    c_sample = np.asarray(c_sample, dtype=np.float32)
    T = 16384
    streams = []
    for b in range(2):
        streams.append((x_prompt[b], np.ascontiguousarray(np.broadcast_to(c_prompt[b:b + 1], (8, D))), 16384))
    for j in range(4):
        streams.append((x_sample[8 * j:8 * j + 8].reshape(T, D), c_sample[8 * j:8 * j + 8], 2048))
    ys = run_streams(streams, weights, T)
    y_prompt = np.stack([ys[0], ys[1]], axis=0).astype(np.float32)
    y_sample = np.concatenate([ys[2 + j].reshape(8, 2048, D) for j in range(4)], axis=0).astype(np.float32)
    return (y_prompt, y_sample)
```

```python
import bisect
import os
import contextlib
import numpy as np
import concourse.bass as bass
import concourse.mybir as mybir
from concourse.bass_utils import run_bass_kernel_spmd

F32 = mybir.dt.float32
BF16 = mybir.dt.bfloat16
AF = mybir.ActivationFunctionType
ALU = mybir.AluOpType
AX = mybir.AxisListType

D = 1024
DFF = 2816
NFC = 22
KC = 8
EPS = 1e-6
NEG = -30000.0
NDS = 56


class Eng:
    def __init__(s, name, h, sem):
        s.name, s.h, s.sem = name, h, sem
        s.cnt = 0
        s.idx = 0
        s.last = None
        s.sig_idx = []
        s.sig_cnt = []
        s.seen = {}


class DSem:
    def __init__(s, h, key):
        s.h, s.key, s.count = h, key, 0


class Buf:
    def __init__(s, name):
        s.name = name
        s.w = None
        s.r = {}
        s.dsem = None


class K:
    def __init__(s, nc, es):
        s.nc = nc
        mk = lambda n: es.enter_context(nc.semaphore(n))
        s.pe = Eng("pe", nc.tensor, mk("s_pe"))
        s.act = Eng("act", nc.scalar, mk("s_act"))
        s.dve = Eng("dve", nc.vector, mk("s_dve"))
        s.pool = Eng("pool", nc.gpsimd, mk("s_pool"))
        s.sp = Eng("sp", nc.sync, mk("s_sp"))
        s.engs = [s.pe, s.act, s.dve, s.pool, s.sp]
        s.dsems = [DSem(mk("d%d" % i), "d%d" % i) for i in range(NDS)]
        s.free = list(s.dsems)
        s.pbufs = []
        s.used = []

    def buf(s, name):
        b = Buf(name)
        s.pbufs.append(b)
        return b

    def _wait(s, eng, ev):
        if ev[0] == "e":
            e2, idx = ev[1], ev[2]
            if e2 is eng and eng is s.pe:
                return
            i = bisect.bisect_left(e2.sig_idx, idx)
            if i < len(e2.sig_idx):
                c = e2.sig_cnt[i]
            else:
                assert e2.idx >= idx and e2.last is not None
                e2.last.then_inc(e2.sem, 1)
                e2.cnt += 1
                e2.sig_idx.append(e2.idx)
                e2.sig_cnt.append(e2.cnt)
                c = e2.cnt
            if eng.seen.get(e2.name, 0) >= c:
                return
            eng.h.wait_ge(e2.sem, c)
            eng.seen[e2.name] = c
        else:
            ds, val = ev[1], ev[2]
            if eng.seen.get(ds.key, 0) >= val:
                return
            eng.h.wait_ge(ds.h, val)
            eng.seen[ds.key] = val

    def _deps(s, eng, r, w):
        for b in r:
            if b.w is not None:
                s._wait(eng, b.w)
        for b in w:
            if b.w is not None:
                s._wait(eng, b.w)
            for ev in b.r.values():
                s._wait(eng, ev)

    def op(s, eng, fn, r=(), w=()):
        s._deps(eng, r, w)
        ins = fn()
        eng.idx += 1
        eng.last = ins
        ev = ("e", eng, eng.idx)
        for b in r:
            b.r[eng.name] = ev
        for b in w:
            b.w = ev
            b.r = {}
        return ins

    def dma(s, pairs, r=(), w=(), q=None):
        q = q or s.sp
        s._deps(q, r, w)
        owner = w[0] if len(w) else r[0]
        if owner.dsem is None:
            owner.dsem = s.free.pop()
            s.used.append(owner.dsem)
        ds = owner.dsem
        for (o, i) in pairs:
            q.h.dma_start(out=o, in_=i).then_inc(ds.h, 16)
            ds.count += 16
        ev = ("d", ds, ds.count)
        for b in r:
            b.r["dma_" + ds.key] = ev
        for b in w:
            b.w = ev
            b.r = {}

    def barrier(s):
        for e in s.engs:
            for e2 in s.engs:
                if e2 is not e and e2.idx > 0 and e2 is not s.sp:
                    s._wait(e, ("e", e2, e2.idx))
            for ds in s.used:
                if ds.count > 0:
                    s._wait(e, ("d", ds, ds.count))
        s.free = list(s.dsems)
        s.used = []
        s.pbufs = []


def bc(ap, shape):
    return ap.to_broadcast(list(shape))


def build(T, NL=4, stop=None):
    NT = T // 128
    SLOT = T // 8
    BPG = SLOT // 128
    GT = 256
    NG = T // GT
    TPG = GT // 128
    nc = bass.Bass("TRN2", target_bir_lowering=False)
    dt_in = lambda name, shape, dt=F32: nc.dram_tensor(name, list(shape), dt, kind="ExternalInput").ap()
    dt_sc = lambda name, shape, dt=F32: nc.dram_tensor(name, list(shape), dt, kind="Internal").ap()
    x_in = dt_in("x", [T, D])
    c8 = dt_in("c8", [8, D])
    ffn_w13 = dt_in("ffn_w13", [4, 2, D, 2 * DFF])
    ffn_w2 = dt_in("ffn_w2", [4, 2, DFF, D])
    ada_w = dt_in("ada_w", [4, D, 9 * D])
    ada_b = dt_in("ada_b", [4, 9 * D])
    norm_w = dt_in("norm_w", [4, 3, D])
    ml_w_in = dt_in("mlstm_w_in", [2, D, 3104])
    ml_bg = dt_in("mlstm_b_gate", [2, 32])
    ml_nw = dt_in("mlstm_norm_w", [2, D])
    ml_w_out = dt_in("mlstm_w_out", [2, D, D])
    at_w_in = [dt_in("swa_w_in", [1, D, 1536]), dt_in("axial_w_in", [1, D, 1536])]
    at_qn = [dt_in("swa_q_norm", [1, 64]), dt_in("axial_q_norm", [1, 64])]
    at_kn = [dt_in("swa_k_norm", [1, 64]), dt_in("axial_k_norm", [1, 64])]
    swa_sink = dt_in("swa_sink", [1, 16])
    at_w_out = [dt_in("swa_w_out", [1, D, D]), dt_in("axial_w_out", [1, D, D])]
    ropec = [dt_in("rope_swa_c", [T, 64]), dt_in("rope_ax_c", [T, 64])]
    ropes = [dt_in("rope_swa_s", [T, 64]), dt_in("rope_ax_s", [T, 64])]
    keepf_d = dt_in("keepf", [128, NT])
    keepb_d = dt_in("keepb", [128, NT])
    swab_d = dt_in("swab", [128, NT * 2])
    amask_d = dt_in("amask", [128, 64])
    y_out = nc.dram_tensor("y", [T, D], F32, kind="ExternalOutput").ap()
    xs = dt_sc("xs", [T, D])
    MR = dt_sc("MR", [NL, 9, 8, 128, D])
    QTK = dt_sc("QTK", [10, 128, T], BF16)
    Vd = dt_sc("Vd", [4, 128, NT, 64], BF16)
    OTd = dt_sc("OTd", [16, 64, T], BF16)
    MQK = dt_sc("MQK", [8, 128, T], BF16)
    Ktm = dt_sc("Ktm", [T, 512])
    Vm = dt_sc("Vm", [T, D], BF16)
    SOd = dt_sc("SOd", [T, D])
    GTd = dt_sc("GTd", [T, 32])
    Hf = dt_sc("Hf", [T, D])
    HGd = dt_sc("HGd", [T, D], BF16)

    top = contextlib.ExitStack()
    with top:
        k = K(nc, top)
        pe, act, dve, pool = k.pe, k.act, k.dve, k.pool
        uid = [0]

        def TT(es, name, shape, dt=F32):
            uid[0] += 1
            return es.enter_context(nc.sbuf_tensor("%s_u%d" % (name, uid[0]), list(shape), dt))

        def PP(es, name, shape, dt=F32):
            uid[0] += 1
            return es.enter_context(nc.psum_tensor("%s_u%d" % (name, uid[0]), list(shape), dt))

        identb = TT(top, "identb", [128, 128], BF16)
        identf = TT(top, "identf", [128, 128])
        TRIi = TT(top, "TRIi", [128, 128])
        TRIr = TT(top, "TRIr", [128, 128])
        ONESf = TT(top, "ONESf", [128, 128])
        Sel = TT(top, "Sel", [8, 8, 128])
        E65 = TT(top, "E65", [65, 64])
        nhalf = TT(top, "nhalf", [128, 32])
        cb = k.buf("consts")

        def mkmask(t, pattern, cmul, cmp, base=0):
            k.op(pool, lambda: nc.gpsimd.memset(t, 1.0), w=[cb])
            k.op(pool, lambda: nc.gpsimd.affine_select(out=t, in_=t, pattern=pattern, compare_op=cmp, fill=0.0,
                                                       base=base, channel_multiplier=cmul), w=[cb])
        mkmask(identf[:], [[-1, 128]], 1, ALU.is_equal)
        mkmask(TRIi[:], [[1, 128]], -1, ALU.is_ge)
        mkmask(TRIr[:], [[-1, 128]], 1, ALU.is_ge)
        mkmask(Sel[:], [[-1, 8], [0, 128]], 1, ALU.is_equal)
        mkmask(E65[:], [[0, 64]], 1, ALU.is_equal, base=-64)
        CAPi = TT(top, "CAPi", [128, 128])
        CAPr = TT(top, "CAPr", [128, 128])
        k.op(pool, lambda: nc.gpsimd.memset(ONESf[:], 1.0), w=[cb])
        k.op(pool, lambda: nc.gpsimd.memset(nhalf[:], -0.5), w=[cb])
        k.op(dve, lambda: nc.vector.tensor_copy(out=identb[:], in_=identf[:]), r=[cb], w=[cb])
        k.op(dve, lambda: nc.vector.tensor_scalar(out=CAPi[:], in0=TRIi[:], scalar1=10016.0, scalar2=-10000.0, op0=ALU.mult, op1=ALU.add), w=[cb])
        k.op(dve, lambda: nc.vector.tensor_scalar(out=CAPr[:], in0=TRIr[:], scalar1=10016.0, scalar2=-10000.0, op0=ALU.mult, op1=ALU.add), w=[cb])
        k.barrier()

        state = {"src": x_in, "nph": 0}

        def phase_done():
            k.barrier()
            state["nph"] += 1
            return stop is not None and state["nph"] >= stop

        def load_weight(es, dst, dstbuf, pieces):
            nmax = max(p[1].shape[-1] for p in pieces)
            stg = [TT(es, "wstg%d" % i, [128, nmax]) for i in range(3)]
            sb = [k.buf("wstg%d" % i) for i in range(3)]
            cv = [dve, pool, act]
            for i, (d_ap, s_ap) in enumerate(pieces):
                P, n = s_ap.shape[0], s_ap.shape[-1]
                j = i % 3
                k.dma([(stg[j][0:P, 0:n], s_ap)], w=[sb[j]])
                e = cv[i % 3]
                if e is act:
                    k.op(act, lambda: nc.scalar.copy(out=d_ap, in_=stg[j][0:P, 0:n]), r=[sb[j]], w=[dstbuf])
                else:
                    k.op(e, lambda: e.h.tensor_copy(out=d_ap, in_=stg[j][0:P, 0:n]), r=[sb[j]], w=[dstbuf])

        class NormCtx:
            def __init__(s, es, nb=2):
                s.hb = [TT(es, "n_hb%d" % i, [128, D], BF16) for i in range(nb)]
                s.hbb = [k.buf("n_hb%d" % i) for i in range(nb)]
                s.sm = [TT(es, "n_sm%d" % i, [128, 4]) for i in range(nb)]
                s.smb = [k.buf("n_sm%d" % i) for i in range(nb)]
                s.pT = [PP(es, "n_pT%d" % i, [128, 8, 128], BF16) for i in range(2)]
                s.pTb = [k.buf("n_pT%d" % i) for i in range(2)]
                s.n = 0

            def part1(s, xa, xab, A, Ab, B, Bb):
                i = s.n % len(s.hb)
                s.cur = i
                hb, hbb, sm, smb = s.hb[i], s.hbb[i], s.sm[i], s.smb[i]
                k.op(act, lambda: nc.scalar.activation(out=hb[:], in_=xa, func=AF.Square, accum_out=sm[:, 0:1]),
                     r=[xab], w=[hbb, smb])
                k.op(dve, lambda: nc.vector.tensor_scalar(out=sm[:, 1:2], in0=sm[:, 0:1], scalar1=1.0 / D, scalar2=EPS,
                                                          op0=ALU.mult, op1=ALU.add), r=[smb], w=[smb])
                k.op(pool, lambda: nc.gpsimd.tensor_tensor(out=sm[:, 2:3], in0=sm[:, 1:2], in1=nhalf[:, 0:1], op=ALU.pow),
                     r=[smb], w=[smb])
                k.op(dve, lambda: nc.vector.scalar_tensor_tensor(out=xa, in0=xa, scalar=sm[:, 2:3], in1=A,
                                                                 op0=ALU.mult, op1=ALU.mult), r=[smb, Ab, xab], w=[xab])
                k.op(pool, lambda: nc.gpsimd.tensor_tensor(out=hb[:], in0=xa, in1=B, op=ALU.add), r=[xab, Bb], w=[hbb])

            def part2(s, hT, hTb):
                i = s.cur
                j = s.n % 2
                s.n += 1
                for kc in range(KC):
                    k.op(pe, lambda: nc.tensor.transpose(s.pT[j][:, kc, :], s.hb[i][:, kc * 128:(kc + 1) * 128], identb[:]),
                         r=[s.hbb[i]], w=[s.pTb[j]])
                k.op(act, lambda: nc.scalar.copy(out=hT, in_=s.pT[j][:]), r=[s.pTb[j]], w=[hTb])

        class ModTiles:
            def __init__(s, es, l, ms, pfx):
                s.l, s.ms = l, ms
                s.t = {m: TT(es, "%s_mod%d" % (pfx, m), [128, D]) for m in ms}
                s.b = {m: k.buf("%s_mod%d" % (pfx, m)) for m in ms}
                s.slot = -1

            def need(s, slot):
                if slot != s.slot:
                    s.slot = slot
                    for m in s.ms:
                        k.dma([(s.t[m][:], MR[s.l, m, slot])], w=[s.b[m]])

        class Epilogue:
            def __init__(s, es, nb=2):
                s.xb = [TT(es, "e_xb%d" % i, [128, D]) for i in range(nb)]
                s.xbb = [k.buf("e_xb%d" % i) for i in range(nb)]
                s.n = 0

            def load(s, src, tok0):
                i = s.n % len(s.xb)
                k.dma([(s.xb[i][:], src[tok0:tok0 + 128, :])], w=[s.xbb[i]])
                return i

            def apply(s, i, py, pyb, G, Gb, dst, tok0):
                for h in range(2):
                    k.op(dve, lambda: nc.vector.tensor_tensor(out=py[h][:], in0=py[h][:], in1=G[:, h * 512:(h + 1) * 512],
                                                              op=ALU.mult), r=[Gb], w=[pyb[h]])
                    k.op(dve, lambda: nc.vector.tensor_tensor(out=s.xb[i][:, h * 512:(h + 1) * 512],
                                                              in0=s.xb[i][:, h * 512:(h + 1) * 512], in1=py[h][:], op=ALU.add),
                         r=[pyb[h]], w=[s.xbb[i]])
                k.dma([(dst[tok0:tok0 + 128, :], s.xb[i][:])], r=[s.xbb[i]])
                s.n += 1

        def phase_mod():
            es = contextlib.ExitStack()
            with es:
                c8t = TT(es, "c8t", [8, D]); c8b = k.buf("c8t")
                c8s = TT(es, "c8s", [8, D], BF16)
                csT = TT(es, "csT", [128, KC, 8], BF16); csTb = k.buf("csT")
                csrep = TT(es, "csrep", [128, 8, KC, 128], BF16); csrb = k.buf("csrep")
                pcs = PP(es, "pcs", [128, KC, 8], BF16); pcsb = k.buf("pcs")
                k.dma([(c8t[:], c8[:, :])], w=[c8b])
                k.op(act, lambda: nc.scalar.activation(out=c8s[:], in_=c8t[:], func=AF.Silu), r=[c8b], w=[c8b])
                for kc in range(KC):
                    k.op(pe, lambda: nc.tensor.transpose(pcs[:, kc, :], c8s[0:8, kc * 128:(kc + 1) * 128], identb[0:8, 0:8]),
                         r=[c8b], w=[pcsb])
                k.op(dve, lambda: nc.vector.tensor_copy(out=csT[:], in_=pcs[:]), r=[pcsb], w=[csTb])
                for sl in range(8):
                    k.op(dve, lambda: nc.vector.tensor_copy(out=csrep[:, sl], in_=bc(csT[:, :, sl:sl + 1], [128, KC, 128])),
                         r=[csTb], w=[csrb])
                stg = [TT(es, "m_stg%d" % i, [128, KC, 512]) for i in range(2)]
                stgb = [k.buf("m_stg%d" % i) for i in range(2)]
                wst = [TT(es, "m_wst%d" % i, [128, KC, 512], BF16) for i in range(2)]
                wstb = [k.buf("m_wst%d" % i) for i in range(2)]
                adb = [TT(es, "m_adb%d" % i, [128, 512]) for i in range(2)]
                adbb = [k.buf("m_adb%d" % i) for i in range(2)]
                nwr = TT(es, "m_nwr", [128, 3, D]); nwrb = k.buf("m_nwr")
                mo = [TT(es, "m_mo%d" % i, [128, 512]) for i in range(3)]
                mob = [k.buf("m_mo%d" % i) for i in range(3)]
                pm = [PP(es, "m_pm%d" % i, [128, 512]) for i in range(3)]
                pmb = [k.buf("m_pm%d" % i) for i in range(3)]
                n = 0
                cgi = 0
                for l in range(NL):
                    k.dma([(nwr[:, j, :], norm_w[l, j:j + 1, :].partition_broadcast(128)) for j in range(3)], w=[nwrb])
                    for m in range(9):
                        for half in range(2):
                            c0 = m * D + half * 512
                            j = cgi % 2
                            cgi += 1
                            k.dma([(stg[j][:], ada_w[l, :, c0:c0 + 512].rearrange("(kc p) n -> p kc n", p=128))], w=[stgb[j]])
                            k.dma([(adb[j][:], ada_b[l:l + 1, c0:c0 + 512].partition_broadcast(128))], w=[adbb[j]])
                            k.op(pool, lambda: nc.gpsimd.tensor_copy(out=wst[j][:], in_=stg[j][:]), r=[stgb[j]], w=[wstb[j]])
                            for sl in range(8):
                                q = n % 3
                                n += 1
                                for kc in range(KC):
                                    k.op(pe, lambda: nc.tensor.matmul(pm[q][:], csrep[:, sl, kc, :], wst[j][:, kc, :],
                                                                      start=(kc == 0), stop=(kc == KC - 1)),
                                         r=[csrb, wstb[j]], w=[pmb[q]])
                                k.op(dve, lambda: nc.vector.tensor_tensor(out=mo[q][:], in0=pm[q][:], in1=adb[j][:], op=ALU.add),
                                     r=[pmb[q], adbb[j]], w=[mob[q]])
                                if m in (1, 4, 7):
                                    k.op(dve, lambda: nc.vector.scalar_tensor_tensor(
                                        out=mo[q][:], in0=mo[q][:], scalar=1.0, in1=nwr[:, m // 3, half * 512:(half + 1) * 512],
                                        op0=ALU.add, op1=ALU.mult), r=[nwrb], w=[mob[q]])
                                elif m in (2, 8):
                                    k.op(dve, lambda: nc.vector.tensor_scalar(out=mo[q][:], in0=mo[q][:], scalar1=0.5, scalar2=None,
                                                                              op0=ALU.mult), w=[mob[q]])
                                k.dma([(MR[l, m, sl, :, half * 512:(half + 1) * 512], mo[q][:])], r=[mob[q]])
            return phase_done()

        def phase_ffn(l, which, dst):
            src = state["src"]
            mi = 0 if which == 0 else 6
            es = contextlib.ExitStack()
            with es:
                W13 = TT(es, "W13", [128, KC, 2 * DFF], BF16); W13b = k.buf("W13")
                W2 = TT(es, "W2", [128, NFC, D], BF16); W2b = k.buf("W2")
                es2 = contextlib.ExitStack()
                with es2:
                    pieces = []
                    for kc in range(KC):
                        for c in range(4):
                            pieces.append((W13[:, kc, c * 1408:(c + 1) * 1408],
                                           ffn_w13[l, which, kc * 128:(kc + 1) * 128, c * 1408:(c + 1) * 1408]))
                    for fc in range(NFC):
                        pieces.append((W2[:, fc, :], ffn_w2[l, which, fc * 128:(fc + 1) * 128, :]))
                    wb = k.buf("Wall")
                    load_weight(es2, None, wb, pieces)
                    k.barrier()
                xa = [TT(es, "f_xa%d" % i, [128, D]) for i in range(2)]
                xab = [k.buf("f_xa%d" % i) for i in range(2)]
                nctx = NormCtx(es)
                mods = ModTiles(es, l, [mi, mi + 1], "f")
                modg = ModTiles(es, l, [mi + 2], "fg")
                hT = [TT(es, "f_hT%d" % i, [128, KC, GT], BF16) for i in range(2)]
                hTb = [k.buf("f_hT%d" % i) for i in range(2)]
                sg = [TT(es, "f_sg%d" % i, [128, GT]) for i in range(2)]
                sgb = [k.buf("f_sg%d" % i) for i in range(2)]
                uT = TT(es, "f_uT", [128, NFC, GT], BF16); uTb = k.buf("f_uT")
                ep = Epilogue(es)
                pg = [PP(es, "f_pg%d" % i, [128, GT]) for i in range(2)]
                pgb = [k.buf("f_pg%d" % i) for i in range(2)]
                pu = [PP(es, "f_pu%d" % i, [128, GT]) for i in range(2)]
                pub = [k.buf("f_pu%d" % i) for i in range(2)]
                py = [PP(es, "f_py%d" % i, [128, 512]) for i in range(2)]
                pyb = [k.buf("f_py%d" % i) for i in range(2)]
                cnt = {"xa": 0}

                def norm1(g):
                    mods.need((g * GT) // SLOT)
                    pend = []
                    for tt in range(TPG):
                        i = cnt["xa"] % 2
                        cnt["xa"] += 1
                        tok0 = g * GT + tt * 128
                        k.dma([(xa[i][:], src[tok0:tok0 + 128, :])], w=[xab[i]])
                        nctx.part1(xa[i][:], xab[i], mods.t[mi + 1][:], mods.b[mi + 1], mods.t[mi][:], mods.b[mi])
                        nctx.part2(hT[g % 2][:, :, tt * 128:(tt + 1) * 128], hTb[g % 2])

                norm1(0)
                for g in range(NG):
                    h_ = hT[g % 2]
                    for fc in range(NFC):
                        q = fc % 2
                        for kc in range(KC):
                            k.op(pe, lambda: nc.tensor.matmul(pg[q][:], W13[:, kc, fc * 128:(fc + 1) * 128], h_[:, kc, :],
                                                              start=(kc == 0), stop=(kc == KC - 1)), r=[hTb[g % 2]], w=[pgb[q]])
                        for kc in range(KC):
                            k.op(pe, lambda: nc.tensor.matmul(pu[q][:], W13[:, kc, DFF + fc * 128:DFF + (fc + 1) * 128], h_[:, kc, :],
                                                              start=(kc == 0), stop=(kc == KC - 1)), r=[hTb[g % 2]], w=[pub[q]])
                        k.op(act, lambda: nc.scalar.activation(out=sg[q][:], in_=pg[q][:], func=AF.Silu), r=[pgb[q]], w=[sgb[q]])
                        k.op(dve, lambda: nc.vector.tensor_tensor(out=uT[:, fc, :], in0=sg[q][:], in1=pu[q][:], op=ALU.mult),
                             r=[sgb[q], pub[q]], w=[uTb])
                        if fc == 10 and g + 1 < NG:
                            norm1(g + 1)
                    modg.need((g * GT) // SLOT)
                    for tt in range(TPG):
                        tok0 = g * GT + tt * 128
                        xi = ep.load(src, tok0)
                        for h in range(2):
                            for fc in range(NFC):
                                k.op(pe, lambda: nc.tensor.matmul(py[h][:], uT[:, fc, tt * 128:(tt + 1) * 128],
                                                                  W2[:, fc, h * 512:(h + 1) * 512],
                                                                  start=(fc == 0), stop=(fc == NFC - 1)), r=[uTb], w=[pyb[h]])
                        ep.apply(xi, py, pyb, modg.t[mi + 2], modg.b[mi + 2], dst, tok0)
            state["src"] = dst
            return phase_done()

        def phase_att_in(l, kind):
            src = state["src"]
            es = contextlib.ExitStack()
            with es:
                Win = TT(es, "a_Win", [128, KC, 1536], BF16); Winb = k.buf("a_Win")
                es2 = contextlib.ExitStack()
                with es2:
                    pieces = [(Win[:, kc, :], at_w_in[kind][0, kc * 128:(kc + 1) * 128, :]) for kc in range(KC)]
                    load_weight(es2, None, Winb, pieces)
                    k.barrier()
                nwr = TT(es, "a_nwr", [128, 20, 64]); nwrb = k.buf("a_nwr")
                nws = TT(es, "a_nws", [128, 2, 64])
                k.dma([(nws[:, 0, :], at_qn[kind][0:1, :].partition_broadcast(128)),
                       (nws[:, 1, :], at_kn[kind][0:1, :].partition_broadcast(128))], w=[nwrb])
                k.op(dve, lambda: nc.vector.tensor_scalar(out=nwr[:, 0:16, :], in0=bc(nws[:, 0:1, :], [128, 16, 64]), scalar1=0.125,
                                                          scalar2=None, op0=ALU.mult), r=[nwrb], w=[nwrb])
                k.op(dve, lambda: nc.vector.tensor_copy(out=nwr[:, 16:20, :], in_=bc(nws[:, 1:2, :], [128, 4, 64])), r=[nwrb], w=[nwrb])
                xa = [TT(es, "a_xa%d" % i, [128, D]) for i in range(2)]
                xab = [k.buf("a_xa%d" % i) for i in range(2)]
                nctx = NormCtx(es)
                mods = ModTiles(es, l, [3, 4], "a")
                hT = [TT(es, "a_hT%d" % i, [128, KC, 128], BF16) for i in range(2)]
                hTb = [k.buf("a_hT%d" % i) for i in range(2)]
                pq = [PP(es, "a_pq%d" % i, [128, 512]) for i in range(3)]
                pqb = [k.buf("a_pq%d" % i) for i in range(3)]
                ptq = PP(es, "a_ptq", [128, 8, 128], BF16); ptqb = k.buf("a_ptq")
                ptk = PP(es, "a_ptk", [128, 2, 128], BF16); ptkb = k.buf("a_ptk")
                NB_ = 2
                qk = [TT(es, "a_qk%d" % i, [128, 20, 64]) for i in range(NB_)]
                qkb = [k.buf("a_qk%d" % i) for i in range(NB_)]
                sq = [TT(es, "a_sq%d" % i, [128, 20, 64]) for i in range(NB_)]
                sqb = [k.buf("a_sq%d" % i) for i in range(NB_)]
                t2 = [TT(es, "a_t2%d" % i, [128, 20, 64]) for i in range(NB_)]
                t2b = [k.buf("a_t2%d" % i) for i in range(NB_)]
                qr = [TT(es, "a_qr%d" % i, [128, 1280], BF16) for i in range(NB_)]
                qrb = [k.buf("a_qr%d" % i) for i in range(NB_)]
                vb = [TT(es, "a_vb%d" % i, [128, 4, 64], BF16) for i in range(NB_)]
                vbb = [k.buf("a_vb%d" % i) for i in range(NB_)]
                sm = [TT(es, "a_sm%d" % i, [128, 3, 20]) for i in range(NB_)]
                smb = [k.buf("a_sm%d" % i) for i in range(NB_)]
                cs = [TT(es, "a_cs%d" % i, [128, 2, 64]) for i in range(NB_)]
                csb = [k.buf("a_cs%d" % i) for i in range(NB_)]
                qT = [TT(es, "a_qT%d" % i, [128, 10, 128], BF16) for i in range(NB_)]
                qTb = [k.buf("a_qT%d" % i) for i in range(NB_)]
                hbk = 32 if kind == 0 else 16
                nbk = 64 // (2 * hbk)
                for t in range(NT):
                    i = t % 2
                    tok0 = t * 128
                    mods.need(tok0 // SLOT)
                    k.dma([(xa[i][:], src[tok0:tok0 + 128, :])], w=[xab[i]])
                    k.dma([(cs[i][:, 0, :], ropec[kind][tok0:tok0 + 128, :]), (cs[i][:, 1, :], ropes[kind][tok0:tok0 + 128, :])], w=[csb[i]])
                    nctx.part1(xa[i][:], xab[i], mods.t[4][:], mods.b[4], mods.t[3][:], mods.b[3])
                    nctx.part2(hT[i][:], hTb[i])
                    for n in range(3):
                        for kc in range(KC):
                            k.op(pe, lambda: nc.tensor.matmul(pq[n][:], hT[i][:, kc, :], Win[:, kc, n * 512:(n + 1) * 512],
                                                              start=(kc == 0), stop=(kc == KC - 1)), r=[hTb[i], Winb], w=[pqb[n]])
                    qkf = qk[i][:].rearrange("p h d -> p (h d)")
                    k.op(act, lambda: nc.scalar.copy(out=qkf[:, 0:512], in_=pq[0][:]), r=[pqb[0]], w=[qkb[i]])
                    k.op(act, lambda: nc.scalar.copy(out=qkf[:, 512:1024], in_=pq[1][:]), r=[pqb[1]], w=[qkb[i]])
                    k.op(act, lambda: nc.scalar.copy(out=qkf[:, 1024:1280], in_=pq[2][:, 0:256]), r=[pqb[2]], w=[qkb[i]])
                    k.op(act, lambda: nc.scalar.copy(out=vb[i][:].rearrange("p h d -> p (h d)"), in_=pq[2][:, 256:512]),
                         r=[pqb[2]], w=[vbb[i]])
                    k.dma([(Vd[:, :, t, :].rearrange("g p d -> p g d"), vb[i][:])], r=[vbb[i]])
                    k.op(act, lambda: nc.scalar.activation(out=sq[i][:], in_=qk[i][:], func=AF.Square), r=[qkb[i]], w=[sqb[i]])
                    k.op(dve, lambda: nc.vector.tensor_reduce(out=sm[i][:, 0, :], in_=sq[i][:], axis=AX.X, op=ALU.add), r=[sqb[i]], w=[smb[i]])
                    k.op(dve, lambda: nc.vector.tensor_scalar(out=sm[i][:, 1, :], in0=sm[i][:, 0, :], scalar1=1.0 / 64, scalar2=EPS,
                                                              op0=ALU.mult, op1=ALU.add), r=[smb[i]], w=[smb[i]])
                    k.op(pool, lambda: nc.gpsimd.tensor_tensor(out=sm[i][:, 2, :], in0=sm[i][:, 1, :], in1=nhalf[:, 0:20], op=ALU.pow),
                         r=[smb[i]], w=[smb[i]])
                    k.op(dve, lambda: nc.vector.tensor_tensor(out=qk[i][:], in0=qk[i][:], in1=bc(sm[i][:, 2, :].unsqueeze(2), [128, 20, 64]),
                                                              op=ALU.mult), r=[smb[i]], w=[qkb[i]])
                    k.op(pool, lambda: nc.gpsimd.tensor_tensor(out=qk[i][:], in0=qk[i][:], in1=nwr[:], op=ALU.mult), r=[nwrb], w=[qkb[i]])
                    k.op(dve, lambda: nc.vector.tensor_tensor(out=sq[i][:], in0=qk[i][:], in1=bc(cs[i][:, 0:1, :], [128, 20, 64]), op=ALU.mult),
                         r=[qkb[i], csb[i]], w=[sqb[i]])
                    q5 = qk[i][:].rearrange("p h (b two e) -> p h b two e", two=2, e=hbk)
                    t5 = t2[i][:].rearrange("p h (b two e) -> p h b two e", two=2, e=hbk)
                    s5 = cs[i][:, 1:2, :].rearrange("p o (b two e) -> p o b two e", two=2, e=hbk)
                    for half in range(2):
                        k.op(pool, lambda: nc.gpsimd.tensor_tensor(out=t5[:, :, :, half, :], in0=q5[:, :, :, 1 - half, :],
                                                                   in1=bc(s5[:, :, :, half, :], [128, 20, nbk, hbk]), op=ALU.mult),
                             r=[qkb[i], csb[i]], w=[t2b[i]])
                    k.op(dve, lambda: nc.vector.tensor_tensor(out=qr[i][:], in0=sq[i][:].rearrange("p h d -> p (h d)"),
                                                              in1=t2[i][:].rearrange("p h d -> p (h d)"), op=ALU.add),
                         r=[sqb[i], t2b[i]], w=[qrb[i]])
                    for j in range(8):
                        k.op(pe, lambda: nc.tensor.transpose(ptq[:, j, :], qr[i][:, j * 128:(j + 1) * 128], identb[:]), r=[qrb[i]], w=[ptqb])
                    for j in range(2):
                        k.op(pe, lambda: nc.tensor.transpose(ptk[:, j, :], qr[i][:, 1024 + j * 128:1024 + (j + 1) * 128], identb[:]),
                             r=[qrb[i]], w=[ptkb])
                    k.op(act, lambda: nc.scalar.copy(out=qT[i][:, 0:8, :], in_=ptq[:]), r=[ptqb], w=[qTb[i]])
                    k.op(act, lambda: nc.scalar.copy(out=qT[i][:, 8:10, :], in_=ptk[:]), r=[ptkb], w=[qTb[i]])
                    k.dma([(QTK[:, :, tok0:tok0 + 128].rearrange("j p t -> p j t"), qT[i][:])], r=[qTb[i]])
            return phase_done()

        def phase_att(l, kind):
            es = contextlib.ExitStack()
            with es:
                GS = 2 if kind == 1 else 1
                ps = [PP(es, "t_ps%d" % i, [128, 1024]) for i in range(2)]
                psb = [k.buf("t_ps%d" % i) for i in range(2)]
                po = [PP(es, "t_po%d" % i, [65, 512]) for i in range(2)]
                pob = [k.buf("t_po%d" % i) for i in range(2)]
                pbt = [PP(es, "t_pb%d" % i, [64, 512]) for i in range(2)]
                pbb = [k.buf("t_pb%d" % i) for i in range(2)]
                KT = [TT(es, "t_KT%d" % i, [64, T], BF16) for i in range(2)]
                KTb = [k.buf("t_KT%d" % i) for i in range(2)]
                V = [TT(es, "t_V%d" % i, [128, NT, 65], BF16) for i in range(2)]
                Vb = [k.buf("t_V%d" % i) for i in range(2)]
                am = TT(es, "t_am", [128, 64]); amb = k.buf("t_am")
                sw = TT(es, "t_sw", [128, NT * 2])
                k.dma([(am[:], amask_d[:, :]), (sw[:], swab_d[:, :])], w=[amb])
                es_s = TT(es, "t_ess", [65, 16])
                esx = TT(es, "t_esx", [65, 16, 128]); esb = k.buf("t_esx")
                if kind == 0:
                    k.dma([(es_s[64:65, :], swa_sink[0:1, :])], w=[esb])
                    k.op(act, lambda: nc.scalar.activation(out=es_s[64:65, :], in_=es_s[64:65, :], func=AF.Exp), r=[esb], w=[esb])
                    k.op(dve, lambda: nc.vector.tensor_copy(out=esx[64:65, :, :], in_=bc(es_s[64:65, :].unsqueeze(2), [1, 16, 128])),
                         r=[esb], w=[esb])
                for i in range(2):
                    k.op(pool, lambda: nc.gpsimd.memset(V[i][:, :, 64:65], 1.0), w=[Vb[i]])
                NQ = 4
                qt = [TT(es, "t_qt%d" % i, [64, 4, 128], BF16) for i in range(NQ)]
                qtb = [k.buf("t_qt%d" % i) for i in range(NQ)]
                NP = 3
                pt = [TT(es, "t_pt%d" % i, [128, 2, 4, 128], BF16) for i in range(NP)]
                ptb = [k.buf("t_pt%d" % i) for i in range(NP)]
                R = [TT(es, "t_R%d" % i, [65, 512]) for i in range(2)]
                Rb = [k.buf("t_R%d" % i) for i in range(2)]
                rec = [TT(es, "t_rec%d" % i, [64, 512]) for i in range(2)]
                recb = [k.buf("t_rec%d" % i) for i in range(2)]
                ot = [TT(es, "t_ot%d" % i, [64, 4, 128], BF16) for i in range(2)]
                otb = [k.buf("t_ot%d" % i) for i in range(2)]

                def load_kv(g):
                    i = g % 2
                    CK = min(T, 2048)
                    k.dma([(KT[i][:, c0:c0 + CK], QTK[8 + g // 2, (g % 2) * 64:(g % 2) * 64 + 64, c0:c0 + CK]) for c0 in range(0, T, CK)], w=[KTb[i]])
                    VB = min(NT, 8)
                    k.dma([(V[i][:, b0:b0 + VB, 0:64], Vd[g, :, b0:b0 + VB, :]) for b0 in range(0, NT, VB)], w=[Vb[i]])

                items = []
                for g in range(4):
                    for i in range(NT):
                        if kind == 0:
                            js = [j for j in (i - 1, i, i + 1) if 0 <= j < NT]
                            grs = [[j] for j in js]
                        else:
                            grs = [list(range(j0, j0 + GS)) for j0 in range(0, NT, GS)]
                        for n, gr in enumerate(grs):
                            items.append((g, i, gr, n == 0, n == len(grs) - 1))
                qslot = {}
                cn = {"q": 0}

                def stage_S(n):
                    g, i, gr, first, last = items[n]
                    gi = g % 2
                    if first:
                        if i == 0 and g == 0:
                            load_kv(0)
                        if i == 1 and g + 1 < 4:
                            load_kv(g + 1)
                        qi = cn["q"] % NQ
                        cn["q"] += 1
                        qslot[(g, i)] = qi
                        k.dma([(qt[qi][:], QTK[2 * g:2 * g + 2, :, i * 128:(i + 1) * 128].rearrange("j (hh d) t -> d (j hh) t", hh=2))],
                              w=[qtb[qi]])
                    qi = qslot[(g, i)]
                    si = n % 2
                    for s_, j in enumerate(gr):
                        k.op(pe, lambda: nc.tensor.matmul(ps[si][:, s_ * 512:(s_ + 1) * 512], KT[gi][:, j * 128:(j + 1) * 128],
                                                          qt[qi][:].rearrange("d h t -> d (h t)"), start=True, stop=True),
                             r=[KTb[gi], qtb[qi]], w=[psb[si]])

                def stage_E(n):
                    g, i, gr, first, last = items[n]
                    si = n % 2
                    pi = n % NP
                    j = gr[0]
                    if kind == 1:
                        col = (i // BPG) * 8 + (j // BPG)
                        bias = am[:, col:col + 1]
                    elif j == i:
                        bias = 0.0
                    elif j < i:
                        bias = sw[:, 2 * i:2 * i + 1]
                    else:
                        bias = sw[:, 2 * i + 1:2 * i + 2]
                    ng = len(gr)
                    ptf = pt[pi][:, 0:ng].rearrange("k s h t -> k (s h t)")
                    k.op(act, lambda: nc.scalar.activation(out=ptf, in_=ps[si][:, 0:ng * 512], func=AF.Exp, bias=bias), r=[psb[si], amb], w=[ptb[pi]])
                    if kind == 0 and j != i:
                        tri = TRIr if j < i else TRIi
                        k.op(dve, lambda: nc.vector.tensor_tensor(out=pt[pi][:, 0], in0=pt[pi][:, 0], in1=bc(tri[:].unsqueeze(1), [128, 4, 128]),
                                                                  op=ALU.mult), r=[cb], w=[ptb[pi]])

                def stage_PV(n):
                    g, i, gr, first, last = items[n]
                    gi = g % 2
                    pi = n % NP
                    oi = (g * NT + i) % 2
                    for s_, j in enumerate(gr):
                        k.op(pe, lambda: nc.tensor.matmul(po[oi][:], V[gi][:, j, :], pt[pi][:, s_].rearrange("k h t -> k (h t)"),
                                                          start=(first and s_ == 0), stop=(last and s_ == len(gr) - 1)),
                             r=[Vb[gi], ptb[pi]], w=[pob[oi]])

                def epi1(n):
                    g, i, gr, first, last = items[n]
                    oi = (g * NT + i) % 2
                    k.op(dve, lambda: nc.vector.tensor_copy(out=R[oi][:], in_=po[oi][:]), r=[pob[oi]], w=[Rb[oi]])
                    if kind == 0:
                        k.op(dve, lambda: nc.vector.tensor_tensor(out=R[oi][64:65, :], in0=R[oi][64:65, :],
                                                                  in1=esx[64:65, 4 * g:4 * g + 4, :].rearrange("p h t -> p (h t)"), op=ALU.add),
                             r=[esb], w=[Rb[oi]])

                def epi2(n):
                    g, i, gr, first, last = items[n]
                    oi = (g * NT + i) % 2
                    k.op(pe, lambda: nc.tensor.matmul(pbt[oi][:], E65[:], R[oi][:], start=True, stop=True), r=[Rb[oi], cb], w=[pbb[oi]])
                    k.op(dve, lambda: nc.vector.reciprocal(out=rec[oi][:], in_=pbt[oi][:]), r=[pbb[oi]], w=[recb[oi]])
                    k.op(dve, lambda: nc.vector.tensor_tensor(out=ot[oi][:].rearrange("d h t -> d (h t)"), in0=R[oi][0:64, :], in1=rec[oi][:],
                                                              op=ALU.mult), r=[Rb[oi], recb[oi]], w=[otb[oi]])
                    k.dma([(OTd[4 * g:4 * g + 4, :, i * 128:(i + 1) * 128].rearrange("h d t -> d h t"), ot[oi][:])], r=[otb[oi]])

                NI = len(items)
                stage_S(0)
                pend = None
                for n in range(NI):
                    if n + 1 < NI:
                        stage_S(n + 1)
                    if pend is not None:
                        epi2(pend)
                        pend = None
                    stage_E(n)
                    stage_PV(n)
                    if items[n][4]:
                        epi1(n)
                        pend = n
                if pend is not None:
                    epi2(pend)
            return phase_done()

        def phase_att_out(l, kind, dst):
            src = state["src"]
            es = contextlib.ExitStack()
            with es:
                Wo = TT(es, "o_Wo", [64, 16, D], BF16); Wob = k.buf("o_Wo")
                es2 = contextlib.ExitStack()
                with es2:
                    pieces = [(Wo[:, h, :], at_w_out[kind][0, h * 64:(h + 1) * 64, :]) for h in range(16)]
                    load_weight(es2, None, Wob, pieces)
                    k.barrier()
                mods = ModTiles(es, l, [5], "o")
                ep = Epilogue(es)
                oT = [TT(es, "o_oT%d" % i, [64, 16, 128], BF16) for i in range(3)]
                oTb = [k.buf("o_oT%d" % i) for i in range(3)]
                py = [[PP(es, "o_py%d_%d" % (i, h), [128, 512]) for h in range(2)] for i in range(2)]
                pyb = [[k.buf("o_py%d_%d" % (i, h)) for h in range(2)] for i in range(2)]
                for t in range(NT):
                    tok0 = t * 128
                    i3 = t % 3
                    i2 = t % 2
                    mods.need(tok0 // SLOT)
                    k.dma([(oT[i3][:], OTd[:, :, tok0:tok0 + 128].rearrange("h d t -> d h t"))], w=[oTb[i3]])
                    xi = ep.load(src, tok0)
                    for h in range(2):
                        for hd in range(16):
                            k.op(pe, lambda: nc.tensor.matmul(py[i2][h][:], oT[i3][:, hd, :], Wo[:, hd, h * 512:(h + 1) * 512],
                                                              start=(hd == 0), stop=(hd == 15)), r=[oTb[i3], Wob], w=[pyb[i2][h]])
                    ep.apply(xi, py[i2], pyb[i2], mods.t[5], mods.b[5], dst, tok0)
            state["src"] = dst
            return phase_done()

        def phase_ml_in(l, j):
            src = state["src"]
            es = contextlib.ExitStack()
            with es:
                Win = TT(es, "m_Win", [128, KC, 3104], BF16); Winb = k.buf("m_Win")
                es2 = contextlib.ExitStack()
                with es2:
                    pieces = []
                    for kc in range(KC):
                        pieces.append((Win[:, kc, 0:1552], ml_w_in[j, kc * 128:(kc + 1) * 128, 0:1552]))
                        pieces.append((Win[:, kc, 1552:3104], ml_w_in[j, kc * 128:(kc + 1) * 128, 1552:3104]))
                    load_weight(es2, None, Winb, pieces)
                    k.barrier()
                bg = TT(es, "m_bg", [128, 32]); bgb = k.buf("m_bg")
                k.dma([(bg[:], ml_bg[j:j + 1, :].partition_broadcast(128))], w=[bgb])
                xa = [TT(es, "mi_xa%d" % i, [128, D]) for i in range(2)]
                xab = [k.buf("mi_xa%d" % i) for i in range(2)]
                nctx = NormCtx(es)
                mods = ModTiles(es, l, [3, 4], "mi")
                hT = [TT(es, "mi_hT%d" % i, [128, KC, 128], BF16) for i in range(2)]
                hTb = [k.buf("mi_hT%d" % i) for i in range(2)]
                pq = [PP(es, "mi_pq%d" % i, [128, 512]) for i in range(4)]
                pqb = [k.buf("mi_pq%d" % i) for i in range(4)]
                ptr = PP(es, "mi_ptr", [128, 8, 128], BF16); ptrb = k.buf("mi_ptr")
                qkb_ = [TT(es, "mi_qkb%d" % i, [128, 1024], BF16) for i in range(2)]
                qkbb = [k.buf("mi_qkb%d" % i) for i in range(2)]
                kf = [TT(es, "mi_kf%d" % i, [128, 512]) for i in range(2)]
                kfb = [k.buf("mi_kf%d" % i) for i in range(2)]
                vt = [TT(es, "mi_vt%d" % i, [128, D], BF16) for i in range(2)]
                vtb = [k.buf("mi_vt%d" % i) for i in range(2)]
                so = [TT(es, "mi_so%d" % i, [128, D]) for i in range(2)]
                sob = [k.buf("mi_so%d" % i) for i in range(2)]
                gg = [TT(es, "mi_gg%d" % i, [128, 4, 32]) for i in range(2)]
                ggb = [k.buf("mi_gg%d" % i) for i in range(2)]
                qT = [TT(es, "mi_qT%d" % i, [128, 8, 128], BF16) for i in range(2)]
                qTb = [k.buf("mi_qT%d" % i) for i in range(2)]
                cn = 0
                for t in range(NT):
                    i = t % 2
                    tok0 = t * 128
                    mods.need(tok0 // SLOT)
                    k.dma([(xa[i][:], src[tok0:tok0 + 128, :])], w=[xab[i]])
                    nctx.part1(xa[i][:], xab[i], mods.t[4][:], mods.b[4], mods.t[3][:], mods.b[3])
                    nctx.part2(hT[i][:], hTb[i])
                    for n in range(7):
                        q = cn % 4
                        cn += 1
                        w_ = 512 if n < 6 else 32
                        for kc in range(KC):
                            k.op(pe, lambda: nc.tensor.matmul(pq[q][:, 0:w_], hT[i][:, kc, :], Win[:, kc, n * 512:n * 512 + w_],
                                                              start=(kc == 0), stop=(kc == KC - 1)), r=[hTb[i], Winb], w=[pqb[q]])
                        SK = os.environ.get("ML_SKIP", "")
                        if n == 0 and "q" in SK: pass
                        elif n == 1 and "k" in SK: pass
                        elif n in (2, 3) and "v" in SK: pass
                        elif n in (4, 5) and "o" in SK: pass
                        elif n == 0:
                            k.op(act, lambda: nc.scalar.activation(out=qkb_[i][:, 0:512], in_=pq[q][:], func=AF.Copy, scale=0.125),
                                 r=[pqb[q]], w=[qkbb[i]])
                        elif n == 1:
                            k.op(dve, lambda: nc.vector.tensor_copy(out=kf[i][:], in_=pq[q][:]), r=[pqb[q]], w=[kfb[i]])
                            k.op(act, lambda: nc.scalar.copy(out=qkb_[i][:, 512:1024], in_=kf[i][:]), r=[kfb[i]], w=[qkbb[i]])
                            k.dma([(Ktm[tok0:tok0 + 128, :], kf[i][:])], r=[kfb[i]])
                        elif n in (2, 3):
                            k.op(dve, lambda: nc.vector.tensor_copy(out=vt[i][:, (n - 2) * 512:(n - 1) * 512], in_=pq[q][:]), r=[pqb[q]], w=[vtb[i]])
                            if n == 3:
                                k.dma([(Vm[tok0:tok0 + 128, :], vt[i][:])], r=[vtb[i]])
                        elif n in (4, 5):
                            k.op(act, lambda: nc.scalar.activation(out=so[i][:, (n - 4) * 512:(n - 3) * 512], in_=pq[q][:], func=AF.Sigmoid),
                                 r=[pqb[q]], w=[sob[i]])
                            if n == 5:
                                k.dma([(SOd[tok0:tok0 + 128, :], so[i][:])], r=[sob[i]])
                        elif "g" not in SK:
                            G = gg[i]
                            k.op(dve, lambda: nc.vector.tensor_tensor(out=G[:, 0, :], in0=pq[q][:, 0:32], in1=bg[:], op=ALU.add),
                                 r=[pqb[q], bgb], w=[ggb[i]])
                            k.op(act, lambda: nc.scalar.activation(out=G[:, 1, :], in_=G[:, 0, :], func=AF.Tanh, scale=1.0 / 15.0), r=[ggb[i]], w=[ggb[i]])
                            k.op(dve, lambda: nc.vector.tensor_scalar(out=G[:, 0, :], in0=G[:, 1, :], scalar1=15.0, scalar2=None, op0=ALU.mult),
                                 r=[ggb[i]], w=[ggb[i]])
                            k.op(act, lambda: nc.scalar.activation(out=G[:, 1, :], in_=G[:, 0, :], func=AF.Exp, scale=-1.0), r=[ggb[i]], w=[ggb[i]])
                            k.op(act, lambda: nc.scalar.activation(out=G[:, 2, :], in_=G[:, 1, :], func=AF.Ln, bias=1.0), r=[ggb[i]], w=[ggb[i]])
                            g4 = G[:, 0, :].rearrange("p (a b h) -> p a b h", a=2, b=2)
                            l4 = G[:, 2, :].rearrange("p (a b h) -> p a b h", a=2, b=2)
                            k.op(dve, lambda: nc.vector.tensor_scalar(out=g4[:, :, 1, :], in0=l4[:, :, 1, :], scalar1=-1.0, scalar2=None, op0=ALU.mult),
                                 r=[ggb[i]], w=[ggb[i]])
                            k.dma([(GTd[tok0:tok0 + 128, :], G[:, 0, :])], r=[ggb[i]])
                    if "t" in SK:
                        continue
                    for jj in range(8):
                        k.op(pe, lambda: nc.tensor.transpose(ptr[:, jj, :], qkb_[i][:, jj * 128:(jj + 1) * 128], identb[:]), r=[qkbb[i]], w=[ptrb])
                    k.op(act, lambda: nc.scalar.copy(out=qT[i][:], in_=ptr[:]), r=[ptrb], w=[qTb[i]])
                    k.dma([(MQK[:, :, tok0:tok0 + 128].rearrange("j p t -> p j t"), qT[i][:])], r=[qTb[i]])
            return phase_done()

        def phase_ml_scan(l, j, direction):
            fwd = direction == 0
            TRI = TRIi if fwd else TRIr
            CAP = CAPi if fwd else CAPr
            es = contextlib.ExitStack()
            with es:
                keep = TT(es, "s_keep", [128, NT]); keepb_ = k.buf("s_keep")
                k.dma([(keep[:], (keepf_d if fwd else keepb_d)[:, :])], w=[keepb_])
                nwr = TT(es, "s_nwr", [128, D]); nwrb = k.buf("s_nwr")
                k.dma([(nwr[:], ml_nw[j:j + 1, :].partition_broadcast(128))], w=[nwrb])
                C = TT(es, "s_C", [64, 8, 129]); Cb_ = k.buf("s_C")
                Cbf = TT(es, "s_Cbf", [64, 8, 129], BF16); Cbfb = k.buf("s_Cbf")
                k.op(dve, lambda: nc.vector.memset(C[:], 0.0), w=[Cb_])
                k.op(pool, lambda: nc.gpsimd.memset(Cbf[:], 0.0), w=[Cbfb])
                NB_ = 2
                mk = lambda nm, shape, dt=F32: ([TT(es, "s_%s%d" % (nm, i), shape, dt) for i in range(NB_)],
                                                [k.buf("s_%s%d" % (nm, i)) for i in range(NB_)])
                QT, QTb = mk("QT", [64, 8, 128], BF16)
                KTt, KTb = mk("KT", [64, 8, 128], BF16)
                Kt, Ktb = mk("Kt", [128, 8, 64])
                Va, Vab = mk("Va", [128, 8, 129], BF16)
                Gt, Gtb = mk("Gt", [128, 32])
                for i in range(NB_):
                    k.op(pool, lambda: nc.gpsimd.memset(Va[i][:, :, 128:129], 1.0), w=[Vab[i]])
                gs, gsb = mk("gs", [128, 6, 8])
                bT, bTb = mk("bT", [8, 128])
                arg, argb = mk("arg", [128, 8, 128])
                eb, ebb = mk("eb", [64, 8, 128])
                QTs, QTsb = mk("QTs", [64, 8, 128], BF16)
                ST, STb = mk("ST", [128, 8, 128], BF16)
                Kw, Kwb = mk("Kw", [128, 8, 64], BF16)
                hd, hdb = mk("hd", [128, 8, 128])
                dn, dnb = mk("dn", [128, 2, 8])
                hfl, hflb = mk("hfl", [128, D])
                sot, sotb = mk("sot", [128, D])
                sq2, sq2b = mk("sq2", [128, 8, 128])
                sm2, sm2b = mk("sm2", [128, 3, 8])
                hg, hgb = mk("hg", [128, D], BF16)
                psm = PP(es, "s_psm", [128, 512]); psmb = k.buf("s_psm")
                pbc = [PP(es, "s_pbc%d" % i, [128, 4, 128]) for i in range(2)]
                pbcb = [k.buf("s_pbc%d" % i) for i in range(2)]
                pss = [PP(es, "s_pss%d" % i, [128, 4, 128]) for i in range(2)]
                pssb = [k.buf("s_pss%d" % i) for i in range(2)]
                hsplit = [(0, 3), (3, 6), (6, 8)]
                pov = [PP(es, "s_po%d" % i, [128, 3, 129]) for i in range(3)]
                pob = [k.buf("s_po%d" % i) for i in range(3)]
                order = list(range(NT)) if fwd else list(range(NT - 1, -1, -1))
                gb = 0 if fwd else 16
                for n_, c in enumerate(order):
                    i = n_ % 2
                    tok0 = c * 128
                    k.dma([(QT[i][:], MQK[0:4, :, tok0:tok0 + 128].rearrange("j (hh d) t -> d (j hh) t", hh=2))], w=[QTb[i]])
                    k.dma([(KTt[i][:], MQK[4:8, :, tok0:tok0 + 128].rearrange("j (hh d) t -> d (j hh) t", hh=2))], w=[KTb[i]])
                    k.dma([(Kt[i][:].rearrange("p h d -> p (h d)"), Ktm[tok0:tok0 + 128, :])], w=[Ktb[i]])
                    k.dma([(Va[i][:, :, 0:128], Vm[tok0:tok0 + 128, :].rearrange("p (h d) -> p h d", h=8))], w=[Vab[i]])
                    k.dma([(Gt[i][:], GTd[tok0:tok0 + 128, :])], w=[Gtb[i]])
                    if not fwd:
                        k.dma([(hfl[i][:], Hf[tok0:tok0 + 128, :])], w=[hflb[i]])
                        k.dma([(sot[i][:], SOd[tok0:tok0 + 128, :])], w=[sotb[i]])
                    ig = Gt[i][:, gb:gb + 8]
                    lf = Gt[i][:, gb + 8:gb + 16]
                    S_ = gs[i]
                    k.op(pe, lambda: nc.tensor.matmul(psm[:, 0:8], TRI[:], lf, start=True, stop=True), r=[Gtb[i], cb], w=[psmb])
                    k.op(pe, lambda: nc.tensor.matmul(psm[:, 8:16], ONESf[:], lf, start=True, stop=True), r=[Gtb[i], cb], w=[psmb])
                    k.op(pe, lambda: nc.tensor.matmul(psm[0:8, 128:256], lf, TRI[:], start=True, stop=True), r=[Gtb[i], cb], w=[psmb])
                    k.op(dve, lambda: nc.vector.tensor_copy(out=S_[:, 0, :], in_=psm[:, 0:8]), r=[psmb], w=[gsb[i]])
                    k.op(dve, lambda: nc.vector.tensor_copy(out=S_[:, 4, :], in_=psm[:, 8:16]), r=[psmb], w=[gsb[i]])
                    k.op(dve, lambda: nc.vector.tensor_copy(out=bT[i][:], in_=psm[0:8, 128:256]), r=[psmb], w=[bTb[i]])
                    k.op(dve, lambda: nc.vector.tensor_tensor(out=S_[:, 1, :], in0=ig, in1=S_[:, 0, :], op=ALU.subtract), r=[Gtb[i]], w=[gsb[i]])
                    k.op(dve, lambda: nc.vector.tensor_tensor(out=S_[:, 2, :], in0=S_[:, 1, :], in1=S_[:, 4, :], op=ALU.add), w=[gsb[i]])
                    k.op(act, lambda: nc.scalar.activation(out=S_[:, 2, :], in_=S_[:, 2, :], func=AF.Exp), w=[gsb[i]])
                    k.op(act, lambda: nc.scalar.activation(out=S_[:, 3, :], in_=S_[:, 4, :], func=AF.Exp), w=[gsb[i]])
                    for h in range(8):
                        k.op(pe, lambda: nc.tensor.matmul(pbc[h // 4][:, h % 4, :], Sel[0:8, h, :], bT[i][:], start=True, stop=True),
                             r=[bTb[i], cb], w=[pbcb[h // 4]])
                    for hh in range(2):
                        for h4 in range(4):
                            h = 4 * hh + h4
                            k.op(dve, lambda: nc.vector.scalar_tensor_tensor(out=arg[i][:, h, :], in0=pbc[hh][:, h4, :], scalar=S_[:, 1, h:h + 1],
                                                                             in1=CAP[:], op0=ALU.add, op1=ALU.min),
                                 r=[pbcb[hh], gsb[i], cb], w=[argb[i]])
                        k.op(dve, lambda: nc.vector.tensor_copy(out=eb[i][:, 4 * hh:4 * hh + 4, :], in_=pbc[hh][0:64]),
                             r=[pbcb[hh]], w=[ebb[i]])
                        k.op(act, lambda: nc.scalar.activation(out=eb[i][:, 4 * hh:4 * hh + 4, :], in_=eb[i][:, 4 * hh:4 * hh + 4, :], func=AF.Exp),
                             w=[ebb[i]])
                    k.op(act, lambda: nc.scalar.activation(out=arg[i][:], in_=arg[i][:], func=AF.Exp), w=[argb[i]])
                    k.op(dve, lambda: nc.vector.tensor_tensor(out=QTs[i][:], in0=QT[i][:], in1=eb[i][:], op=ALU.mult), r=[QTb[i], ebb[i]], w=[QTsb[i]])
                    for h in range(8):
                        k.op(pe, lambda: nc.tensor.matmul(pss[h // 4][:, h % 4, :], KTt[i][:, h, :], QT[i][:, h, :], start=True, stop=True),
                             r=[KTb[i], QTb[i]], w=[pssb[h // 4]])
                    for hh in range(2):
                        k.op(dve, lambda: nc.vector.tensor_tensor(out=ST[i][:, 4 * hh:4 * hh + 4, :], in0=pss[hh][:], in1=arg[i][:, 4 * hh:4 * hh + 4, :],
                                                                  op=ALU.mult), r=[pssb[hh], argb[i]], w=[STb[i]])
                    k.op(dve, lambda: nc.vector.tensor_tensor(out=Kw[i][:], in0=Kt[i][:], in1=bc(S_[:, 2, :].unsqueeze(2), [128, 8, 64]), op=ALU.mult),
                         r=[Ktb[i], gsb[i]], w=[Kwb[i]])
                    for h in range(8):
                        pi_ = 0 if h < 3 else (1 if h < 6 else 2)
                        hl = h - hsplit[pi_][0]
                        k.op(pe, lambda: nc.tensor.matmul(pov[pi_][:, hl, :], ST[i][:, h, :], Va[i][:, h, :], start=True, stop=False),
                             r=[STb[i], Vab[i]], w=[pob[pi_]])
                        k.op(pe, lambda: nc.tensor.matmul(pov[pi_][:, hl, :], QTs[i][:, h, :], Cbf[:, h, :], start=False, stop=True),
                             r=[QTsb[i], Cbfb], w=[pob[pi_]])
                    for pi_, (h0, h1) in enumerate(hsplit):
                        nh = h1 - h0
                        k.op(dve, lambda: nc.vector.tensor_copy(out=dn[i][:, 1, h0:h1], in_=pov[pi_][:, 0:nh, 128]), r=[pob[pi_]], w=[dnb[i]])
                    k.op(dve, lambda: nc.vector.tensor_tensor(out=dn[i][:, 0, :], in0=dn[i][:, 1, :], in1=dn[i][:, 1, :], op=ALU.mult), w=[dnb[i]])
                    k.op(dve, lambda: nc.vector.tensor_scalar(out=dn[i][:, 0, :], in0=dn[i][:, 0, :], scalar1=1.0, scalar2=None, op0=ALU.max), w=[dnb[i]])
                    k.op(pool, lambda: nc.gpsimd.tensor_tensor(out=dn[i][:, 1, :], in0=dn[i][:, 0, :], in1=nhalf[:, 0:8], op=ALU.pow), w=[dnb[i]])
                    for pi_, (h0, h1) in enumerate(hsplit):
                        nh = h1 - h0
                        k.op(dve, lambda: nc.vector.tensor_tensor(out=hd[i][:, h0:h1, :], in0=pov[pi_][:, 0:nh, 0:128],
                                                                  in1=bc(dn[i][:, 1, h0:h1].unsqueeze(2), [128, nh, 128]), op=ALU.mult),
                             r=[pob[pi_], dnb[i]], w=[hdb[i]])
                    hdf = hd[i][:].rearrange("p h d -> p (h d)")
                    if fwd:
                        k.dma([(Hf[tok0:tok0 + 128, :], hdf)], r=[hdb[i]])
                    else:
                        k.op(dve, lambda: nc.vector.tensor_tensor(out=hdf, in0=hdf, in1=hfl[i][:], op=ALU.add), r=[hflb[i]], w=[hdb[i]])
                        k.op(act, lambda: nc.scalar.activation(out=sq2[i][:], in_=hd[i][:], func=AF.Square), r=[hdb[i]], w=[sq2b[i]])
                        k.op(dve, lambda: nc.vector.tensor_reduce(out=sm2[i][:, 0, :], in_=sq2[i][:], axis=AX.X, op=ALU.add), r=[sq2b[i]], w=[sm2b[i]])
                        k.op(dve, lambda: nc.vector.tensor_scalar(out=sm2[i][:, 1, :], in0=sm2[i][:, 0, :], scalar1=1.0 / 128, scalar2=EPS,
                                                                  op0=ALU.mult, op1=ALU.add), w=[sm2b[i]])
                        k.op(pool, lambda: nc.gpsimd.tensor_tensor(out=sm2[i][:, 2, :], in0=sm2[i][:, 1, :], in1=nhalf[:, 0:8], op=ALU.pow), w=[sm2b[i]])
                        k.op(dve, lambda: nc.vector.tensor_tensor(out=hd[i][:], in0=hd[i][:], in1=bc(sm2[i][:, 2, :].unsqueeze(2), [128, 8, 128]),
                                                                  op=ALU.mult), r=[sm2b[i]], w=[hdb[i]])
                        k.op(dve, lambda: nc.vector.tensor_tensor(out=hdf, in0=hdf, in1=nwr[:], op=ALU.mult), r=[nwrb], w=[hdb[i]])
                        k.op(dve, lambda: nc.vector.tensor_tensor(out=hg[i][:], in0=hdf, in1=sot[i][:], op=ALU.mult), r=[hdb[i], sotb[i]], w=[hgb[i]])
                        k.dma([(HGd[tok0:tok0 + 128, :], hg[i][:])], r=[hgb[i]])
                    for h in range(8):
                        pi_ = 0 if h < 3 else (1 if h < 6 else 2)
                        hl = h - hsplit[pi_][0]
                        k.op(pe, lambda: nc.tensor.matmul(pov[pi_][0:64, hl, :], Kw[i][:, h, :], Va[i][:, h, :], start=True, stop=True),
                             r=[Kwb[i], Vab[i]], w=[pob[pi_]])
                    k.op(dve, lambda: nc.vector.tensor_tensor(out=C[:], in0=C[:], in1=bc(S_[0:64, 3, :].unsqueeze(2), [64, 8, 129]), op=ALU.mult),
                         r=[gsb[i]], w=[Cb_])
                    for pi_, (h0, h1) in enumerate(hsplit):
                        nh = h1 - h0
                        k.op(dve, lambda: nc.vector.tensor_tensor(out=C[:, h0:h1, :], in0=C[:, h0:h1, :], in1=pov[pi_][0:64, 0:nh, :], op=ALU.add),
                             r=[pob[pi_]], w=[Cb_])
                    k.op(dve, lambda: nc.vector.tensor_scalar(out=C[:], in0=C[:], scalar1=keep[0:64, c:c + 1], scalar2=None, op0=ALU.mult),
                         r=[keepb_], w=[Cb_])
                    k.op(act, lambda: nc.scalar.copy(out=Cbf[:], in_=C[:]), r=[Cb_], w=[Cbfb])
            return phase_done()

        def phase_ml_out(l, j, dst):
            src = state["src"]
            es = contextlib.ExitStack()
            with es:
                Wo = TT(es, "mo_Wo", [128, KC, D], BF16); Wob = k.buf("mo_Wo")
                es2 = contextlib.ExitStack()
                with es2:
                    pieces = [(Wo[:, kc, :], ml_w_out[j, kc * 128:(kc + 1) * 128, :]) for kc in range(KC)]
                    load_weight(es2, None, Wob, pieces)
                    k.barrier()
                mods = ModTiles(es, l, [5], "mo")
                ep = Epilogue(es)
                hgt = [TT(es, "mo_hg%d" % i, [128, D], BF16) for i in range(2)]
                hgtb = [k.buf("mo_hg%d" % i) for i in range(2)]
                hT = [TT(es, "mo_hT%d" % i, [128, KC, 128], BF16) for i in range(2)]
                hTb = [k.buf("mo_hT%d" % i) for i in range(2)]
                pT = [PP(es, "mo_pT%d" % i, [128, 8, 128], BF16) for i in range(2)]
                pTb = [k.buf("mo_pT%d" % i) for i in range(2)]
                py = [[PP(es, "mo_py%d_%d" % (i, h), [128, 512]) for h in range(2)] for i in range(2)]
                pyb = [[k.buf("mo_py%d_%d" % (i, h)) for h in range(2)] for i in range(2)]
                for t in range(NT):
                    tok0 = t * 128
                    i = t % 2
                    mods.need(tok0 // SLOT)
                    k.dma([(hgt[i][:], HGd[tok0:tok0 + 128, :])], w=[hgtb[i]])
                    xi = ep.load(src, tok0)
                    for kc in range(KC):
                        k.op(pe, lambda: nc.tensor.transpose(pT[i][:, kc, :], hgt[i][:, kc * 128:(kc + 1) * 128], identb[:]), r=[hgtb[i]], w=[pTb[i]])
                    k.op(act, lambda: nc.scalar.copy(out=hT[i][:], in_=pT[i][:]), r=[pTb[i]], w=[hTb[i]])
                    for h in range(2):
                        for kc in range(KC):
                            k.op(pe, lambda: nc.tensor.matmul(py[i][h][:], hT[i][:, kc, :], Wo[:, kc, h * 512:(h + 1) * 512],
                                                              start=(kc == 0), stop=(kc == KC - 1)), r=[hTb[i], Wob], w=[pyb[i][h]])
                    ep.apply(xi, py[i], pyb[i], mods.t[5], mods.b[5], dst, tok0)
            state["src"] = dst
            return phase_done()

        def program():
            if phase_mod():
                return
            for l in range(NL):
                last = (l == NL - 1)
                if phase_ffn(l, 0, xs):
                    return
                kind, j = l % 3, l // 3
                if kind == 0:
                    if phase_ml_in(l, j): return
                    if phase_ml_scan(l, j, 0): return
                    if phase_ml_scan(l, j, 1): return
                    if phase_ml_out(l, j, xs): return
                else:
                    if phase_att_in(l, kind - 1): return
                    if phase_att(l, kind - 1): return
                    if phase_att_out(l, kind - 1, xs): return
                if phase_ffn(l, 1, y_out if last else xs):
                    return
        program()
        if state["src"] is not y_out:
            es = contextlib.ExitStack()
            with es:
                tb = [TT(es, "cp%d" % i, [128, D]) for i in range(2)]
                tbb = [k.buf("cp%d" % i) for i in range(2)]
                for t in range(NT):
                    k.dma([(tb[t % 2][:], state["src"][t * 128:(t + 1) * 128, :])], w=[tbb[t % 2]])
                    k.dma([(y_out[t * 128:(t + 1) * 128, :], tb[t % 2][:])], r=[tbb[t % 2]])
            k.barrier()
    return nc


def rope_np(pos, dim):
    inv = (np.float32(10000.0) ** (-np.arange(0, dim, 2, dtype=np.float32) / np.float32(dim))).astype(np.float32)
    ang = pos.astype(np.float32)[:, None] * inv[None, :]
    ang = np.concatenate([ang, ang], axis=-1)
    return np.cos(ang).astype(np.float32), np.sin(ang).astype(np.float32)


def core_tables(T, S):
    NT = T // 128
    bps = S // 128
    pos = np.arange(T) % S
    c, s = rope_np(pos, 64)
    s_sw = np.concatenate([-s[:, :32], s[:, 32:]], axis=1)
    rc, rs = rope_np(pos // 64, 32)
    cc, cs_ = rope_np(pos % 64, 32)
    c_ax = np.concatenate([rc, cc], axis=1)
    s_ax = np.concatenate([-rs[:, :16], rs[:, 16:], -cs_[:, :16], cs_[:, 16:]], axis=1)
    blk = np.arange(NT)
    keepf = ((blk + 1) % bps != 0).astype(np.float32)
    keepb = (blk % bps != 0).astype(np.float32)
    swab = np.zeros((NT, 2), np.float32)
    swab[blk % bps == 0, 0] = NEG
    swab[(blk + 1) % bps == 0, 1] = NEG
    slot_seq = (np.arange(8) * (T // 8)) // S
    amask = np.where(slot_seq[:, None] == slot_seq[None, :], 0.0, NEG).astype(np.float32)
    rep = lambda a: np.ascontiguousarray(np.broadcast_to(a.reshape(1, -1), (128, a.size))).astype(np.float32)
    return {
        "rope_swa_c": c, "rope_swa_s": s_sw.astype(np.float32), "rope_ax_c": c_ax.astype(np.float32), "rope_ax_s": s_ax.astype(np.float32),
        "keepf": rep(keepf), "keepb": rep(keepb), "swab": rep(swab), "amask": rep(amask),
    }


WNAMES = ["ffn_w13", "ffn_w2", "ada_w", "ada_b", "norm_w", "mlstm_w_in", "mlstm_b_gate", "mlstm_norm_w", "mlstm_w_out",
          "swa_w_in", "swa_q_norm", "swa_k_norm", "swa_sink", "swa_w_out", "axial_w_in", "axial_q_norm", "axial_k_norm", "axial_w_out"]


def run_streams(streams, weights, T, NL=4, stop=None, ncores=8):
    nc = build(T, NL, stop)
    w = {n: np.ascontiguousarray(np.asarray(weights[n], dtype=np.float32)) for n in WNAMES}
    in_maps = []
    for c in range(ncores):
        x, c8, S = streams[c] if c < len(streams) else streams[-1]
        m = {"x": np.ascontiguousarray(x, dtype=np.float32), "c8": np.ascontiguousarray(c8, dtype=np.float32)}
        m.update(core_tables(T, S))
        m.update(w)
        in_maps.append(m)
    res = run_bass_kernel_spmd(nc, in_maps, core_ids=list(range(ncores)))
    return [res.results[c]["y"] for c in range(len(streams))]


def kernel(x_prompt, x_sample, c_prompt, c_sample, **weights):
    x_prompt = np.asarray(x_prompt, dtype=np.float32)
    x_sample = np.asarray(x_sample, dtype=np.float32)
    c_prompt = np.asarray(c_prompt, dtype=np.float32)
    c_sample = np.asarray(c_sample, dtype=np.float32)
    T = 16384
    streams = []
    for b in range(2):
        streams.append((x_prompt[b], np.ascontiguousarray(np.broadcast_to(c_prompt[b:b + 1], (8, D))), 16384))
    for j in range(4):
        streams.append((x_sample[8 * j:8 * j + 8].reshape(T, D), c_sample[8 * j:8 * j + 8], 2048))
    ys = run_streams(streams, weights, T)
    y_prompt = np.stack([ys[0], ys[1]], axis=0).astype(np.float32)
    y_sample = np.concatenate([ys[2 + j].reshape(8, 2048, D) for j in range(4)], axis=0).astype(np.float32)
    return (y_prompt, y_sample)
```

```python
import bisect
import os
import contextlib
import numpy as np
import concourse.bass as bass
import concourse.mybir as mybir
from concourse.bass_utils import run_bass_kernel_spmd

F32 = mybir.dt.float32
BF16 = mybir.dt.bfloat16
AF = mybir.ActivationFunctionType
ALU = mybir.AluOpType
AX = mybir.AxisListType

D = 1024
DFF = 2816
NFC = 22
KC = 8
EPS = 1e-6
NEG = -30000.0
NDS = 56


class Eng:
    def __init__(s, name, h, sem):
        s.name, s.h, s.sem = name, h, sem
        s.cnt = 0
        s.idx = 0
        s.last = None
        s.sig_idx = []
        s.sig_cnt = []
        s.seen = {}


class DSem:
    def __init__(s, h, key):
        s.h, s.key, s.count = h, key, 0


class Buf:
    def __init__(s, name):
        s.name = name
        s.w = None
        s.r = {}
        s.dsem = None


class K:
    def __init__(s, nc, es):
        s.nc = nc
        mk = lambda n: es.enter_context(nc.semaphore(n))
        s.pe = Eng("pe", nc.tensor, mk("s_pe"))
        s.act = Eng("act", nc.scalar, mk("s_act"))
        s.dve = Eng("dve", nc.vector, mk("s_dve"))
        s.pool = Eng("pool", nc.gpsimd, mk("s_pool"))
        s.sp = Eng("sp", nc.sync, mk("s_sp"))
        s.engs = [s.pe, s.act, s.dve, s.pool, s.sp]
        s.dsems = [DSem(mk("d%d" % i), "d%d" % i) for i in range(NDS)]
        s.free = list(s.dsems)
        s.pbufs = []
        s.used = []
        s.pe_eager = True

    def buf(s, name):
        b = Buf(name)
        s.pbufs.append(b)
        return b

    def _wait(s, eng, ev):
        if ev[0] == "e":
            e2, idx = ev[1], ev[2]
            if e2 is eng and eng is s.pe:
                return
            i = bisect.bisect_left(e2.sig_idx, idx)
            if i < len(e2.sig_idx):
                c = e2.sig_cnt[i]
            else:
                assert e2.idx >= idx and e2.last is not None
                e2.last.then_inc(e2.sem, 1)
                e2.cnt += 1
                e2.sig_idx.append(e2.idx)
                e2.sig_cnt.append(e2.cnt)
                c = e2.cnt
            if eng.seen.get(e2.name, 0) >= c:
                return
            eng.h.wait_ge(e2.sem, c)
            eng.seen[e2.name] = c
        else:
            ds, val = ev[1], ev[2]
            if eng.seen.get(ds.key, 0) >= val:
                return
            eng.h.wait_ge(ds.h, val)
            eng.seen[ds.key] = val

    def _deps(s, eng, r, w):
        for b in r:
            if b.w is not None:
                s._wait(eng, b.w)
        for b in w:
            if b.w is not None:
                s._wait(eng, b.w)
            for ev in b.r.values():
                s._wait(eng, ev)

    def op(s, eng, fn, r=(), w=(), sig=None):
        s._deps(eng, r, w)
        ins = fn()
        eng.idx += 1
        eng.last = ins
        if sig is None:
            sig = (eng is not s.pe) or s.pe_eager
        if sig:
            ins.then_inc(eng.sem, 1)
            eng.cnt += 1
            eng.sig_idx.append(eng.idx)
            eng.sig_cnt.append(eng.cnt)
        ev = ("e", eng, eng.idx)
        for b in r:
            b.r[eng.name] = ev
        for b in w:
            b.w = ev
            b.r = {}
        return ins

    def dma(s, pairs, r=(), w=(), q=None):
        q = q or s.sp
        s._deps(q, r, w)
        owner = w[0] if len(w) else r[0]
        if owner.dsem is None:
            owner.dsem = s.free.pop()
            s.used.append(owner.dsem)
        ds = owner.dsem
        for (o, i) in pairs:
            q.h.dma_start(out=o, in_=i).then_inc(ds.h, 16)
            ds.count += 16
        ev = ("d", ds, ds.count)
        for b in r:
            b.r["dma_" + ds.key] = ev
        for b in w:
            b.w = ev
            b.r = {}

    def barrier(s):
        for e in s.engs:
            for e2 in s.engs:
                if e2 is not e and e2.idx > 0 and e2 is not s.sp:
                    s._wait(e, ("e", e2, e2.idx))
            for ds in s.used:
                if ds.count > 0:
                    s._wait(e, ("d", ds, ds.count))
        s.free = list(s.dsems)
        s.used = []
        s.pbufs = []


def bc(ap, shape):
    return ap.to_broadcast(list(shape))


def build(T, NL=4, stop=None):
    NT = T // 128
    SLOT = T // 8
    BPG = SLOT // 128
    GT = 256
    NG = T // GT
    TPG = GT // 128
    nc = bass.Bass("TRN2", target_bir_lowering=False)
    dt_in = lambda name, shape, dt=F32: nc.dram_tensor(name, list(shape), dt, kind="ExternalInput").ap()
    dt_sc = lambda name, shape, dt=F32: nc.dram_tensor(name, list(shape), dt, kind="Internal").ap()
    x_in = dt_in("x", [T, D])
    c8 = dt_in("c8", [8, D])
    ffn_w13 = dt_in("ffn_w13", [4, 2, D, 2 * DFF])
    ffn_w2 = dt_in("ffn_w2", [4, 2, DFF, D])
    ada_w = dt_in("ada_w", [4, D, 9 * D])
    ada_b = dt_in("ada_b", [4, 9 * D])
    norm_w = dt_in("norm_w", [4, 3, D])
    ml_w_in = dt_in("mlstm_w_in", [2, D, 3104])
    ml_bg = dt_in("mlstm_b_gate", [2, 32])
    ml_nw = dt_in("mlstm_norm_w", [2, D])
    ml_w_out = dt_in("mlstm_w_out", [2, D, D])
    at_w_in = [dt_in("swa_w_in", [1, D, 1536]), dt_in("axial_w_in", [1, D, 1536])]
    at_qn = [dt_in("swa_q_norm", [1, 64]), dt_in("axial_q_norm", [1, 64])]
    at_kn = [dt_in("swa_k_norm", [1, 64]), dt_in("axial_k_norm", [1, 64])]
    swa_sink = dt_in("swa_sink", [1, 16])
    at_w_out = [dt_in("swa_w_out", [1, D, D]), dt_in("axial_w_out", [1, D, D])]
    ropec = [dt_in("rope_swa_c", [T, 64]), dt_in("rope_ax_c", [T, 64])]
    ropes = [dt_in("rope_swa_s", [T, 64]), dt_in("rope_ax_s", [T, 64])]
    keepf_d = dt_in("keepf", [128, NT])
    keepb_d = dt_in("keepb", [128, NT])
    swab_d = dt_in("swab", [128, NT * 2])
    amask_d = dt_in("amask", [128, 64])
    y_out = nc.dram_tensor("y", [T, D], F32, kind="ExternalOutput").ap()
    xs = dt_sc("xs", [T, D])
    MR = dt_sc("MR", [NL, 9, 8, 128, D])
    QTK = dt_sc("QTK", [10, 128, T], BF16)
    Vd = dt_sc("Vd", [4, 128, NT, 64], BF16)
    OTd = dt_sc("OTd", [16, 64, T], BF16)
    MQK = dt_sc("MQK", [8, 128, T], BF16)
    Ktm = dt_sc("Ktm", [T, 512])
    Vm = dt_sc("Vm", [T, D], BF16)
    SOd = dt_sc("SOd", [T, D])
    GTd = dt_sc("GTd", [T, 32])
    Hf = dt_sc("Hf", [T, D])
    HGd = dt_sc("HGd", [T, D], BF16)

    top = contextlib.ExitStack()
    with top:
        k = K(nc, top)
        pe, act, dve, pool = k.pe, k.act, k.dve, k.pool
        uid = [0]

        def TT(es, name, shape, dt=F32):
            uid[0] += 1
            return es.enter_context(nc.sbuf_tensor("%s_u%d" % (name, uid[0]), list(shape), dt))

        def PP(es, name, shape, dt=F32):
            uid[0] += 1
            return es.enter_context(nc.psum_tensor("%s_u%d" % (name, uid[0]), list(shape), dt))

        identb = TT(top, "identb", [128, 128], BF16)
        identf = TT(top, "identf", [128, 128])
        TRIi = TT(top, "TRIi", [128, 128])
        TRIr = TT(top, "TRIr", [128, 128])
        ONESf = TT(top, "ONESf", [128, 128])
        Sel = TT(top, "Sel", [8, 8, 128])
        E65 = TT(top, "E65", [65, 64])
        nhalf = TT(top, "nhalf", [128, 32])
        cb = k.buf("consts")

        def mkmask(t, pattern, cmul, cmp, base=0):
            k.op(pool, lambda: nc.gpsimd.memset(t, 1.0), w=[cb])
            k.op(pool, lambda: nc.gpsimd.affine_select(out=t, in_=t, pattern=pattern, compare_op=cmp, fill=0.0,
                                                       base=base, channel_multiplier=cmul), w=[cb])
        mkmask(identf[:], [[-1, 128]], 1, ALU.is_equal)
        mkmask(TRIi[:], [[1, 128]], -1, ALU.is_ge)
        mkmask(TRIr[:], [[-1, 128]], 1, ALU.is_ge)
        mkmask(Sel[:], [[-1, 8], [0, 128]], 1, ALU.is_equal)
        mkmask(E65[:], [[0, 64]], 1, ALU.is_equal, base=-64)
        CAPi = TT(top, "CAPi", [128, 128])
        CAPr = TT(top, "CAPr", [128, 128])
        k.op(pool, lambda: nc.gpsimd.memset(ONESf[:], 1.0), w=[cb])
        k.op(pool, lambda: nc.gpsimd.memset(nhalf[:], -0.5), w=[cb])
        k.op(dve, lambda: nc.vector.tensor_copy(out=identb[:], in_=identf[:]), r=[cb], w=[cb])
        k.op(dve, lambda: nc.vector.tensor_scalar(out=CAPi[:], in0=TRIi[:], scalar1=10016.0, scalar2=-10000.0, op0=ALU.mult, op1=ALU.add), w=[cb])
        k.op(dve, lambda: nc.vector.tensor_scalar(out=CAPr[:], in0=TRIr[:], scalar1=10016.0, scalar2=-10000.0, op0=ALU.mult, op1=ALU.add), w=[cb])
        k.barrier()

        state = {"src": x_in, "nph": 0}

        def phase_done():
            k.barrier()
            state["nph"] += 1
            return stop is not None and state["nph"] >= stop

        def load_weight(es, dst, dstbuf, pieces):
            nmax = max(p[1].shape[-1] for p in pieces)
            stg = [TT(es, "wstg%d" % i, [128, nmax]) for i in range(3)]
            sb = [k.buf("wstg%d" % i) for i in range(3)]
            cv = [dve, pool, act]
            for i, (d_ap, s_ap) in enumerate(pieces):
                P, n = s_ap.shape[0], s_ap.shape[-1]
                j = i % 3
                k.dma([(stg[j][0:P, 0:n], s_ap)], w=[sb[j]])
                e = cv[i % 3]
                if e is act:
                    k.op(act, lambda: nc.scalar.copy(out=d_ap, in_=stg[j][0:P, 0:n]), r=[sb[j]], w=[dstbuf])
                else:
                    k.op(e, lambda: e.h.tensor_copy(out=d_ap, in_=stg[j][0:P, 0:n]), r=[sb[j]], w=[dstbuf])

        class NormCtx:
            def __init__(s, es, nb=2):
                s.hb = [TT(es, "n_hb%d" % i, [128, D], BF16) for i in range(nb)]
                s.hbb = [k.buf("n_hb%d" % i) for i in range(nb)]
                s.sm = [TT(es, "n_sm%d" % i, [128, 4]) for i in range(nb)]
                s.smb = [k.buf("n_sm%d" % i) for i in range(nb)]
                s.pT = [PP(es, "n_pT%d" % i, [128, 8, 128], BF16) for i in range(2)]
                s.pTb = [k.buf("n_pT%d" % i) for i in range(2)]
                s.n = 0

            def part1(s, xa, xab, A, Ab, B, Bb):
                i = s.n % len(s.hb)
                s.cur = i
                hb, hbb, sm, smb = s.hb[i], s.hbb[i], s.sm[i], s.smb[i]
                k.op(act, lambda: nc.scalar.activation(out=hb[:], in_=xa, func=AF.Square, accum_out=sm[:, 0:1]),
                     r=[xab], w=[hbb, smb])
                k.op(dve, lambda: nc.vector.tensor_scalar(out=sm[:, 1:2], in0=sm[:, 0:1], scalar1=1.0 / D, scalar2=EPS,
                                                          op0=ALU.mult, op1=ALU.add), r=[smb], w=[smb])
                k.op(pool, lambda: nc.gpsimd.tensor_tensor(out=sm[:, 2:3], in0=sm[:, 1:2], in1=nhalf[:, 0:1], op=ALU.pow),
                     r=[smb], w=[smb])
                k.op(dve, lambda: nc.vector.scalar_tensor_tensor(out=xa, in0=xa, scalar=sm[:, 2:3], in1=A,
                                                                 op0=ALU.mult, op1=ALU.mult), r=[smb, Ab, xab], w=[xab])
                k.op(pool, lambda: nc.gpsimd.tensor_tensor(out=hb[:], in0=xa, in1=B, op=ALU.add), r=[xab, Bb], w=[hbb])

            def part2(s, hT, hTb):
                i = s.cur
                j = s.n % 2
                s.n += 1
                for kc in range(KC):
                    k.op(pe, lambda: nc.tensor.transpose(s.pT[j][:, kc, :], s.hb[i][:, kc * 128:(kc + 1) * 128], identb[:]),
                         r=[s.hbb[i]], w=[s.pTb[j]])
                k.op(act, lambda: nc.scalar.copy(out=hT, in_=s.pT[j][:]), r=[s.pTb[j]], w=[hTb])

        class ModTiles:
            def __init__(s, es, l, ms, pfx):
                s.l, s.ms = l, ms
                s.t = {m: TT(es, "%s_mod%d" % (pfx, m), [128, D]) for m in ms}
                s.b = {m: k.buf("%s_mod%d" % (pfx, m)) for m in ms}
                s.slot = -1

            def need(s, slot):
                if slot != s.slot:
                    s.slot = slot
                    for m in s.ms:
                        k.dma([(s.t[m][:], MR[s.l, m, slot])], w=[s.b[m]])

        class Epilogue:
            def __init__(s, es, nb=2):
                s.xb = [TT(es, "e_xb%d" % i, [128, D]) for i in range(nb)]
                s.xbb = [k.buf("e_xb%d" % i) for i in range(nb)]
                s.n = 0

            def load(s, src, tok0):
                i = s.n % len(s.xb)
                k.dma([(s.xb[i][:], src[tok0:tok0 + 128, :])], w=[s.xbb[i]])
                return i

            def apply(s, i, py, pyb, G, Gb, dst, tok0):
                for h in range(2):
                    k.op(dve, lambda: nc.vector.tensor_tensor(out=py[h][:], in0=py[h][:], in1=G[:, h * 512:(h + 1) * 512],
                                                              op=ALU.mult), r=[Gb], w=[pyb[h]])
                    k.op(dve, lambda: nc.vector.tensor_tensor(out=s.xb[i][:, h * 512:(h + 1) * 512],
                                                              in0=s.xb[i][:, h * 512:(h + 1) * 512], in1=py[h][:], op=ALU.add),
                         r=[pyb[h]], w=[s.xbb[i]])
                k.dma([(dst[tok0:tok0 + 128, :], s.xb[i][:])], r=[s.xbb[i]])
                s.n += 1

        def phase_mod():
            es = contextlib.ExitStack()
            with es:
                c8t = TT(es, "c8t", [8, D]); c8b = k.buf("c8t")
                c8s = TT(es, "c8s", [8, D], BF16)
                csT = TT(es, "csT", [128, KC, 8], BF16); csTb = k.buf("csT")
                csrep = TT(es, "csrep", [128, 8, KC, 128], BF16); csrb = k.buf("csrep")
                pcs = PP(es, "pcs", [128, KC, 8], BF16); pcsb = k.buf("pcs")
                k.dma([(c8t[:], c8[:, :])], w=[c8b])
                k.op(act, lambda: nc.scalar.activation(out=c8s[:], in_=c8t[:], func=AF.Silu), r=[c8b], w=[c8b])
                for kc in range(KC):
                    k.op(pe, lambda: nc.tensor.transpose(pcs[:, kc, :], c8s[0:8, kc * 128:(kc + 1) * 128], identb[0:8, 0:8]),
                         r=[c8b], w=[pcsb])
                k.op(dve, lambda: nc.vector.tensor_copy(out=csT[:], in_=pcs[:]), r=[pcsb], w=[csTb])
                for sl in range(8):
                    k.op(dve, lambda: nc.vector.tensor_copy(out=csrep[:, sl], in_=bc(csT[:, :, sl:sl + 1], [128, KC, 128])),
                         r=[csTb], w=[csrb])
                stg = [TT(es, "m_stg%d" % i, [128, KC, 512]) for i in range(2)]
                stgb = [k.buf("m_stg%d" % i) for i in range(2)]
                wst = [TT(es, "m_wst%d" % i, [128, KC, 512], BF16) for i in range(2)]
                wstb = [k.buf("m_wst%d" % i) for i in range(2)]
                adb = [TT(es, "m_adb%d" % i, [128, 512]) for i in range(2)]
                adbb = [k.buf("m_adb%d" % i) for i in range(2)]
                nwr = TT(es, "m_nwr", [128, 3, D]); nwrb = k.buf("m_nwr")
                mo = [TT(es, "m_mo%d" % i, [128, 512]) for i in range(3)]
                mob = [k.buf("m_mo%d" % i) for i in range(3)]
                pm = [PP(es, "m_pm%d" % i, [128, 512]) for i in range(3)]
                pmb = [k.buf("m_pm%d" % i) for i in range(3)]
                n = 0
                cgi = 0
                for l in range(NL):
                    k.dma([(nwr[:, j, :], norm_w[l, j:j + 1, :].partition_broadcast(128)) for j in range(3)], w=[nwrb])
                    for m in range(9):
                        for half in range(2):
                            c0 = m * D + half * 512
                            j = cgi % 2
                            cgi += 1
                            k.dma([(stg[j][:], ada_w[l, :, c0:c0 + 512].rearrange("(kc p) n -> p kc n", p=128))], w=[stgb[j]])
                            k.dma([(adb[j][:], ada_b[l:l + 1, c0:c0 + 512].partition_broadcast(128))], w=[adbb[j]])
                            k.op(pool, lambda: nc.gpsimd.tensor_copy(out=wst[j][:], in_=stg[j][:]), r=[stgb[j]], w=[wstb[j]])
                            for sl in range(8):
                                q = n % 3
                                n += 1
                                for kc in range(KC):
                                    k.op(pe, lambda: nc.tensor.matmul(pm[q][:], csrep[:, sl, kc, :], wst[j][:, kc, :],
                                                                      start=(kc == 0), stop=(kc == KC - 1)),
                                         r=[csrb, wstb[j]], w=[pmb[q]])
                                k.op(dve, lambda: nc.vector.tensor_tensor(out=mo[q][:], in0=pm[q][:], in1=adb[j][:], op=ALU.add),
                                     r=[pmb[q], adbb[j]], w=[mob[q]])
                                if m in (1, 4, 7):
                                    k.op(dve, lambda: nc.vector.scalar_tensor_tensor(
                                        out=mo[q][:], in0=mo[q][:], scalar=1.0, in1=nwr[:, m // 3, half * 512:(half + 1) * 512],
                                        op0=ALU.add, op1=ALU.mult), r=[nwrb], w=[mob[q]])
                                elif m in (2, 8):
                                    k.op(dve, lambda: nc.vector.tensor_scalar(out=mo[q][:], in0=mo[q][:], scalar1=0.5, scalar2=None,
                                                                              op0=ALU.mult), w=[mob[q]])
                                k.dma([(MR[l, m, sl, :, half * 512:(half + 1) * 512], mo[q][:])], r=[mob[q]])
            return phase_done()

        def phase_ffn(l, which, dst):
            src = state["src"]
            k.pe_eager = False
            mi = 0 if which == 0 else 6
            es = contextlib.ExitStack()
            with es:
                W13 = TT(es, "W13", [128, KC, 2 * DFF], BF16); W13b = k.buf("W13")
                W2 = TT(es, "W2", [128, NFC, D], BF16); W2b = k.buf("W2")
                es2 = contextlib.ExitStack()
                with es2:
                    pieces = []
                    for kc in range(KC):
                        for c in range(4):
                            pieces.append((W13[:, kc, c * 1408:(c + 1) * 1408],
                                           ffn_w13[l, which, kc * 128:(kc + 1) * 128, c * 1408:(c + 1) * 1408]))
                    for fc in range(NFC):
                        pieces.append((W2[:, fc, :], ffn_w2[l, which, fc * 128:(fc + 1) * 128, :]))
                    wb = k.buf("Wall")
                    load_weight(es2, None, wb, pieces)
                    k.barrier()
                xa = [TT(es, "f_xa%d" % i, [128, D]) for i in range(2)]
                xab = [k.buf("f_xa%d" % i) for i in range(2)]
                nctx = NormCtx(es)
                mods = ModTiles(es, l, [mi, mi + 1], "f")
                modg = ModTiles(es, l, [mi + 2], "fg")
                hT = [TT(es, "f_hT%d" % i, [128, KC, GT], BF16) for i in range(2)]
                hTb = [k.buf("f_hT%d" % i) for i in range(2)]
                sg = [TT(es, "f_sg%d" % i, [128, GT]) for i in range(2)]
                sgb = [k.buf("f_sg%d" % i) for i in range(2)]
                uT = TT(es, "f_uT", [128, NFC, GT], BF16); uTb = k.buf("f_uT")
                ep = Epilogue(es)
                pg = [PP(es, "f_pg%d" % i, [128, GT]) for i in range(2)]
                pgb = [k.buf("f_pg%d" % i) for i in range(2)]
                pu = [PP(es, "f_pu%d" % i, [128, GT]) for i in range(2)]
                pub = [k.buf("f_pu%d" % i) for i in range(2)]
                py = [PP(es, "f_py%d" % i, [128, 512]) for i in range(2)]
                pyb = [k.buf("f_py%d" % i) for i in range(2)]
                cnt = {"xa": 0}

                def norm1(g):
                    mods.need((g * GT) // SLOT)
                    pend = []
                    for tt in range(TPG):
                        i = cnt["xa"] % 2
                        cnt["xa"] += 1
                        tok0 = g * GT + tt * 128
                        k.dma([(xa[i][:], src[tok0:tok0 + 128, :])], w=[xab[i]])
                        nctx.part1(xa[i][:], xab[i], mods.t[mi + 1][:], mods.b[mi + 1], mods.t[mi][:], mods.b[mi])
                        nctx.part2(hT[g % 2][:, :, tt * 128:(tt + 1) * 128], hTb[g % 2])

                norm1(0)
                for g in range(NG):
                    h_ = hT[g % 2]
                    for fc in range(NFC):
                        q = fc % 2
                        for kc in range(KC):
                            k.op(pe, lambda: nc.tensor.matmul(pg[q][:], W13[:, kc, fc * 128:(fc + 1) * 128], h_[:, kc, :],
                                                              start=(kc == 0), stop=(kc == KC - 1)), r=[hTb[g % 2]], w=[pgb[q]])
                        for kc in range(KC):
                            k.op(pe, lambda: nc.tensor.matmul(pu[q][:], W13[:, kc, DFF + fc * 128:DFF + (fc + 1) * 128], h_[:, kc, :],
                                                              start=(kc == 0), stop=(kc == KC - 1)), r=[hTb[g % 2]], w=[pub[q]])
                        k.op(act, lambda: nc.scalar.activation(out=sg[q][:], in_=pg[q][:], func=AF.Silu), r=[pgb[q]], w=[sgb[q]])
                        k.op(dve, lambda: nc.vector.tensor_tensor(out=uT[:, fc, :], in0=sg[q][:], in1=pu[q][:], op=ALU.mult),
                             r=[sgb[q], pub[q]], w=[uTb])
                        if fc == 10 and g + 1 < NG:
                            norm1(g + 1)
                    modg.need((g * GT) // SLOT)
                    for tt in range(TPG):
                        tok0 = g * GT + tt * 128
                        xi = ep.load(src, tok0)
                        for h in range(2):
                            for fc in range(NFC):
                                k.op(pe, lambda: nc.tensor.matmul(py[h][:], uT[:, fc, tt * 128:(tt + 1) * 128],
                                                                  W2[:, fc, h * 512:(h + 1) * 512],
                                                                  start=(fc == 0), stop=(fc == NFC - 1)), r=[uTb], w=[pyb[h]])
                        ep.apply(xi, py, pyb, modg.t[mi + 2], modg.b[mi + 2], dst, tok0)
            state["src"] = dst
            k.pe_eager = True
            return phase_done()

        def phase_att_in(l, kind):
            src = state["src"]
            es = contextlib.ExitStack()
            with es:
                Win = TT(es, "a_Win", [128, KC, 1536], BF16); Winb = k.buf("a_Win")
                es2 = contextlib.ExitStack()
                with es2:
                    pieces = [(Win[:, kc, :], at_w_in[kind][0, kc * 128:(kc + 1) * 128, :]) for kc in range(KC)]
                    load_weight(es2, None, Winb, pieces)
                    k.barrier()
                nwr = TT(es, "a_nwr", [128, 20, 64]); nwrb = k.buf("a_nwr")
                nws = TT(es, "a_nws", [128, 2, 64])
                k.dma([(nws[:, 0, :], at_qn[kind][0:1, :].partition_broadcast(128)),
                       (nws[:, 1, :], at_kn[kind][0:1, :].partition_broadcast(128))], w=[nwrb])
                k.op(dve, lambda: nc.vector.tensor_scalar(out=nwr[:, 0:16, :], in0=bc(nws[:, 0:1, :], [128, 16, 64]), scalar1=0.125,
                                                          scalar2=None, op0=ALU.mult), r=[nwrb], w=[nwrb])
                k.op(dve, lambda: nc.vector.tensor_copy(out=nwr[:, 16:20, :], in_=bc(nws[:, 1:2, :], [128, 4, 64])), r=[nwrb], w=[nwrb])
                xa = [TT(es, "a_xa%d" % i, [128, D]) for i in range(2)]
                xab = [k.buf("a_xa%d" % i) for i in range(2)]
                nctx = NormCtx(es)
                mods = ModTiles(es, l, [3, 4], "a")
                hT = [TT(es, "a_hT%d" % i, [128, KC, 128], BF16) for i in range(2)]
                hTb = [k.buf("a_hT%d" % i) for i in range(2)]
                pq = [PP(es, "a_pq%d" % i, [128, 512]) for i in range(3)]
                pqb = [k.buf("a_pq%d" % i) for i in range(3)]
                ptq = PP(es, "a_ptq", [128, 8, 128], BF16); ptqb = k.buf("a_ptq")
                ptk = PP(es, "a_ptk", [128, 2, 128], BF16); ptkb = k.buf("a_ptk")
                NB_ = 2
                qk = [TT(es, "a_qk%d" % i, [128, 20, 64]) for i in range(NB_)]
                qkb = [k.buf("a_qk%d" % i) for i in range(NB_)]
                sq = [TT(es, "a_sq%d" % i, [128, 20, 64]) for i in range(NB_)]
                sqb = [k.buf("a_sq%d" % i) for i in range(NB_)]
                t2 = [TT(es, "a_t2%d" % i, [128, 20, 64]) for i in range(NB_)]
                t2b = [k.buf("a_t2%d" % i) for i in range(NB_)]
                qr = [TT(es, "a_qr%d" % i, [128, 1280], BF16) for i in range(NB_)]
                qrb = [k.buf("a_qr%d" % i) for i in range(NB_)]
                vb = [TT(es, "a_vb%d" % i, [128, 4, 64], BF16) for i in range(NB_)]
                vbb = [k.buf("a_vb%d" % i) for i in range(NB_)]
                sm = [TT(es, "a_sm%d" % i, [128, 3, 20]) for i in range(NB_)]
                smb = [k.buf("a_sm%d" % i) for i in range(NB_)]
                cs = [TT(es, "a_cs%d" % i, [128, 2, 64]) for i in range(NB_)]
                csb = [k.buf("a_cs%d" % i) for i in range(NB_)]
                qT = [TT(es, "a_qT%d" % i, [128, 10, 128], BF16) for i in range(NB_)]
                qTb = [k.buf("a_qT%d" % i) for i in range(NB_)]
                hbk = 32 if kind == 0 else 16
                nbk = 64 // (2 * hbk)
                for t in range(NT):
                    i = t % 2
                    tok0 = t * 128
                    mods.need(tok0 // SLOT)
                    k.dma([(xa[i][:], src[tok0:tok0 + 128, :])], w=[xab[i]])
                    k.dma([(cs[i][:, 0, :], ropec[kind][tok0:tok0 + 128, :]), (cs[i][:, 1, :], ropes[kind][tok0:tok0 + 128, :])], w=[csb[i]])
                    nctx.part1(xa[i][:], xab[i], mods.t[4][:], mods.b[4], mods.t[3][:], mods.b[3])
                    nctx.part2(hT[i][:], hTb[i])
                    for n in range(3):
                        for kc in range(KC):
                            k.op(pe, lambda: nc.tensor.matmul(pq[n][:], hT[i][:, kc, :], Win[:, kc, n * 512:(n + 1) * 512],
                                                              start=(kc == 0), stop=(kc == KC - 1)), r=[hTb[i], Winb], w=[pqb[n]])
                    qkf = qk[i][:].rearrange("p h d -> p (h d)")
                    k.op(act, lambda: nc.scalar.copy(out=qkf[:, 0:512], in_=pq[0][:]), r=[pqb[0]], w=[qkb[i]])
                    k.op(act, lambda: nc.scalar.copy(out=qkf[:, 512:1024], in_=pq[1][:]), r=[pqb[1]], w=[qkb[i]])
                    k.op(act, lambda: nc.scalar.copy(out=qkf[:, 1024:1280], in_=pq[2][:, 0:256]), r=[pqb[2]], w=[qkb[i]])
                    k.op(act, lambda: nc.scalar.copy(out=vb[i][:].rearrange("p h d -> p (h d)"), in_=pq[2][:, 256:512]),
                         r=[pqb[2]], w=[vbb[i]])
                    k.dma([(Vd[:, :, t, :].rearrange("g p d -> p g d"), vb[i][:])], r=[vbb[i]])
                    k.op(act, lambda: nc.scalar.activation(out=sq[i][:], in_=qk[i][:], func=AF.Square), r=[qkb[i]], w=[sqb[i]])
                    k.op(dve, lambda: nc.vector.tensor_reduce(out=sm[i][:, 0, :], in_=sq[i][:], axis=AX.X, op=ALU.add), r=[sqb[i]], w=[smb[i]])
                    k.op(dve, lambda: nc.vector.tensor_scalar(out=sm[i][:, 1, :], in0=sm[i][:, 0, :], scalar1=1.0 / 64, scalar2=EPS,
                                                              op0=ALU.mult, op1=ALU.add), r=[smb[i]], w=[smb[i]])
                    k.op(pool, lambda: nc.gpsimd.tensor_tensor(out=sm[i][:, 2, :], in0=sm[i][:, 1, :], in1=nhalf[:, 0:20], op=ALU.pow),
                         r=[smb[i]], w=[smb[i]])
                    k.op(dve, lambda: nc.vector.tensor_tensor(out=qk[i][:], in0=qk[i][:], in1=bc(sm[i][:, 2, :].unsqueeze(2), [128, 20, 64]),
                                                              op=ALU.mult), r=[smb[i]], w=[qkb[i]])
                    k.op(pool, lambda: nc.gpsimd.tensor_tensor(out=qk[i][:], in0=qk[i][:], in1=nwr[:], op=ALU.mult), r=[nwrb], w=[qkb[i]])
                    k.op(dve, lambda: nc.vector.tensor_tensor(out=sq[i][:], in0=qk[i][:], in1=bc(cs[i][:, 0:1, :], [128, 20, 64]), op=ALU.mult),
                         r=[qkb[i], csb[i]], w=[sqb[i]])
                    q5 = qk[i][:].rearrange("p h (b two e) -> p h b two e", two=2, e=hbk)
                    t5 = t2[i][:].rearrange("p h (b two e) -> p h b two e", two=2, e=hbk)
                    s5 = cs[i][:, 1:2, :].rearrange("p o (b two e) -> p o b two e", two=2, e=hbk)
                    for half in range(2):
                        k.op(pool, lambda: nc.gpsimd.tensor_tensor(out=t5[:, :, :, half, :], in0=q5[:, :, :, 1 - half, :],
                                                                   in1=bc(s5[:, :, :, half, :], [128, 20, nbk, hbk]), op=ALU.mult),
                             r=[qkb[i], csb[i]], w=[t2b[i]])
                    k.op(dve, lambda: nc.vector.tensor_tensor(out=qr[i][:], in0=sq[i][:].rearrange("p h d -> p (h d)"),
                                                              in1=t2[i][:].rearrange("p h d -> p (h d)"), op=ALU.add),
                         r=[sqb[i], t2b[i]], w=[qrb[i]])
                    for j in range(8):
                        k.op(pe, lambda: nc.tensor.transpose(ptq[:, j, :], qr[i][:, j * 128:(j + 1) * 128], identb[:]), r=[qrb[i]], w=[ptqb])
                    for j in range(2):
                        k.op(pe, lambda: nc.tensor.transpose(ptk[:, j, :], qr[i][:, 1024 + j * 128:1024 + (j + 1) * 128], identb[:]),
                             r=[qrb[i]], w=[ptkb])
                    k.op(act, lambda: nc.scalar.copy(out=qT[i][:, 0:8, :], in_=ptq[:]), r=[ptqb], w=[qTb[i]])
                    k.op(act, lambda: nc.scalar.copy(out=qT[i][:, 8:10, :], in_=ptk[:]), r=[ptkb], w=[qTb[i]])
                    k.dma([(QTK[:, :, tok0:tok0 + 128].rearrange("j p t -> p j t"), qT[i][:])], r=[qTb[i]])
            return phase_done()

        def phase_att(l, kind):
            es = contextlib.ExitStack()
            with es:
                GS = 2 if kind == 1 else 1
                ps = [PP(es, "t_ps%d" % i, [128, 1024]) for i in range(2)]
                psb = [k.buf("t_ps%d" % i) for i in range(2)]
                po = [PP(es, "t_po%d" % i, [65, 512]) for i in range(2)]
                pob = [k.buf("t_po%d" % i) for i in range(2)]
                pbt = [PP(es, "t_pb%d" % i, [64, 512]) for i in range(2)]
                pbb = [k.buf("t_pb%d" % i) for i in range(2)]
                KT = [TT(es, "t_KT%d" % i, [128, T], BF16) for i in range(2)]
                KTb = [k.buf("t_KT%d" % i) for i in range(2)]
                V = [TT(es, "t_V%d" % i, [128, NT, 65], BF16) for i in range(2)]
                Vb = [k.buf("t_V%d" % i) for i in range(2)]
                am = TT(es, "t_am", [128, 64]); amb = k.buf("t_am")
                sw = TT(es, "t_sw", [128, NT * 2])
                k.dma([(am[:], amask_d[:, :]), (sw[:], swab_d[:, :])], w=[amb])
                es_s = TT(es, "t_ess", [65, 16])
                esx = TT(es, "t_esx", [65, 16, 128]); esb = k.buf("t_esx")
                if kind == 0:
                    k.dma([(es_s[64:65, :], swa_sink[0:1, :])], w=[esb])
                    k.op(act, lambda: nc.scalar.activation(out=es_s[64:65, :], in_=es_s[64:65, :], func=AF.Exp), r=[esb], w=[esb])
                    k.op(dve, lambda: nc.vector.tensor_copy(out=esx[64:65, :, :], in_=bc(es_s[64:65, :].unsqueeze(2), [1, 16, 128])),
                         r=[esb], w=[esb])
                for i in range(2):
                    k.op(pool, lambda: nc.gpsimd.memset(V[i][:, :, 64:65], 1.0), w=[Vb[i]])
                NQ = 4
                qt = [TT(es, "t_qt%d" % i, [128, 4, 128], BF16) for i in range(NQ)]
                qtb = [k.buf("t_qt%d" % i) for i in range(NQ)]
                NP = 3
                pt = [TT(es, "t_pt%d" % i, [128, 2, 4, 128], BF16) for i in range(NP)]
                ptb = [k.buf("t_pt%d" % i) for i in range(NP)]
                R = [TT(es, "t_R%d" % i, [65, 512]) for i in range(2)]
                Rb = [k.buf("t_R%d" % i) for i in range(2)]
                rec = [TT(es, "t_rec%d" % i, [64, 512]) for i in range(2)]
                recb = [k.buf("t_rec%d" % i) for i in range(2)]
                ot = [TT(es, "t_ot%d" % i, [64, 4, 128], BF16) for i in range(2)]
                otb = [k.buf("t_ot%d" % i) for i in range(2)]

                def load_kv(g):
                    i = g % 2
                    CK = min(T, 2048)
                    ksrc = QTK[8 + g // 2, (g % 2) * 64:(g % 2) * 64 + 64, :]
                    if kind == 0:
                        k.dma([(KT[i][0:64, c0:c0 + CK], ksrc[:, c0:c0 + CK]) for c0 in range(0, T, CK)], w=[KTb[i]])
                    else:
                        ks4 = ksrc.rearrange("d (n two t) -> d n two t", two=2, t=128)
                        NBH = NT // 2
                        CB = min(NBH, 16)
                        prs = []
                        for par in range(2):
                            kd3 = KT[i][par * 64:(par + 1) * 64, 0:T // 2].rearrange("d (n t) -> d n t", t=128)
                            for b0 in range(0, NBH, CB):
                                prs.append((kd3[:, b0:b0 + CB, :], ks4[:, b0:b0 + CB, par, :]))
                        k.dma(prs, w=[KTb[i]])
                    VB = min(NT, 8)
                    k.dma([(V[i][:, b0:b0 + VB, 0:64], Vd[g, :, b0:b0 + VB, :]) for b0 in range(0, NT, VB)], w=[Vb[i]])

                items = []
                for g in range(4):
                    for i in range(NT):
                        if kind == 0:
                            js = [j for j in (i - 1, i, i + 1) if 0 <= j < NT]
                            grs = [[j] for j in js]
                        else:
                            grs = [list(range(j0, j0 + GS)) for j0 in range(0, NT, GS)]
                        for n, gr in enumerate(grs):
                            items.append((g, i, gr, n == 0, n == len(grs) - 1))
                qslot = {}
                cn = {"q": 0}

                def stage_S(n):
                    g, i, gr, first, last = items[n]
                    gi = g % 2
                    if first:
                        if i == 0 and g == 0:
                            load_kv(0)
                        if i == 1 and g + 1 < 4:
                            load_kv(g + 1)
                        qi = cn["q"] % NQ
                        cn["q"] += 1
                        qslot[(g, i)] = qi
                        qsrc = QTK[2 * g:2 * g + 2, :, i * 128:(i + 1) * 128].rearrange("j (hh d) t -> d (j hh) t", hh=2)
                        if kind == 0:
                            k.dma([(qt[qi][0:64], qsrc)], w=[qtb[qi]])
                        else:
                            k.dma([(qt[qi][0:64], qsrc), (qt[qi][64:128], qsrc)], w=[qtb[qi]])
                    qi = qslot[(g, i)]
                    si = n % 2
                    for s_, j in enumerate(gr):
                        if kind == 0:
                            lhs = KT[gi][0:64, j * 128:(j + 1) * 128]
                            rhs = qt[qi][0:64].rearrange("d h t -> d (h t)")
                        else:
                            par = j % 2
                            lhs = KT[gi][par * 64:(par + 1) * 64, (j // 2) * 128:(j // 2 + 1) * 128]
                            rhs = qt[qi][par * 64:(par + 1) * 64].rearrange("d h t -> d (h t)")
                        k.op(pe, lambda: nc.tensor.matmul(ps[si][:, s_ * 512:(s_ + 1) * 512], lhs, rhs, start=True, stop=True),
                             r=[KTb[gi], qtb[qi]], w=[psb[si]], sig=(s_ == len(gr) - 1))

                def stage_E(n):
                    g, i, gr, first, last = items[n]
                    si = n % 2
                    pi = n % NP
                    j = gr[0]
                    if kind == 1:
                        col = (i // BPG) * 8 + (j // BPG)
                        bias = am[:, col:col + 1]
                    elif j == i:
                        bias = 0.0
                    elif j < i:
                        bias = sw[:, 2 * i:2 * i + 1]
                    else:
                        bias = sw[:, 2 * i + 1:2 * i + 2]
                    ng = len(gr)
                    ptf = pt[pi][:, 0:ng].rearrange("k s h t -> k (s h t)")
                    k.op(act, lambda: nc.scalar.activation(out=ptf, in_=ps[si][:, 0:ng * 512], func=AF.Exp, bias=bias), r=[psb[si], amb], w=[ptb[pi]])
                    if kind == 0 and j != i:
                        tri = TRIr if j < i else TRIi
                        k.op(dve, lambda: nc.vector.tensor_tensor(out=pt[pi][:, 0], in0=pt[pi][:, 0], in1=bc(tri[:].unsqueeze(1), [128, 4, 128]),
                                                                  op=ALU.mult), r=[cb], w=[ptb[pi]])

                def stage_PV(n):
                    g, i, gr, first, last = items[n]
                    gi = g % 2
                    pi = n % NP
                    oi = (g * NT + i) % 2
                    for s_, j in enumerate(gr):
                        k.op(pe, lambda: nc.tensor.matmul(po[oi][:], V[gi][:, j, :], pt[pi][:, s_].rearrange("k h t -> k (h t)"),
                                                          start=(first and s_ == 0), stop=(last and s_ == len(gr) - 1)),
                             r=[Vb[gi], ptb[pi]], w=[pob[oi]])

                def epi1(n):
                    g, i, gr, first, last = items[n]
                    oi = (g * NT + i) % 2
                    k.op(dve, lambda: nc.vector.tensor_copy(out=R[oi][:], in_=po[oi][:]), r=[pob[oi]], w=[Rb[oi]])
                    if kind == 0:
                        k.op(dve, lambda: nc.vector.tensor_tensor(out=R[oi][64:65, :], in0=R[oi][64:65, :],
                                                                  in1=esx[64:65, 4 * g:4 * g + 4, :].rearrange("p h t -> p (h t)"), op=ALU.add),
                             r=[esb], w=[Rb[oi]])

                def epi2(n):
                    g, i, gr, first, last = items[n]
                    oi = (g * NT + i) % 2
                    k.op(pe, lambda: nc.tensor.matmul(pbt[oi][:], E65[:], R[oi][:], start=True, stop=True), r=[Rb[oi], cb], w=[pbb[oi]])
                    k.op(dve, lambda: nc.vector.reciprocal(out=rec[oi][:], in_=pbt[oi][:]), r=[pbb[oi]], w=[recb[oi]])
                    k.op(dve, lambda: nc.vector.tensor_tensor(out=ot[oi][:].rearrange("d h t -> d (h t)"), in0=R[oi][0:64, :], in1=rec[oi][:],
                                                              op=ALU.mult), r=[Rb[oi], recb[oi]], w=[otb[oi]])
                    k.dma([(OTd[4 * g:4 * g + 4, :, i * 128:(i + 1) * 128].rearrange("h d t -> d h t"), ot[oi][:])], r=[otb[oi]])

                NI = len(items)
                stage_S(0)
                pend = None
                for n in range(NI):
                    if n + 1 < NI:
                        stage_S(n + 1)
                    if pend is not None:
                        epi2(pend)
                        pend = None
                    stage_E(n)
                    stage_PV(n)
                    if items[n][4]:
                        epi1(n)
                        pend = n
                if pend is not None:
                    epi2(pend)
            return phase_done()

        def phase_att_out(l, kind, dst):
            src = state["src"]
            es = contextlib.ExitStack()
            with es:
                Wo = TT(es, "o_Wo", [64, 16, D], BF16); Wob = k.buf("o_Wo")
                es2 = contextlib.ExitStack()
                with es2:
                    pieces = [(Wo[:, h, :], at_w_out[kind][0, h * 64:(h + 1) * 64, :]) for h in range(16)]
                    load_weight(es2, None, Wob, pieces)
                    k.barrier()
                mods = ModTiles(es, l, [5], "o")
                ep = Epilogue(es)
                oT = [TT(es, "o_oT%d" % i, [64, 16, 128], BF16) for i in range(3)]
                oTb = [k.buf("o_oT%d" % i) for i in range(3)]
                py = [[PP(es, "o_py%d_%d" % (i, h), [128, 512]) for h in range(2)] for i in range(2)]
                pyb = [[k.buf("o_py%d_%d" % (i, h)) for h in range(2)] for i in range(2)]
                for t in range(NT):
                    tok0 = t * 128
                    i3 = t % 3
                    i2 = t % 2
                    mods.need(tok0 // SLOT)
                    k.dma([(oT[i3][:], OTd[:, :, tok0:tok0 + 128].rearrange("h d t -> d h t"))], w=[oTb[i3]])
                    xi = ep.load(src, tok0)
                    for h in range(2):
                        for hd in range(16):
                            k.op(pe, lambda: nc.tensor.matmul(py[i2][h][:], oT[i3][:, hd, :], Wo[:, hd, h * 512:(h + 1) * 512],
                                                              start=(hd == 0), stop=(hd == 15)), r=[oTb[i3], Wob], w=[pyb[i2][h]])
                    ep.apply(xi, py[i2], pyb[i2], mods.t[5], mods.b[5], dst, tok0)
            state["src"] = dst
            return phase_done()

        def phase_ml_in(l, j):
            src = state["src"]
            es = contextlib.ExitStack()
            with es:
                Win = TT(es, "m_Win", [128, KC, 3104], BF16); Winb = k.buf("m_Win")
                es2 = contextlib.ExitStack()
                with es2:
                    pieces = []
                    for kc in range(KC):
                        pieces.append((Win[:, kc, 0:1552], ml_w_in[j, kc * 128:(kc + 1) * 128, 0:1552]))
                        pieces.append((Win[:, kc, 1552:3104], ml_w_in[j, kc * 128:(kc + 1) * 128, 1552:3104]))
                    load_weight(es2, None, Winb, pieces)
                    k.barrier()
                bg = TT(es, "m_bg", [128, 32]); bgb = k.buf("m_bg")
                k.dma([(bg[:], ml_bg[j:j + 1, :].partition_broadcast(128))], w=[bgb])
                xa = [TT(es, "mi_xa%d" % i, [128, D]) for i in range(2)]
                xab = [k.buf("mi_xa%d" % i) for i in range(2)]
                nctx = NormCtx(es)
                mods = ModTiles(es, l, [3, 4], "mi")
                hT = [TT(es, "mi_hT%d" % i, [128, KC, 128], BF16) for i in range(2)]
                hTb = [k.buf("mi_hT%d" % i) for i in range(2)]
                pq = [PP(es, "mi_pq%d" % i, [128, 512]) for i in range(4)]
                pqb = [k.buf("mi_pq%d" % i) for i in range(4)]
                ptr = PP(es, "mi_ptr", [128, 8, 128], BF16); ptrb = k.buf("mi_ptr")
                qkb_ = [TT(es, "mi_qkb%d" % i, [128, 1024], BF16) for i in range(2)]
                qkbb = [k.buf("mi_qkb%d" % i) for i in range(2)]
                kf = [TT(es, "mi_kf%d" % i, [128, 512]) for i in range(2)]
                kfb = [k.buf("mi_kf%d" % i) for i in range(2)]
                vt = [TT(es, "mi_vt%d" % i, [128, D], BF16) for i in range(2)]
                vtb = [k.buf("mi_vt%d" % i) for i in range(2)]
                so = [TT(es, "mi_so%d" % i, [128, D]) for i in range(2)]
                sob = [k.buf("mi_so%d" % i) for i in range(2)]
                gg = [TT(es, "mi_gg%d" % i, [128, 4, 32]) for i in range(2)]
                ggb = [k.buf("mi_gg%d" % i) for i in range(2)]
                qT = [TT(es, "mi_qT%d" % i, [128, 8, 128], BF16) for i in range(2)]
                qTb = [k.buf("mi_qT%d" % i) for i in range(2)]
                cn = 0
                for t in range(NT):
                    i = t % 2
                    tok0 = t * 128
                    mods.need(tok0 // SLOT)
                    k.dma([(xa[i][:], src[tok0:tok0 + 128, :])], w=[xab[i]])
                    nctx.part1(xa[i][:], xab[i], mods.t[4][:], mods.b[4], mods.t[3][:], mods.b[3])
                    nctx.part2(hT[i][:], hTb[i])
                    for n in range(7):
                        q = cn % 4
                        cn += 1
                        w_ = 512 if n < 6 else 32
                        for kc in range(KC):
                            k.op(pe, lambda: nc.tensor.matmul(pq[q][:, 0:w_], hT[i][:, kc, :], Win[:, kc, n * 512:n * 512 + w_],
                                                              start=(kc == 0), stop=(kc == KC - 1)), r=[hTb[i], Winb], w=[pqb[q]])
                        SK = os.environ.get("ML_SKIP", "")
                        if n == 0 and "q" in SK: pass
                        elif n == 1 and "k" in SK: pass
                        elif n in (2, 3) and "v" in SK: pass
                        elif n in (4, 5) and "o" in SK: pass
                        elif n == 0:
                            k.op(act, lambda: nc.scalar.activation(out=qkb_[i][:, 0:512], in_=pq[q][:], func=AF.Copy, scale=0.125),
                                 r=[pqb[q]], w=[qkbb[i]])
                        elif n == 1:
                            k.op(dve, lambda: nc.vector.tensor_copy(out=kf[i][:], in_=pq[q][:]), r=[pqb[q]], w=[kfb[i]])
                            k.op(act, lambda: nc.scalar.copy(out=qkb_[i][:, 512:1024], in_=kf[i][:]), r=[kfb[i]], w=[qkbb[i]])
                            k.dma([(Ktm[tok0:tok0 + 128, :], kf[i][:])], r=[kfb[i]])
                        elif n in (2, 3):
                            k.op(dve, lambda: nc.vector.tensor_copy(out=vt[i][:, (n - 2) * 512:(n - 1) * 512], in_=pq[q][:]), r=[pqb[q]], w=[vtb[i]])
                            if n == 3:
                                k.dma([(Vm[tok0:tok0 + 128, :], vt[i][:])], r=[vtb[i]])
                        elif n in (4, 5):
                            k.op(act, lambda: nc.scalar.activation(out=so[i][:, (n - 4) * 512:(n - 3) * 512], in_=pq[q][:], func=AF.Sigmoid),
                                 r=[pqb[q]], w=[sob[i]])
                            if n == 5:
                                k.dma([(SOd[tok0:tok0 + 128, :], so[i][:])], r=[sob[i]])
                        elif "g" not in SK:
                            G = gg[i]
                            k.op(dve, lambda: nc.vector.tensor_tensor(out=G[:, 0, :], in0=pq[q][:, 0:32], in1=bg[:], op=ALU.add),
                                 r=[pqb[q], bgb], w=[ggb[i]])
                            k.op(act, lambda: nc.scalar.activation(out=G[:, 1, :], in_=G[:, 0, :], func=AF.Tanh, scale=1.0 / 15.0), r=[ggb[i]], w=[ggb[i]])
                            k.op(dve, lambda: nc.vector.tensor_scalar(out=G[:, 0, :], in0=G[:, 1, :], scalar1=15.0, scalar2=None, op0=ALU.mult),
                                 r=[ggb[i]], w=[ggb[i]])
                            k.op(act, lambda: nc.scalar.activation(out=G[:, 1, :], in_=G[:, 0, :], func=AF.Exp, scale=-1.0), r=[ggb[i]], w=[ggb[i]])
                            k.op(act, lambda: nc.scalar.activation(out=G[:, 2, :], in_=G[:, 1, :], func=AF.Ln, bias=1.0), r=[ggb[i]], w=[ggb[i]])
                            g4 = G[:, 0, :].rearrange("p (a b h) -> p a b h", a=2, b=2)
                            l4 = G[:, 2, :].rearrange("p (a b h) -> p a b h", a=2, b=2)
                            k.op(dve, lambda: nc.vector.tensor_scalar(out=g4[:, :, 1, :], in0=l4[:, :, 1, :], scalar1=-1.0, scalar2=None, op0=ALU.mult),
                                 r=[ggb[i]], w=[ggb[i]])
                            k.dma([(GTd[tok0:tok0 + 128, :], G[:, 0, :])], r=[ggb[i]])
                    if "t" in SK:
                        continue
                    for jj in range(8):
                        k.op(pe, lambda: nc.tensor.transpose(ptr[:, jj, :], qkb_[i][:, jj * 128:(jj + 1) * 128], identb[:]), r=[qkbb[i]], w=[ptrb])
                    k.op(act, lambda: nc.scalar.copy(out=qT[i][:], in_=ptr[:]), r=[ptrb], w=[qTb[i]])
                    k.dma([(MQK[:, :, tok0:tok0 + 128].rearrange("j p t -> p j t"), qT[i][:])], r=[qTb[i]])
            return phase_done()

        def phase_ml_scan(l, j, direction):
            fwd = direction == 0
            TRI = TRIi if fwd else TRIr
            CAP = CAPi if fwd else CAPr
            es = contextlib.ExitStack()
            with es:
                keep = TT(es, "s_keep", [128, NT]); keepb_ = k.buf("s_keep")
                k.dma([(keep[:], (keepf_d if fwd else keepb_d)[:, :])], w=[keepb_])
                nwr = TT(es, "s_nwr", [128, D]); nwrb = k.buf("s_nwr")
                k.dma([(nwr[:], ml_nw[j:j + 1, :].partition_broadcast(128))], w=[nwrb])
                C = TT(es, "s_C", [64, 8, 129]); Cb_ = k.buf("s_C")
                Cbf = TT(es, "s_Cbf", [64, 8, 129], BF16); Cbfb = k.buf("s_Cbf")
                k.op(dve, lambda: nc.vector.memset(C[:], 0.0), w=[Cb_])
                k.op(pool, lambda: nc.gpsimd.memset(Cbf[:], 0.0), w=[Cbfb])
                NB_ = 2
                mk = lambda nm, shape, dt=F32: ([TT(es, "s_%s%d" % (nm, i), shape, dt) for i in range(NB_)],
                                                [k.buf("s_%s%d" % (nm, i)) for i in range(NB_)])
                QT, QTb = mk("QT", [64, 8, 128], BF16)
                KTt, KTb = mk("KT", [64, 8, 128], BF16)
                Kt, Ktb = mk("Kt", [128, 8, 64])
                Va, Vab = mk("Va", [128, 8, 129], BF16)
                Gt, Gtb = mk("Gt", [128, 32])
                for i in range(NB_):
                    k.op(pool, lambda: nc.gpsimd.memset(Va[i][:, :, 128:129], 1.0), w=[Vab[i]])
                gs, gsb = mk("gs", [128, 6, 8])
                bT, bTb = mk("bT", [8, 128])
                arg, argb = mk("arg", [128, 8, 128])
                eb, ebb = mk("eb", [64, 8, 128])
                QTs, QTsb = mk("QTs", [64, 8, 128], BF16)
                ST, STb = mk("ST", [128, 8, 128], BF16)
                Kw, Kwb = mk("Kw", [128, 8, 64], BF16)
                hd, hdb = mk("hd", [128, 8, 128])
                dn, dnb = mk("dn", [128, 2, 8])
                hfl, hflb = mk("hfl", [128, D])
                sot, sotb = mk("sot", [128, D])
                sq2, sq2b = mk("sq2", [128, 8, 128])
                sm2, sm2b = mk("sm2", [128, 3, 8])
                hg, hgb = mk("hg", [128, D], BF16)
                psm = PP(es, "s_psm", [128, 512]); psmb = k.buf("s_psm")
                pbc = [PP(es, "s_pbc%d" % i, [128, 4, 128]) for i in range(2)]
                pbcb = [k.buf("s_pbc%d" % i) for i in range(2)]
                pss = [PP(es, "s_pss%d" % i, [128, 4, 128]) for i in range(2)]
                pssb = [k.buf("s_pss%d" % i) for i in range(2)]
                hsplit = [(0, 3), (3, 6), (6, 8)]
                pov = [PP(es, "s_po%d" % i, [128, 3, 129]) for i in range(3)]
                pob = [k.buf("s_po%d" % i) for i in range(3)]
                order = list(range(NT)) if fwd else list(range(NT - 1, -1, -1))
                gb = 0 if fwd else 16
                for n_, c in enumerate(order):
                    i = n_ % 2
                    tok0 = c * 128
                    k.dma([(QT[i][:], MQK[0:4, :, tok0:tok0 + 128].rearrange("j (hh d) t -> d (j hh) t", hh=2))], w=[QTb[i]])
                    k.dma([(KTt[i][:], MQK[4:8, :, tok0:tok0 + 128].rearrange("j (hh d) t -> d (j hh) t", hh=2))], w=[KTb[i]])
                    k.dma([(Kt[i][:].rearrange("p h d -> p (h d)"), Ktm[tok0:tok0 + 128, :])], w=[Ktb[i]])
                    k.dma([(Va[i][:, :, 0:128], Vm[tok0:tok0 + 128, :].rearrange("p (h d) -> p h d", h=8))], w=[Vab[i]])
                    k.dma([(Gt[i][:], GTd[tok0:tok0 + 128, :])], w=[Gtb[i]])
                    if not fwd:
                        k.dma([(hfl[i][:], Hf[tok0:tok0 + 128, :])], w=[hflb[i]])
                        k.dma([(sot[i][:], SOd[tok0:tok0 + 128, :])], w=[sotb[i]])
                    ig = Gt[i][:, gb:gb + 8]
                    lf = Gt[i][:, gb + 8:gb + 16]
                    S_ = gs[i]
                    k.op(pe, lambda: nc.tensor.matmul(psm[:, 0:8], TRI[:], lf, start=True, stop=True), r=[Gtb[i], cb], w=[psmb])
                    k.op(pe, lambda: nc.tensor.matmul(psm[:, 8:16], ONESf[:], lf, start=True, stop=True), r=[Gtb[i], cb], w=[psmb])
                    k.op(pe, lambda: nc.tensor.matmul(psm[0:8, 128:256], lf, TRI[:], start=True, stop=True), r=[Gtb[i], cb], w=[psmb])
                    k.op(dve, lambda: nc.vector.tensor_copy(out=S_[:, 0, :], in_=psm[:, 0:8]), r=[psmb], w=[gsb[i]])
                    k.op(dve, lambda: nc.vector.tensor_copy(out=S_[:, 4, :], in_=psm[:, 8:16]), r=[psmb], w=[gsb[i]])
                    k.op(dve, lambda: nc.vector.tensor_copy(out=bT[i][:], in_=psm[0:8, 128:256]), r=[psmb], w=[bTb[i]])
                    k.op(dve, lambda: nc.vector.tensor_tensor(out=S_[:, 1, :], in0=ig, in1=S_[:, 0, :], op=ALU.subtract), r=[Gtb[i]], w=[gsb[i]])
                    k.op(dve, lambda: nc.vector.tensor_tensor(out=S_[:, 2, :], in0=S_[:, 1, :], in1=S_[:, 4, :], op=ALU.add), w=[gsb[i]])
                    k.op(act, lambda: nc.scalar.activation(out=S_[:, 2, :], in_=S_[:, 2, :], func=AF.Exp), w=[gsb[i]])
                    k.op(act, lambda: nc.scalar.activation(out=S_[:, 3, :], in_=S_[:, 4, :], func=AF.Exp), w=[gsb[i]])
                    for h in range(8):
                        k.op(pe, lambda: nc.tensor.matmul(pbc[h // 4][:, h % 4, :], Sel[0:8, h, :], bT[i][:], start=True, stop=True),
                             r=[bTb[i], cb], w=[pbcb[h // 4]])
                    for hh in range(2):
                        for h4 in range(4):
                            h = 4 * hh + h4
                            k.op(dve, lambda: nc.vector.scalar_tensor_tensor(out=arg[i][:, h, :], in0=pbc[hh][:, h4, :], scalar=S_[:, 1, h:h + 1],
                                                                             in1=CAP[:], op0=ALU.add, op1=ALU.min),
                                 r=[pbcb[hh], gsb[i], cb], w=[argb[i]])
                        k.op(dve, lambda: nc.vector.tensor_copy(out=eb[i][:, 4 * hh:4 * hh + 4, :], in_=pbc[hh][0:64]),
                             r=[pbcb[hh]], w=[ebb[i]])
                        k.op(act, lambda: nc.scalar.activation(out=eb[i][:, 4 * hh:4 * hh + 4, :], in_=eb[i][:, 4 * hh:4 * hh + 4, :], func=AF.Exp),
                             w=[ebb[i]])
                    k.op(act, lambda: nc.scalar.activation(out=arg[i][:], in_=arg[i][:], func=AF.Exp), w=[argb[i]])
                    k.op(dve, lambda: nc.vector.tensor_tensor(out=QTs[i][:], in0=QT[i][:], in1=eb[i][:], op=ALU.mult), r=[QTb[i], ebb[i]], w=[QTsb[i]])
                    for h in range(8):
                        k.op(pe, lambda: nc.tensor.matmul(pss[h // 4][:, h % 4, :], KTt[i][:, h, :], QT[i][:, h, :], start=True, stop=True),
                             r=[KTb[i], QTb[i]], w=[pssb[h // 4]])
                    for hh in range(2):
                        k.op(dve, lambda: nc.vector.tensor_tensor(out=ST[i][:, 4 * hh:4 * hh + 4, :], in0=pss[hh][:], in1=arg[i][:, 4 * hh:4 * hh + 4, :],
                                                                  op=ALU.mult), r=[pssb[hh], argb[i]], w=[STb[i]])
                    k.op(dve, lambda: nc.vector.tensor_tensor(out=Kw[i][:], in0=Kt[i][:], in1=bc(S_[:, 2, :].unsqueeze(2), [128, 8, 64]), op=ALU.mult),
                         r=[Ktb[i], gsb[i]], w=[Kwb[i]])
                    for h in range(8):
                        pi_ = 0 if h < 3 else (1 if h < 6 else 2)
                        hl = h - hsplit[pi_][0]
                        k.op(pe, lambda: nc.tensor.matmul(pov[pi_][:, hl, :], ST[i][:, h, :], Va[i][:, h, :], start=True, stop=False),
                             r=[STb[i], Vab[i]], w=[pob[pi_]])
                        k.op(pe, lambda: nc.tensor.matmul(pov[pi_][:, hl, :], QTs[i][:, h, :], Cbf[:, h, :], start=False, stop=True),
                             r=[QTsb[i], Cbfb], w=[pob[pi_]])
                    for pi_, (h0, h1) in enumerate(hsplit):
                        nh = h1 - h0
                        k.op(dve, lambda: nc.vector.tensor_copy(out=dn[i][:, 1, h0:h1], in_=pov[pi_][:, 0:nh, 128]), r=[pob[pi_]], w=[dnb[i]])
                    k.op(dve, lambda: nc.vector.tensor_tensor(out=dn[i][:, 0, :], in0=dn[i][:, 1, :], in1=dn[i][:, 1, :], op=ALU.mult), w=[dnb[i]])
                    k.op(dve, lambda: nc.vector.tensor_scalar(out=dn[i][:, 0, :], in0=dn[i][:, 0, :], scalar1=1.0, scalar2=None, op0=ALU.max), w=[dnb[i]])
                    k.op(pool, lambda: nc.gpsimd.tensor_tensor(out=dn[i][:, 1, :], in0=dn[i][:, 0, :], in1=nhalf[:, 0:8], op=ALU.pow), w=[dnb[i]])
                    for pi_, (h0, h1) in enumerate(hsplit):
                        nh = h1 - h0
                        k.op(dve, lambda: nc.vector.tensor_tensor(out=hd[i][:, h0:h1, :], in0=pov[pi_][:, 0:nh, 0:128],
                                                                  in1=bc(dn[i][:, 1, h0:h1].unsqueeze(2), [128, nh, 128]), op=ALU.mult),
                             r=[pob[pi_], dnb[i]], w=[hdb[i]])
                    hdf = hd[i][:].rearrange("p h d -> p (h d)")
                    if fwd:
                        k.dma([(Hf[tok0:tok0 + 128, :], hdf)], r=[hdb[i]])
                    else:
                        k.op(dve, lambda: nc.vector.tensor_tensor(out=hdf, in0=hdf, in1=hfl[i][:], op=ALU.add), r=[hflb[i]], w=[hdb[i]])
                        k.op(act, lambda: nc.scalar.activation(out=sq2[i][:], in_=hd[i][:], func=AF.Square), r=[hdb[i]], w=[sq2b[i]])
                        k.op(dve, lambda: nc.vector.tensor_reduce(out=sm2[i][:, 0, :], in_=sq2[i][:], axis=AX.X, op=ALU.add), r=[sq2b[i]], w=[sm2b[i]])
                        k.op(dve, lambda: nc.vector.tensor_scalar(out=sm2[i][:, 1, :], in0=sm2[i][:, 0, :], scalar1=1.0 / 128, scalar2=EPS,
                                                                  op0=ALU.mult, op1=ALU.add), w=[sm2b[i]])
                        k.op(pool, lambda: nc.gpsimd.tensor_tensor(out=sm2[i][:, 2, :], in0=sm2[i][:, 1, :], in1=nhalf[:, 0:8], op=ALU.pow), w=[sm2b[i]])
                        k.op(dve, lambda: nc.vector.tensor_tensor(out=hd[i][:], in0=hd[i][:], in1=bc(sm2[i][:, 2, :].unsqueeze(2), [128, 8, 128]),
                                                                  op=ALU.mult), r=[sm2b[i]], w=[hdb[i]])
                        k.op(dve, lambda: nc.vector.tensor_tensor(out=hdf, in0=hdf, in1=nwr[:], op=ALU.mult), r=[nwrb], w=[hdb[i]])
                        k.op(dve, lambda: nc.vector.tensor_tensor(out=hg[i][:], in0=hdf, in1=sot[i][:], op=ALU.mult), r=[hdb[i], sotb[i]], w=[hgb[i]])
                        k.dma([(HGd[tok0:tok0 + 128, :], hg[i][:])], r=[hgb[i]])
                    for h in range(8):
                        pi_ = 0 if h < 3 else (1 if h < 6 else 2)
                        hl = h - hsplit[pi_][0]
                        k.op(pe, lambda: nc.tensor.matmul(pov[pi_][0:64, hl, :], Kw[i][:, h, :], Va[i][:, h, :], start=True, stop=True),
                             r=[Kwb[i], Vab[i]], w=[pob[pi_]])
                    k.op(dve, lambda: nc.vector.tensor_tensor(out=C[:], in0=C[:], in1=bc(S_[0:64, 3, :].unsqueeze(2), [64, 8, 129]), op=ALU.mult),
                         r=[gsb[i]], w=[Cb_])
                    for pi_, (h0, h1) in enumerate(hsplit):
                        nh = h1 - h0
                        k.op(dve, lambda: nc.vector.tensor_tensor(out=C[:, h0:h1, :], in0=C[:, h0:h1, :], in1=pov[pi_][0:64, 0:nh, :], op=ALU.add),
                             r=[pob[pi_]], w=[Cb_])
                    k.op(dve, lambda: nc.vector.tensor_scalar(out=C[:], in0=C[:], scalar1=keep[0:64, c:c + 1], scalar2=None, op0=ALU.mult),
                         r=[keepb_], w=[Cb_])
                    k.op(act, lambda: nc.scalar.copy(out=Cbf[:], in_=C[:]), r=[Cb_], w=[Cbfb])
            return phase_done()

        def phase_ml_out(l, j, dst):
            src = state["src"]
            es = contextlib.ExitStack()
            with es:
                Wo = TT(es, "mo_Wo", [128, KC, D], BF16); Wob = k.buf("mo_Wo")
                es2 = contextlib.ExitStack()
                with es2:
                    pieces = [(Wo[:, kc, :], ml_w_out[j, kc * 128:(kc + 1) * 128, :]) for kc in range(KC)]
                    load_weight(es2, None, Wob, pieces)
                    k.barrier()
                mods = ModTiles(es, l, [5], "mo")
                ep = Epilogue(es)
                hgt = [TT(es, "mo_hg%d" % i, [128, D], BF16) for i in range(2)]
                hgtb = [k.buf("mo_hg%d" % i) for i in range(2)]
                hT = [TT(es, "mo_hT%d" % i, [128, KC, 128], BF16) for i in range(2)]
                hTb = [k.buf("mo_hT%d" % i) for i in range(2)]
                pT = [PP(es, "mo_pT%d" % i, [128, 8, 128], BF16) for i in range(2)]
                pTb = [k.buf("mo_pT%d" % i) for i in range(2)]
                py = [[PP(es, "mo_py%d_%d" % (i, h), [128, 512]) for h in range(2)] for i in range(2)]
                pyb = [[k.buf("mo_py%d_%d" % (i, h)) for h in range(2)] for i in range(2)]
                for t in range(NT):
                    tok0 = t * 128
                    i = t % 2
                    mods.need(tok0 // SLOT)
                    k.dma([(hgt[i][:], HGd[tok0:tok0 + 128, :])], w=[hgtb[i]])
                    xi = ep.load(src, tok0)
                    for kc in range(KC):
                        k.op(pe, lambda: nc.tensor.transpose(pT[i][:, kc, :], hgt[i][:, kc * 128:(kc + 1) * 128], identb[:]), r=[hgtb[i]], w=[pTb[i]])
                    k.op(act, lambda: nc.scalar.copy(out=hT[i][:], in_=pT[i][:]), r=[pTb[i]], w=[hTb[i]])
                    for h in range(2):
                        for kc in range(KC):
                            k.op(pe, lambda: nc.tensor.matmul(py[i][h][:], hT[i][:, kc, :], Wo[:, kc, h * 512:(h + 1) * 512],
                                                              start=(kc == 0), stop=(kc == KC - 1)), r=[hTb[i], Wob], w=[pyb[i][h]])
                    ep.apply(xi, py[i], pyb[i], mods.t[5], mods.b[5], dst, tok0)
            state["src"] = dst
            return phase_done()

        def program():
            if phase_mod():
                return
            for l in range(NL):
                last = (l == NL - 1)
                if phase_ffn(l, 0, xs):
                    return
                kind, j = l % 3, l // 3
                if kind == 0:
                    if phase_ml_in(l, j): return
                    if phase_ml_scan(l, j, 0): return
                    if phase_ml_scan(l, j, 1): return
                    if phase_ml_out(l, j, xs): return
                else:
                    if phase_att_in(l, kind - 1): return
                    if phase_att(l, kind - 1): return
                    if phase_att_out(l, kind - 1, xs): return
                if phase_ffn(l, 1, y_out if last else xs):
                    return
        program()
        if state["src"] is not y_out:
            es = contextlib.ExitStack()
            with es:
                tb = [TT(es, "cp%d" % i, [128, D]) for i in range(2)]
                tbb = [k.buf("cp%d" % i) for i in range(2)]
                for t in range(NT):
                    k.dma([(tb[t % 2][:], state["src"][t * 128:(t + 1) * 128, :])], w=[tbb[t % 2]])
                    k.dma([(y_out[t * 128:(t + 1) * 128, :], tb[t % 2][:])], r=[tbb[t % 2]])
            k.barrier()
    return nc


def rope_np(pos, dim):
    inv = (np.float32(10000.0) ** (-np.arange(0, dim, 2, dtype=np.float32) / np.float32(dim))).astype(np.float32)
    ang = pos.astype(np.float32)[:, None] * inv[None, :]
    ang = np.concatenate([ang, ang], axis=-1)
    return np.cos(ang).astype(np.float32), np.sin(ang).astype(np.float32)


def core_tables(T, S):
    NT = T // 128
    bps = S // 128
    pos = np.arange(T) % S
    c, s = rope_np(pos, 64)
    s_sw = np.concatenate([-s[:, :32], s[:, 32:]], axis=1)
    rc, rs = rope_np(pos // 64, 32)
    cc, cs_ = rope_np(pos % 64, 32)
    c_ax = np.concatenate([rc, cc], axis=1)
    s_ax = np.concatenate([-rs[:, :16], rs[:, 16:], -cs_[:, :16], cs_[:, 16:]], axis=1)
    blk = np.arange(NT)
    keepf = ((blk + 1) % bps != 0).astype(np.float32)
    keepb = (blk % bps != 0).astype(np.float32)
    swab = np.zeros((NT, 2), np.float32)
    swab[blk % bps == 0, 0] = NEG
    swab[(blk + 1) % bps == 0, 1] = NEG
    slot_seq = (np.arange(8) * (T // 8)) // S
    amask = np.where(slot_seq[:, None] == slot_seq[None, :], 0.0, NEG).astype(np.float32)
    rep = lambda a: np.ascontiguousarray(np.broadcast_to(a.reshape(1, -1), (128, a.size))).astype(np.float32)
    return {
        "rope_swa_c": c, "rope_swa_s": s_sw.astype(np.float32), "rope_ax_c": c_ax.astype(np.float32), "rope_ax_s": s_ax.astype(np.float32),
        "keepf": rep(keepf), "keepb": rep(keepb), "swab": rep(swab), "amask": rep(amask),
    }


WNAMES = ["ffn_w13", "ffn_w2", "ada_w", "ada_b", "norm_w", "mlstm_w_in", "mlstm_b_gate", "mlstm_norm_w", "mlstm_w_out",
          "swa_w_in", "swa_q_norm", "swa_k_norm", "swa_sink", "swa_w_out", "axial_w_in", "axial_q_norm", "axial_k_norm", "axial_w_out"]


def run_streams(streams, weights, T, NL=4, stop=None, ncores=8):
    nc = build(T, NL, stop)
    w = {n: np.ascontiguousarray(np.asarray(weights[n], dtype=np.float32)) for n in WNAMES}
    in_maps = []
    for c in range(ncores):
        x, c8, S = streams[c] if c < len(streams) else streams[-1]
        m = {"x": np.ascontiguousarray(x, dtype=np.float32), "c8": np.ascontiguousarray(c8, dtype=np.float32)}
        m.update(core_tables(T, S))
        m.update(w)
        in_maps.append(m)
    if os.environ.get("KTRACE"):
        res = run_bass_kernel_spmd(nc, in_maps, core_ids=list(range(ncores)), trace=True)
        print("EXEC_TIME_NS", res.exec_time_ns)
    else:
        res = run_bass_kernel_spmd(nc, in_maps, core_ids=list(range(ncores)))
    return [res.results[c]["y"] for c in range(len(streams))]


def kernel(x_prompt, x_sample, c_prompt, c_sample, **weights):
    x_prompt = np.asarray(x_prompt, dtype=np.float32)
    x_sample = np.asarray(x_sample, dtype=np.float32)
    c_prompt = np.asarray(c_prompt, dtype=np.float32)
    c_sample = np.asarray(c_sample, dtype=np.float32)
    T = 16384
    streams = []
    for b in range(2):
        streams.append((x_prompt[b], np.ascontiguousarray(np.broadcast_to(c_prompt[b:b + 1], (8, D))), 16384))
    for j in range(4):
        streams.append((x_sample[8 * j:8 * j + 8].reshape(T, D), c_sample[8 * j:8 * j + 8], 2048))
    ys = run_streams(streams, weights, T)
    y_prompt = np.stack([ys[0], ys[1]], axis=0).astype(np.float32)
    y_sample = np.concatenate([ys[2 + j].reshape(8, 2048, D) for j in range(4)], axis=0).astype(np.float32)
    return (y_prompt, y_sample)
```

```python
import bisect
import os
import contextlib
import numpy as np
import concourse.bass as bass
import concourse.mybir as mybir
from concourse.bass_utils import run_bass_kernel_spmd

F32 = mybir.dt.float32
BF16 = mybir.dt.bfloat16
AF = mybir.ActivationFunctionType
ALU = mybir.AluOpType
AX = mybir.AxisListType

D = 1024
DFF = 2816
NFC = 22
KC = 8
EPS = 1e-6
NEG = -30000.0
NDS = 56


class Eng:
    def __init__(s, name, h, sem):
        s.name, s.h, s.sem = name, h, sem
        s.cnt = 0
        s.idx = 0
        s.last = None
        s.sig_idx = []
        s.sig_cnt = []
        s.seen = {}


class DSem:
    def __init__(s, h, key):
        s.h, s.key, s.count = h, key, 0


class Buf:
    def __init__(s, name):
        s.name = name
        s.w = None
        s.r = {}
        s.dsem = None


class K:
    def __init__(s, nc, es):
        s.nc = nc
        mk = lambda n: es.enter_context(nc.semaphore(n))
        s.pe = Eng("pe", nc.tensor, mk("s_pe"))
        s.act = Eng("act", nc.scalar, mk("s_act"))
        s.dve = Eng("dve", nc.vector, mk("s_dve"))
        s.pool = Eng("pool", nc.gpsimd, mk("s_pool"))
        s.sp = Eng("sp", nc.sync, mk("s_sp"))
        s.engs = [s.pe, s.act, s.dve, s.pool, s.sp]
        s.dsems = [DSem(mk("d%d" % i), "d%d" % i) for i in range(NDS)]
        s.free = list(s.dsems)
        s.pbufs = []
        s.used = []
        s.pe_eager = True

    def buf(s, name):
        b = Buf(name)
        s.pbufs.append(b)
        return b

    def _wait(s, eng, ev):
        if ev[0] == "e":
            e2, idx = ev[1], ev[2]
            if e2 is eng and eng is s.pe:
                return
            i = bisect.bisect_left(e2.sig_idx, idx)
            if i < len(e2.sig_idx):
                c = e2.sig_cnt[i]
            else:
                assert e2.idx >= idx and e2.last is not None
                e2.last.then_inc(e2.sem, 1)
                e2.cnt += 1
                e2.sig_idx.append(e2.idx)
                e2.sig_cnt.append(e2.cnt)
                c = e2.cnt
            if eng.seen.get(e2.name, 0) >= c:
                return
            eng.h.wait_ge(e2.sem, c)
            eng.seen[e2.name] = c
        else:
            ds, val = ev[1], ev[2]
            if eng.seen.get(ds.key, 0) >= val:
                return
            eng.h.wait_ge(ds.h, val)
            eng.seen[ds.key] = val

    def _deps(s, eng, r, w):
        for b in r:
            if b.w is not None:
                s._wait(eng, b.w)
        for b in w:
            if b.w is not None:
                s._wait(eng, b.w)
            for ev in b.r.values():
                s._wait(eng, ev)

    def op(s, eng, fn, r=(), w=(), sig=None):
        s._deps(eng, r, w)
        ins = fn()
        eng.idx += 1
        eng.last = ins
        if sig is None:
            sig = (eng is not s.pe) or s.pe_eager
        if sig:
            ins.then_inc(eng.sem, 1)
            eng.cnt += 1
            eng.sig_idx.append(eng.idx)
            eng.sig_cnt.append(eng.cnt)
        ev = ("e", eng, eng.idx)
        for b in r:
            b.r[eng.name] = ev
        for b in w:
            b.w = ev
            b.r = {}
        return ins

    def dma(s, pairs, r=(), w=(), q=None):
        q = q or s.sp
        s._deps(q, r, w)
        owner = w[0] if len(w) else r[0]
        if owner.dsem is None:
            owner.dsem = s.free.pop()
            s.used.append(owner.dsem)
        ds = owner.dsem
        for (o, i) in pairs:
            q.h.dma_start(out=o, in_=i).then_inc(ds.h, 16)
            ds.count += 16
        ev = ("d", ds, ds.count)
        for b in r:
            b.r["dma_" + ds.key] = ev
        for b in w:
            b.w = ev
            b.r = {}

    def barrier(s):
        for e in s.engs:
            for e2 in s.engs:
                if e2 is not e and e2.idx > 0 and e2 is not s.sp:
                    s._wait(e, ("e", e2, e2.idx))
            for ds in s.used:
                if ds.count > 0:
                    s._wait(e, ("d", ds, ds.count))
        s.free = list(s.dsems)
        s.used = []
        s.pbufs = []


def bc(ap, shape):
    return ap.to_broadcast(list(shape))


def build(T, NL=4, stop=None):
    NT = T // 128
    SLOT = T // 8
    BPG = SLOT // 128
    GT = 256
    NG = T // GT
    TPG = GT // 128
    nc = bass.Bass("TRN2", target_bir_lowering=False)
    dt_in = lambda name, shape, dt=F32: nc.dram_tensor(name, list(shape), dt, kind="ExternalInput").ap()
    dt_sc = lambda name, shape, dt=F32: nc.dram_tensor(name, list(shape), dt, kind="Internal").ap()
    x_in = dt_in("x", [T, D])
    c8 = dt_in("c8", [8, D])
    ffn_w13 = dt_in("ffn_w13", [4, 2, D, 2 * DFF])
    ffn_w2 = dt_in("ffn_w2", [4, 2, DFF, D])
    ada_w = dt_in("ada_w", [4, D, 9 * D])
    ada_b = dt_in("ada_b", [4, 9 * D])
    norm_w = dt_in("norm_w", [4, 3, D])
    ml_w_in = dt_in("mlstm_w_in", [2, D, 3104])
    ml_bg = dt_in("mlstm_b_gate", [2, 32])
    ml_nw = dt_in("mlstm_norm_w", [2, D])
    ml_w_out = dt_in("mlstm_w_out", [2, D, D])
    at_w_in = [dt_in("swa_w_in", [1, D, 1536]), dt_in("axial_w_in", [1, D, 1536])]
    at_qn = [dt_in("swa_q_norm", [1, 64]), dt_in("axial_q_norm", [1, 64])]
    at_kn = [dt_in("swa_k_norm", [1, 64]), dt_in("axial_k_norm", [1, 64])]
    swa_sink = dt_in("swa_sink", [1, 16])
    at_w_out = [dt_in("swa_w_out", [1, D, D]), dt_in("axial_w_out", [1, D, D])]
    ropec = [dt_in("rope_swa_c", [T, 64]), dt_in("rope_ax_c", [T, 64])]
    ropes = [dt_in("rope_swa_s", [T, 64]), dt_in("rope_ax_s", [T, 64])]
    keepf_d = dt_in("keepf", [128, NT])
    keepb_d = dt_in("keepb", [128, NT])
    swab_d = dt_in("swab", [128, NT * 2])
    amask_d = dt_in("amask", [128, 64])
    y_out = nc.dram_tensor("y", [T, D], F32, kind="ExternalOutput").ap()
    xs = dt_sc("xs", [T, D])
    MR = dt_sc("MR", [NL, 9, 8, 128, D])
    QTK = dt_sc("QTK", [10, 128, T], BF16)
    Vd = dt_sc("Vd", [4, 128, NT, 64], BF16)
    OTd = dt_sc("OTd", [16, 64, T], BF16)
    MQK = dt_sc("MQK", [8, 128, T], BF16)
    Ktm = dt_sc("Ktm", [T, 512])
    Vm = dt_sc("Vm", [T, D], BF16)
    SOd = dt_sc("SOd", [T, D])
    GTd = dt_sc("GTd", [T, 32])
    Hf = dt_sc("Hf", [T, D])
    HGd = dt_sc("HGd", [T, D], BF16)

    top = contextlib.ExitStack()
    with top:
        k = K(nc, top)
        pe, act, dve, pool = k.pe, k.act, k.dve, k.pool
        uid = [0]

        def TT(es, name, shape, dt=F32):
            uid[0] += 1
            return es.enter_context(nc.sbuf_tensor("%s_u%d" % (name, uid[0]), list(shape), dt))

        def PP(es, name, shape, dt=F32):
            uid[0] += 1
            return es.enter_context(nc.psum_tensor("%s_u%d" % (name, uid[0]), list(shape), dt))

        identb = TT(top, "identb", [128, 128], BF16)
        identf = TT(top, "identf", [128, 128])
        TRIi = TT(top, "TRIi", [128, 128])
        TRIr = TT(top, "TRIr", [128, 128])
        ONESf = TT(top, "ONESf", [128, 128])
        Sel = TT(top, "Sel", [8, 8, 128])
        E65 = TT(top, "E65", [65, 64])
        nhalf = TT(top, "nhalf", [128, 32])
        cb = k.buf("consts")

        def mkmask(t, pattern, cmul, cmp, base=0):
            k.op(pool, lambda: nc.gpsimd.memset(t, 1.0), w=[cb])
            k.op(pool, lambda: nc.gpsimd.affine_select(out=t, in_=t, pattern=pattern, compare_op=cmp, fill=0.0,
                                                       base=base, channel_multiplier=cmul), w=[cb])
        mkmask(identf[:], [[-1, 128]], 1, ALU.is_equal)
        mkmask(TRIi[:], [[1, 128]], -1, ALU.is_ge)
        mkmask(TRIr[:], [[-1, 128]], 1, ALU.is_ge)
        mkmask(Sel[:], [[-1, 8], [0, 128]], 1, ALU.is_equal)
        mkmask(E65[:], [[0, 64]], 1, ALU.is_equal, base=-64)
        CAPi = TT(top, "CAPi", [128, 128])
        CAPr = TT(top, "CAPr", [128, 128])
        k.op(pool, lambda: nc.gpsimd.memset(ONESf[:], 1.0), w=[cb])
        k.op(pool, lambda: nc.gpsimd.memset(nhalf[:], -0.5), w=[cb])
        k.op(dve, lambda: nc.vector.tensor_copy(out=identb[:], in_=identf[:]), r=[cb], w=[cb])
        k.op(dve, lambda: nc.vector.tensor_scalar(out=CAPi[:], in0=TRIi[:], scalar1=10016.0, scalar2=-10000.0, op0=ALU.mult, op1=ALU.add), w=[cb])
        k.op(dve, lambda: nc.vector.tensor_scalar(out=CAPr[:], in0=TRIr[:], scalar1=10016.0, scalar2=-10000.0, op0=ALU.mult, op1=ALU.add), w=[cb])
        k.barrier()

        state = {"src": x_in, "nph": 0}

        def phase_done():
            k.barrier()
            state["nph"] += 1
            return stop is not None and state["nph"] >= stop

        def load_weight(es, dst, dstbuf, pieces):
            nmax = max(p[1].shape[-1] for p in pieces)
            stg = [TT(es, "wstg%d" % i, [128, nmax]) for i in range(3)]
            sb = [k.buf("wstg%d" % i) for i in range(3)]
            cv = [dve, pool, act]
            for i, (d_ap, s_ap) in enumerate(pieces):
                P, n = s_ap.shape[0], s_ap.shape[-1]
                j = i % 3
                k.dma([(stg[j][0:P, 0:n], s_ap)], w=[sb[j]])
                e = cv[i % 3]
                if e is act:
                    k.op(act, lambda: nc.scalar.copy(out=d_ap, in_=stg[j][0:P, 0:n]), r=[sb[j]], w=[dstbuf])
                else:
                    k.op(e, lambda: e.h.tensor_copy(out=d_ap, in_=stg[j][0:P, 0:n]), r=[sb[j]], w=[dstbuf])

        class NormCtx:
            def __init__(s, es, nb=2):
                s.hb = [TT(es, "n_hb%d" % i, [128, D], BF16) for i in range(nb)]
                s.hbb = [k.buf("n_hb%d" % i) for i in range(nb)]
                s.sm = [TT(es, "n_sm%d" % i, [128, 4]) for i in range(nb)]
                s.smb = [k.buf("n_sm%d" % i) for i in range(nb)]
                s.pT = [PP(es, "n_pT%d" % i, [128, 8, 128], BF16) for i in range(2)]
                s.pTb = [k.buf("n_pT%d" % i) for i in range(2)]
                s.n = 0

            def part1(s, xa, xab, A, Ab, B, Bb):
                i = s.n % len(s.hb)
                s.cur = i
                hb, hbb, sm, smb = s.hb[i], s.hbb[i], s.sm[i], s.smb[i]
                k.op(act, lambda: nc.scalar.activation(out=hb[:], in_=xa, func=AF.Square, accum_out=sm[:, 0:1]),
                     r=[xab], w=[hbb, smb])
                k.op(dve, lambda: nc.vector.tensor_scalar(out=sm[:, 1:2], in0=sm[:, 0:1], scalar1=1.0 / D, scalar2=EPS,
                                                          op0=ALU.mult, op1=ALU.add), r=[smb], w=[smb])
                k.op(pool, lambda: nc.gpsimd.tensor_tensor(out=sm[:, 2:3], in0=sm[:, 1:2], in1=nhalf[:, 0:1], op=ALU.pow),
                     r=[smb], w=[smb])
                k.op(dve, lambda: nc.vector.scalar_tensor_tensor(out=xa, in0=xa, scalar=sm[:, 2:3], in1=A,
                                                                 op0=ALU.mult, op1=ALU.mult), r=[smb, Ab, xab], w=[xab])
                k.op(pool, lambda: nc.gpsimd.tensor_tensor(out=hb[:], in0=xa, in1=B, op=ALU.add), r=[xab, Bb], w=[hbb])

            def part2(s, hT, hTb):
                i = s.cur
                j = s.n % 2
                s.n += 1
                for kc in range(KC):
                    k.op(pe, lambda: nc.tensor.transpose(s.pT[j][:, kc, :], s.hb[i][:, kc * 128:(kc + 1) * 128], identb[:]),
                         r=[s.hbb[i]], w=[s.pTb[j]])
                k.op(act, lambda: nc.scalar.copy(out=hT, in_=s.pT[j][:]), r=[s.pTb[j]], w=[hTb])

        class ModTiles:
            def __init__(s, es, l, ms, pfx):
                s.l, s.ms = l, ms
                s.t = {m: TT(es, "%s_mod%d" % (pfx, m), [128, D]) for m in ms}
                s.b = {m: k.buf("%s_mod%d" % (pfx, m)) for m in ms}
                s.slot = -1

            def need(s, slot):
                if slot != s.slot:
                    s.slot = slot
                    for m in s.ms:
                        k.dma([(s.t[m][:], MR[s.l, m, slot])], w=[s.b[m]])

        class Epilogue:
            def __init__(s, es, nb=2):
                s.xb = [TT(es, "e_xb%d" % i, [128, D]) for i in range(nb)]
                s.xbb = [k.buf("e_xb%d" % i) for i in range(nb)]
                s.n = 0

            def load(s, src, tok0):
                i = s.n % len(s.xb)
                k.dma([(s.xb[i][:], src[tok0:tok0 + 128, :])], w=[s.xbb[i]])
                return i

            def apply(s, i, py, pyb, G, Gb, dst, tok0):
                for h in range(2):
                    k.op(dve, lambda: nc.vector.tensor_tensor(out=py[h][:], in0=py[h][:], in1=G[:, h * 512:(h + 1) * 512],
                                                              op=ALU.mult), r=[Gb], w=[pyb[h]])
                    k.op(dve, lambda: nc.vector.tensor_tensor(out=s.xb[i][:, h * 512:(h + 1) * 512],
                                                              in0=s.xb[i][:, h * 512:(h + 1) * 512], in1=py[h][:], op=ALU.add),
                         r=[pyb[h]], w=[s.xbb[i]])
                k.dma([(dst[tok0:tok0 + 128, :], s.xb[i][:])], r=[s.xbb[i]])
                s.n += 1

        def phase_mod():
            es = contextlib.ExitStack()
            with es:
                c8t = TT(es, "c8t", [8, D]); c8b = k.buf("c8t")
                c8s = TT(es, "c8s", [8, D], BF16)
                csT = TT(es, "csT", [128, KC, 8], BF16); csTb = k.buf("csT")
                csrep = TT(es, "csrep", [128, 8, KC, 128], BF16); csrb = k.buf("csrep")
                pcs = PP(es, "pcs", [128, KC, 8], BF16); pcsb = k.buf("pcs")
                k.dma([(c8t[:], c8[:, :])], w=[c8b])
                k.op(act, lambda: nc.scalar.activation(out=c8s[:], in_=c8t[:], func=AF.Silu), r=[c8b], w=[c8b])
                for kc in range(KC):
                    k.op(pe, lambda: nc.tensor.transpose(pcs[:, kc, :], c8s[0:8, kc * 128:(kc + 1) * 128], identb[0:8, 0:8]),
                         r=[c8b], w=[pcsb])
                k.op(dve, lambda: nc.vector.tensor_copy(out=csT[:], in_=pcs[:]), r=[pcsb], w=[csTb])
                for sl in range(8):
                    k.op(dve, lambda: nc.vector.tensor_copy(out=csrep[:, sl], in_=bc(csT[:, :, sl:sl + 1], [128, KC, 128])),
                         r=[csTb], w=[csrb])
                stg = [TT(es, "m_stg%d" % i, [128, KC, 512]) for i in range(2)]
                stgb = [k.buf("m_stg%d" % i) for i in range(2)]
                wst = [TT(es, "m_wst%d" % i, [128, KC, 512], BF16) for i in range(2)]
                wstb = [k.buf("m_wst%d" % i) for i in range(2)]
                adb = [TT(es, "m_adb%d" % i, [128, 512]) for i in range(2)]
                adbb = [k.buf("m_adb%d" % i) for i in range(2)]
                nwr = TT(es, "m_nwr", [128, 3, D]); nwrb = k.buf("m_nwr")
                mo = [TT(es, "m_mo%d" % i, [128, 512]) for i in range(3)]
                mob = [k.buf("m_mo%d" % i) for i in range(3)]
                pm = [PP(es, "m_pm%d" % i, [128, 512]) for i in range(3)]
                pmb = [k.buf("m_pm%d" % i) for i in range(3)]
                n = 0
                cgi = 0
                for l in range(NL):
                    k.dma([(nwr[:, j, :], norm_w[l, j:j + 1, :].partition_broadcast(128)) for j in range(3)], w=[nwrb])
                    for m in range(9):
                        for half in range(2):
                            c0 = m * D + half * 512
                            j = cgi % 2
                            cgi += 1
                            k.dma([(stg[j][:], ada_w[l, :, c0:c0 + 512].rearrange("(kc p) n -> p kc n", p=128))], w=[stgb[j]])
                            k.dma([(adb[j][:], ada_b[l:l + 1, c0:c0 + 512].partition_broadcast(128))], w=[adbb[j]])
                            k.op(pool, lambda: nc.gpsimd.tensor_copy(out=wst[j][:], in_=stg[j][:]), r=[stgb[j]], w=[wstb[j]])
                            for sl in range(8):
                                q = n % 3
                                n += 1
                                for kc in range(KC):
                                    k.op(pe, lambda: nc.tensor.matmul(pm[q][:], csrep[:, sl, kc, :], wst[j][:, kc, :],
                                                                      start=(kc == 0), stop=(kc == KC - 1)),
                                         r=[csrb, wstb[j]], w=[pmb[q]])
                                k.op(dve, lambda: nc.vector.tensor_tensor(out=mo[q][:], in0=pm[q][:], in1=adb[j][:], op=ALU.add),
                                     r=[pmb[q], adbb[j]], w=[mob[q]])
                                if m in (1, 4, 7):
                                    k.op(dve, lambda: nc.vector.scalar_tensor_tensor(
                                        out=mo[q][:], in0=mo[q][:], scalar=1.0, in1=nwr[:, m // 3, half * 512:(half + 1) * 512],
                                        op0=ALU.add, op1=ALU.mult), r=[nwrb], w=[mob[q]])
                                elif m in (2, 8):
                                    k.op(dve, lambda: nc.vector.tensor_scalar(out=mo[q][:], in0=mo[q][:], scalar1=0.5, scalar2=None,
                                                                              op0=ALU.mult), w=[mob[q]])
                                k.dma([(MR[l, m, sl, :, half * 512:(half + 1) * 512], mo[q][:])], r=[mob[q]])
            return phase_done()

        def phase_ffn(l, which, dst):
            src = state["src"]
            k.pe_eager = False
            mi = 0 if which == 0 else 6
            es = contextlib.ExitStack()
            with es:
                W13 = TT(es, "W13", [128, KC, 2 * DFF], BF16); W13b = k.buf("W13")
                W2 = TT(es, "W2", [128, NFC, D], BF16); W2b = k.buf("W2")
                es2 = contextlib.ExitStack()
                with es2:
                    pieces = []
                    for kc in range(KC):
                        for c in range(4):
                            pieces.append((W13[:, kc, c * 1408:(c + 1) * 1408],
                                           ffn_w13[l, which, kc * 128:(kc + 1) * 128, c * 1408:(c + 1) * 1408]))
                    for fc in range(NFC):
                        pieces.append((W2[:, fc, :], ffn_w2[l, which, fc * 128:(fc + 1) * 128, :]))
                    wb = k.buf("Wall")
                    load_weight(es2, None, wb, pieces)
                    k.barrier()
                xa = [TT(es, "f_xa%d" % i, [128, D]) for i in range(2)]
                xab = [k.buf("f_xa%d" % i) for i in range(2)]
                nctx = NormCtx(es)
                mods = ModTiles(es, l, [mi, mi + 1], "f")
                modg = ModTiles(es, l, [mi + 2], "fg")
                hT = [TT(es, "f_hT%d" % i, [128, KC, GT], BF16) for i in range(2)]
                hTb = [k.buf("f_hT%d" % i) for i in range(2)]
                sg = [TT(es, "f_sg%d" % i, [128, GT]) for i in range(2)]
                sgb = [k.buf("f_sg%d" % i) for i in range(2)]
                uT = TT(es, "f_uT", [128, NFC, GT], BF16); uTb = k.buf("f_uT")
                ep = Epilogue(es)
                pg = [PP(es, "f_pg%d" % i, [128, GT]) for i in range(2)]
                pgb = [k.buf("f_pg%d" % i) for i in range(2)]
                pu = [PP(es, "f_pu%d" % i, [128, GT]) for i in range(2)]
                pub = [k.buf("f_pu%d" % i) for i in range(2)]
                py = [PP(es, "f_py%d" % i, [128, 512]) for i in range(2)]
                pyb = [k.buf("f_py%d" % i) for i in range(2)]
                cnt = {"xa": 0}

                def norm1(g):
                    mods.need((g * GT) // SLOT)
                    pend = []
                    for tt in range(TPG):
                        i = cnt["xa"] % 2
                        cnt["xa"] += 1
                        tok0 = g * GT + tt * 128
                        k.dma([(xa[i][:], src[tok0:tok0 + 128, :])], w=[xab[i]])
                        nctx.part1(xa[i][:], xab[i], mods.t[mi + 1][:], mods.b[mi + 1], mods.t[mi][:], mods.b[mi])
                        nctx.part2(hT[g % 2][:, :, tt * 128:(tt + 1) * 128], hTb[g % 2])

                norm1(0)
                for g in range(NG):
                    h_ = hT[g % 2]
                    for fc in range(NFC):
                        q = fc % 2
                        for kc in range(KC):
                            k.op(pe, lambda: nc.tensor.matmul(pg[q][:], W13[:, kc, fc * 128:(fc + 1) * 128], h_[:, kc, :],
                                                              start=(kc == 0), stop=(kc == KC - 1)), r=[hTb[g % 2]], w=[pgb[q]], sig=(kc == KC - 1))
                        for kc in range(KC):
                            k.op(pe, lambda: nc.tensor.matmul(pu[q][:], W13[:, kc, DFF + fc * 128:DFF + (fc + 1) * 128], h_[:, kc, :],
                                                              start=(kc == 0), stop=(kc == KC - 1)), r=[hTb[g % 2]], w=[pub[q]], sig=(kc == KC - 1))
                        k.op(act, lambda: nc.scalar.activation(out=sg[q][:], in_=pg[q][:], func=AF.Silu), r=[pgb[q]], w=[sgb[q]])
                        k.op(dve, lambda: nc.vector.tensor_tensor(out=uT[:, fc, :], in0=sg[q][:], in1=pu[q][:], op=ALU.mult),
                             r=[sgb[q], pub[q]], w=[uTb])
                        if fc == 10 and g + 1 < NG:
                            norm1(g + 1)
                    modg.need((g * GT) // SLOT)
                    for tt in range(TPG):
                        tok0 = g * GT + tt * 128
                        xi = ep.load(src, tok0)
                        for h in range(2):
                            for fc in range(NFC):
                                k.op(pe, lambda: nc.tensor.matmul(py[h][:], uT[:, fc, tt * 128:(tt + 1) * 128],
                                                                  W2[:, fc, h * 512:(h + 1) * 512],
                                                                  start=(fc == 0), stop=(fc == NFC - 1)), r=[uTb], w=[pyb[h]], sig=(fc == NFC - 1))
                        ep.apply(xi, py, pyb, modg.t[mi + 2], modg.b[mi + 2], dst, tok0)
            state["src"] = dst
            k.pe_eager = True
            return phase_done()

        def phase_att_in(l, kind):
            src = state["src"]
            es = contextlib.ExitStack()
            with es:
                Win = TT(es, "a_Win", [128, KC, 1536], BF16); Winb = k.buf("a_Win")
                es2 = contextlib.ExitStack()
                with es2:
                    pieces = [(Win[:, kc, :], at_w_in[kind][0, kc * 128:(kc + 1) * 128, :]) for kc in range(KC)]
                    load_weight(es2, None, Winb, pieces)
                    k.barrier()
                nwr = TT(es, "a_nwr", [128, 20, 64]); nwrb = k.buf("a_nwr")
                nws = TT(es, "a_nws", [128, 2, 64])
                k.dma([(nws[:, 0, :], at_qn[kind][0:1, :].partition_broadcast(128)),
                       (nws[:, 1, :], at_kn[kind][0:1, :].partition_broadcast(128))], w=[nwrb])
                k.op(dve, lambda: nc.vector.tensor_scalar(out=nwr[:, 0:16, :], in0=bc(nws[:, 0:1, :], [128, 16, 64]), scalar1=0.125,
                                                          scalar2=None, op0=ALU.mult), r=[nwrb], w=[nwrb])
                k.op(dve, lambda: nc.vector.tensor_copy(out=nwr[:, 16:20, :], in_=bc(nws[:, 1:2, :], [128, 4, 64])), r=[nwrb], w=[nwrb])
                xa = [TT(es, "a_xa%d" % i, [128, D]) for i in range(2)]
                xab = [k.buf("a_xa%d" % i) for i in range(2)]
                nctx = NormCtx(es)
                mods = ModTiles(es, l, [3, 4], "a")
                hT = [TT(es, "a_hT%d" % i, [128, KC, 128], BF16) for i in range(2)]
                hTb = [k.buf("a_hT%d" % i) for i in range(2)]
                pq = [PP(es, "a_pq%d" % i, [128, 512]) for i in range(3)]
                pqb = [k.buf("a_pq%d" % i) for i in range(3)]
                ptq = PP(es, "a_ptq", [128, 8, 128], BF16); ptqb = k.buf("a_ptq")
                ptk = PP(es, "a_ptk", [128, 2, 128], BF16); ptkb = k.buf("a_ptk")
                NB_ = 2
                qk = [TT(es, "a_qk%d" % i, [128, 20, 64]) for i in range(NB_)]
                qkb = [k.buf("a_qk%d" % i) for i in range(NB_)]
                sq = [TT(es, "a_sq%d" % i, [128, 20, 64]) for i in range(NB_)]
                sqb = [k.buf("a_sq%d" % i) for i in range(NB_)]
                t2 = [TT(es, "a_t2%d" % i, [128, 20, 64]) for i in range(NB_)]
                t2b = [k.buf("a_t2%d" % i) for i in range(NB_)]
                qr = [TT(es, "a_qr%d" % i, [128, 1280], BF16) for i in range(NB_)]
                qrb = [k.buf("a_qr%d" % i) for i in range(NB_)]
                vb = [TT(es, "a_vb%d" % i, [128, 4, 64], BF16) for i in range(NB_)]
                vbb = [k.buf("a_vb%d" % i) for i in range(NB_)]
                sm = [TT(es, "a_sm%d" % i, [128, 3, 20]) for i in range(NB_)]
                smb = [k.buf("a_sm%d" % i) for i in range(NB_)]
                cs = [TT(es, "a_cs%d" % i, [128, 2, 64]) for i in range(NB_)]
                csb = [k.buf("a_cs%d" % i) for i in range(NB_)]
                qT = [TT(es, "a_qT%d" % i, [128, 10, 128], BF16) for i in range(NB_)]
                qTb = [k.buf("a_qT%d" % i) for i in range(NB_)]
                hbk = 32 if kind == 0 else 16
                nbk = 64 // (2 * hbk)
                for t in range(NT):
                    i = t % 2
                    tok0 = t * 128
                    mods.need(tok0 // SLOT)
                    k.dma([(xa[i][:], src[tok0:tok0 + 128, :])], w=[xab[i]])
                    k.dma([(cs[i][:, 0, :], ropec[kind][tok0:tok0 + 128, :]), (cs[i][:, 1, :], ropes[kind][tok0:tok0 + 128, :])], w=[csb[i]])
                    nctx.part1(xa[i][:], xab[i], mods.t[4][:], mods.b[4], mods.t[3][:], mods.b[3])
                    nctx.part2(hT[i][:], hTb[i])
                    for n in range(3):
                        for kc in range(KC):
                            k.op(pe, lambda: nc.tensor.matmul(pq[n][:], hT[i][:, kc, :], Win[:, kc, n * 512:(n + 1) * 512],
                                                              start=(kc == 0), stop=(kc == KC - 1)), r=[hTb[i], Winb], w=[pqb[n]])
                    qkf = qk[i][:].rearrange("p h d -> p (h d)")
                    k.op(act, lambda: nc.scalar.copy(out=qkf[:, 0:512], in_=pq[0][:]), r=[pqb[0]], w=[qkb[i]])
                    k.op(act, lambda: nc.scalar.copy(out=qkf[:, 512:1024], in_=pq[1][:]), r=[pqb[1]], w=[qkb[i]])
                    k.op(act, lambda: nc.scalar.copy(out=qkf[:, 1024:1280], in_=pq[2][:, 0:256]), r=[pqb[2]], w=[qkb[i]])
                    k.op(act, lambda: nc.scalar.copy(out=vb[i][:].rearrange("p h d -> p (h d)"), in_=pq[2][:, 256:512]),
                         r=[pqb[2]], w=[vbb[i]])
                    k.dma([(Vd[:, :, t, :].rearrange("g p d -> p g d"), vb[i][:])], r=[vbb[i]])
                    k.op(act, lambda: nc.scalar.activation(out=sq[i][:], in_=qk[i][:], func=AF.Square), r=[qkb[i]], w=[sqb[i]])
                    k.op(dve, lambda: nc.vector.tensor_reduce(out=sm[i][:, 0, :], in_=sq[i][:], axis=AX.X, op=ALU.add), r=[sqb[i]], w=[smb[i]])
                    k.op(dve, lambda: nc.vector.tensor_scalar(out=sm[i][:, 1, :], in0=sm[i][:, 0, :], scalar1=1.0 / 64, scalar2=EPS,
                                                              op0=ALU.mult, op1=ALU.add), r=[smb[i]], w=[smb[i]])
                    k.op(pool, lambda: nc.gpsimd.tensor_tensor(out=sm[i][:, 2, :], in0=sm[i][:, 1, :], in1=nhalf[:, 0:20], op=ALU.pow),
                         r=[smb[i]], w=[smb[i]])
                    k.op(dve, lambda: nc.vector.tensor_tensor(out=qk[i][:], in0=qk[i][:], in1=bc(sm[i][:, 2, :].unsqueeze(2), [128, 20, 64]),
                                                              op=ALU.mult), r=[smb[i]], w=[qkb[i]])
                    k.op(pool, lambda: nc.gpsimd.tensor_tensor(out=qk[i][:], in0=qk[i][:], in1=nwr[:], op=ALU.mult), r=[nwrb], w=[qkb[i]])
                    k.op(dve, lambda: nc.vector.tensor_tensor(out=sq[i][:], in0=qk[i][:], in1=bc(cs[i][:, 0:1, :], [128, 20, 64]), op=ALU.mult),
                         r=[qkb[i], csb[i]], w=[sqb[i]])
                    q5 = qk[i][:].rearrange("p h (b two e) -> p h b two e", two=2, e=hbk)
                    t5 = t2[i][:].rearrange("p h (b two e) -> p h b two e", two=2, e=hbk)
                    s5 = cs[i][:, 1:2, :].rearrange("p o (b two e) -> p o b two e", two=2, e=hbk)
                    for half in range(2):
                        k.op(pool, lambda: nc.gpsimd.tensor_tensor(out=t5[:, :, :, half, :], in0=q5[:, :, :, 1 - half, :],
                                                                   in1=bc(s5[:, :, :, half, :], [128, 20, nbk, hbk]), op=ALU.mult),
                             r=[qkb[i], csb[i]], w=[t2b[i]])
                    k.op(dve, lambda: nc.vector.tensor_tensor(out=qr[i][:], in0=sq[i][:].rearrange("p h d -> p (h d)"),
                                                              in1=t2[i][:].rearrange("p h d -> p (h d)"), op=ALU.add),
                         r=[sqb[i], t2b[i]], w=[qrb[i]])
                    for j in range(8):
                        k.op(pe, lambda: nc.tensor.transpose(ptq[:, j, :], qr[i][:, j * 128:(j + 1) * 128], identb[:]), r=[qrb[i]], w=[ptqb])
                    for j in range(2):
                        k.op(pe, lambda: nc.tensor.transpose(ptk[:, j, :], qr[i][:, 1024 + j * 128:1024 + (j + 1) * 128], identb[:]),
                             r=[qrb[i]], w=[ptkb])
                    k.op(act, lambda: nc.scalar.copy(out=qT[i][:, 0:8, :], in_=ptq[:]), r=[ptqb], w=[qTb[i]])
                    k.op(act, lambda: nc.scalar.copy(out=qT[i][:, 8:10, :], in_=ptk[:]), r=[ptkb], w=[qTb[i]])
                    k.dma([(QTK[:, :, tok0:tok0 + 128].rearrange("j p t -> p j t"), qT[i][:])], r=[qTb[i]])
            return phase_done()

        def phase_att(l, kind):
            es = contextlib.ExitStack()
            with es:
                GS = 2 if kind == 1 else 1
                ps = [PP(es, "t_ps%d" % i, [128, 1024]) for i in range(2)]
                psb = [k.buf("t_ps%d" % i) for i in range(2)]
                po = [PP(es, "t_po%d" % i, [65, 512]) for i in range(2)]
                pob = [k.buf("t_po%d" % i) for i in range(2)]
                pbt = [PP(es, "t_pb%d" % i, [64, 512]) for i in range(2)]
                pbb = [k.buf("t_pb%d" % i) for i in range(2)]
                KT = [TT(es, "t_KT%d" % i, [128, T], BF16) for i in range(2)]
                KTb = [k.buf("t_KT%d" % i) for i in range(2)]
                V = [TT(es, "t_V%d" % i, [128, NT, 65], BF16) for i in range(2)]
                Vb = [k.buf("t_V%d" % i) for i in range(2)]
                am = TT(es, "t_am", [128, 64]); amb = k.buf("t_am")
                sw = TT(es, "t_sw", [128, NT * 2])
                k.dma([(am[:], amask_d[:, :]), (sw[:], swab_d[:, :])], w=[amb])
                es_s = TT(es, "t_ess", [65, 16])
                esx = TT(es, "t_esx", [65, 16, 128]); esb = k.buf("t_esx")
                if kind == 0:
                    k.dma([(es_s[64:65, :], swa_sink[0:1, :])], w=[esb])
                    k.op(act, lambda: nc.scalar.activation(out=es_s[64:65, :], in_=es_s[64:65, :], func=AF.Exp), r=[esb], w=[esb])
                    k.op(dve, lambda: nc.vector.tensor_copy(out=esx[64:65, :, :], in_=bc(es_s[64:65, :].unsqueeze(2), [1, 16, 128])),
                         r=[esb], w=[esb])
                for i in range(2):
                    k.op(pool, lambda: nc.gpsimd.memset(V[i][:, :, 64:65], 1.0), w=[Vb[i]])
                NQ = 4
                qt = [TT(es, "t_qt%d" % i, [128, 4, 128], BF16) for i in range(NQ)]
                qtb = [k.buf("t_qt%d" % i) for i in range(NQ)]
                NP = 3
                pt = [TT(es, "t_pt%d" % i, [128, 2, 4, 128], BF16) for i in range(NP)]
                ptb = [k.buf("t_pt%d" % i) for i in range(NP)]
                R = [TT(es, "t_R%d" % i, [65, 512]) for i in range(2)]
                Rb = [k.buf("t_R%d" % i) for i in range(2)]
                rec = [TT(es, "t_rec%d" % i, [64, 512]) for i in range(2)]
                recb = [k.buf("t_rec%d" % i) for i in range(2)]
                ot = [TT(es, "t_ot%d" % i, [64, 4, 128], BF16) for i in range(2)]
                otb = [k.buf("t_ot%d" % i) for i in range(2)]

                def load_kv(g):
                    i = g % 2
                    CK = min(T, 2048)
                    ksrc = QTK[8 + g // 2, (g % 2) * 64:(g % 2) * 64 + 64, :]
                    if kind == 0:
                        k.dma([(KT[i][0:64, c0:c0 + CK], ksrc[:, c0:c0 + CK]) for c0 in range(0, T, CK)], w=[KTb[i]])
                    else:
                        ks4 = ksrc.rearrange("d (n two t) -> d n two t", two=2, t=128)
                        NBH = NT // 2
                        CB = min(NBH, 16)
                        prs = []
                        for par in range(2):
                            kd3 = KT[i][par * 64:(par + 1) * 64, 0:T // 2].rearrange("d (n t) -> d n t", t=128)
                            for b0 in range(0, NBH, CB):
                                prs.append((kd3[:, b0:b0 + CB, :], ks4[:, b0:b0 + CB, par, :]))
                        k.dma(prs, w=[KTb[i]])
                    VB = min(NT, 8)
                    k.dma([(V[i][:, b0:b0 + VB, 0:64], Vd[g, :, b0:b0 + VB, :]) for b0 in range(0, NT, VB)], w=[Vb[i]])

                items = []
                for g in range(4):
                    for i in range(NT):
                        if kind == 0:
                            js = [j for j in (i - 1, i, i + 1) if 0 <= j < NT]
                            grs = [[j] for j in js]
                        else:
                            grs = [list(range(j0, j0 + GS)) for j0 in range(0, NT, GS)]
                        for n, gr in enumerate(grs):
                            items.append((g, i, gr, n == 0, n == len(grs) - 1))
                qslot = {}
                cn = {"q": 0}

                def stage_S(n):
                    g, i, gr, first, last = items[n]
                    gi = g % 2
                    if first:
                        if i == 0 and g == 0:
                            load_kv(0)
                        if i == 1 and g + 1 < 4:
                            load_kv(g + 1)
                        qi = cn["q"] % NQ
                        cn["q"] += 1
                        qslot[(g, i)] = qi
                        qsrc = QTK[2 * g:2 * g + 2, :, i * 128:(i + 1) * 128].rearrange("j (hh d) t -> d (j hh) t", hh=2)
                        if kind == 0:
                            k.dma([(qt[qi][0:64], qsrc)], w=[qtb[qi]])
                        else:
                            k.dma([(qt[qi][0:64], qsrc), (qt[qi][64:128], qsrc)], w=[qtb[qi]])
                    qi = qslot[(g, i)]
                    si = n % 2
                    for s_, j in enumerate(gr):
                        if kind == 0:
                            lhs = KT[gi][0:64, j * 128:(j + 1) * 128]
                            rhs = qt[qi][0:64].rearrange("d h t -> d (h t)")
                        else:
                            par = j % 2
                            lhs = KT[gi][par * 64:(par + 1) * 64, (j // 2) * 128:(j // 2 + 1) * 128]
                            rhs = qt[qi][par * 64:(par + 1) * 64].rearrange("d h t -> d (h t)")
                        k.op(pe, lambda: nc.tensor.matmul(ps[si][:, s_ * 512:(s_ + 1) * 512], lhs, rhs, start=True, stop=True),
                             r=[KTb[gi], qtb[qi]], w=[psb[si]], sig=(s_ == len(gr) - 1))

                def stage_E(n):
                    g, i, gr, first, last = items[n]
                    si = n % 2
                    pi = n % NP
                    j = gr[0]
                    if kind == 1:
                        col = (i // BPG) * 8 + (j // BPG)
                        bias = am[:, col:col + 1]
                    elif j == i:
                        bias = 0.0
                    elif j < i:
                        bias = sw[:, 2 * i:2 * i + 1]
                    else:
                        bias = sw[:, 2 * i + 1:2 * i + 2]
                    ng = len(gr)
                    ptf = pt[pi][:, 0:ng].rearrange("k s h t -> k (s h t)")
                    k.op(act, lambda: nc.scalar.activation(out=ptf, in_=ps[si][:, 0:ng * 512], func=AF.Exp, bias=bias), r=[psb[si], amb], w=[ptb[pi]])
                    if kind == 0 and j != i:
                        tri = TRIr if j < i else TRIi
                        k.op(dve, lambda: nc.vector.tensor_tensor(out=pt[pi][:, 0], in0=pt[pi][:, 0], in1=bc(tri[:].unsqueeze(1), [128, 4, 128]),
                                                                  op=ALU.mult), r=[cb], w=[ptb[pi]])

                def stage_PV(n):
                    g, i, gr, first, last = items[n]
                    gi = g % 2
                    pi = n % NP
                    oi = (g * NT + i) % 2
                    for s_, j in enumerate(gr):
                        k.op(pe, lambda: nc.tensor.matmul(po[oi][:], V[gi][:, j, :], pt[pi][:, s_].rearrange("k h t -> k (h t)"),
                                                          start=(first and s_ == 0), stop=(last and s_ == len(gr) - 1)),
                             r=[Vb[gi], ptb[pi]], w=[pob[oi]])

                def epi1(n):
                    g, i, gr, first, last = items[n]
                    oi = (g * NT + i) % 2
                    k.op(dve, lambda: nc.vector.tensor_copy(out=R[oi][:], in_=po[oi][:]), r=[pob[oi]], w=[Rb[oi]])
                    if kind == 0:
                        k.op(dve, lambda: nc.vector.tensor_tensor(out=R[oi][64:65, :], in0=R[oi][64:65, :],
                                                                  in1=esx[64:65, 4 * g:4 * g + 4, :].rearrange("p h t -> p (h t)"), op=ALU.add),
                             r=[esb], w=[Rb[oi]])

                def epi2(n):
                    g, i, gr, first, last = items[n]
                    oi = (g * NT + i) % 2
                    k.op(pe, lambda: nc.tensor.matmul(pbt[oi][:], E65[:], R[oi][:], start=True, stop=True), r=[Rb[oi], cb], w=[pbb[oi]])
                    k.op(dve, lambda: nc.vector.reciprocal(out=rec[oi][:], in_=pbt[oi][:]), r=[pbb[oi]], w=[recb[oi]])
                    k.op(dve, lambda: nc.vector.tensor_tensor(out=ot[oi][:].rearrange("d h t -> d (h t)"), in0=R[oi][0:64, :], in1=rec[oi][:],
                                                              op=ALU.mult), r=[Rb[oi], recb[oi]], w=[otb[oi]])
                    k.dma([(OTd[4 * g:4 * g + 4, :, i * 128:(i + 1) * 128].rearrange("h d t -> d h t"), ot[oi][:])], r=[otb[oi]])

                NI = len(items)
                stage_S(0)
                pend = None
                for n in range(NI):
                    if n + 1 < NI:
                        stage_S(n + 1)
                    if pend is not None:
                        epi2(pend)
                        pend = None
                    stage_E(n)
                    stage_PV(n)
                    if items[n][4]:
                        epi1(n)
                        pend = n
                if pend is not None:
                    epi2(pend)
            return phase_done()

        def phase_att_out(l, kind, dst):
            src = state["src"]
            es = contextlib.ExitStack()
            with es:
                Wo = TT(es, "o_Wo", [64, 16, D], BF16); Wob = k.buf("o_Wo")
                es2 = contextlib.ExitStack()
                with es2:
                    pieces = [(Wo[:, h, :], at_w_out[kind][0, h * 64:(h + 1) * 64, :]) for h in range(16)]
                    load_weight(es2, None, Wob, pieces)
                    k.barrier()
                mods = ModTiles(es, l, [5], "o")
                ep = Epilogue(es)
                oT = [TT(es, "o_oT%d" % i, [64, 16, 128], BF16) for i in range(3)]
                oTb = [k.buf("o_oT%d" % i) for i in range(3)]
                py = [[PP(es, "o_py%d_%d" % (i, h), [128, 512]) for h in range(2)] for i in range(2)]
                pyb = [[k.buf("o_py%d_%d" % (i, h)) for h in range(2)] for i in range(2)]
                for t in range(NT):
                    tok0 = t * 128
                    i3 = t % 3
                    i2 = t % 2
                    mods.need(tok0 // SLOT)
                    k.dma([(oT[i3][:], OTd[:, :, tok0:tok0 + 128].rearrange("h d t -> d h t"))], w=[oTb[i3]])
                    xi = ep.load(src, tok0)
                    for h in range(2):
                        for hd in range(16):
                            k.op(pe, lambda: nc.tensor.matmul(py[i2][h][:], oT[i3][:, hd, :], Wo[:, hd, h * 512:(h + 1) * 512],
                                                              start=(hd == 0), stop=(hd == 15)), r=[oTb[i3], Wob], w=[pyb[i2][h]])
                    ep.apply(xi, py[i2], pyb[i2], mods.t[5], mods.b[5], dst, tok0)
            state["src"] = dst
            return phase_done()

        def phase_ml_in(l, j):
            src = state["src"]
            es = contextlib.ExitStack()
            with es:
                Win = TT(es, "m_Win", [128, KC, 3104], BF16); Winb = k.buf("m_Win")
                es2 = contextlib.ExitStack()
                with es2:
                    pieces = []
                    for kc in range(KC):
                        pieces.append((Win[:, kc, 0:1552], ml_w_in[j, kc * 128:(kc + 1) * 128, 0:1552]))
                        pieces.append((Win[:, kc, 1552:3104], ml_w_in[j, kc * 128:(kc + 1) * 128, 1552:3104]))
                    load_weight(es2, None, Winb, pieces)
                    k.barrier()
                bg = TT(es, "m_bg", [128, 32]); bgb = k.buf("m_bg")
                k.dma([(bg[:], ml_bg[j:j + 1, :].partition_broadcast(128))], w=[bgb])
                xa = [TT(es, "mi_xa%d" % i, [128, D]) for i in range(2)]
                xab = [k.buf("mi_xa%d" % i) for i in range(2)]
                nctx = NormCtx(es)
                mods = ModTiles(es, l, [3, 4], "mi")
                hT = [TT(es, "mi_hT%d" % i, [128, KC, 128], BF16) for i in range(2)]
                hTb = [k.buf("mi_hT%d" % i) for i in range(2)]
                pq = [PP(es, "mi_pq%d" % i, [128, 512]) for i in range(4)]
                pqb = [k.buf("mi_pq%d" % i) for i in range(4)]
                ptr = PP(es, "mi_ptr", [128, 8, 128], BF16); ptrb = k.buf("mi_ptr")
                qkb_ = [TT(es, "mi_qkb%d" % i, [128, 1024], BF16) for i in range(2)]
                qkbb = [k.buf("mi_qkb%d" % i) for i in range(2)]
                kf = [TT(es, "mi_kf%d" % i, [128, 512]) for i in range(2)]
                kfb = [k.buf("mi_kf%d" % i) for i in range(2)]
                vt = [TT(es, "mi_vt%d" % i, [128, D], BF16) for i in range(2)]
                vtb = [k.buf("mi_vt%d" % i) for i in range(2)]
                so = [TT(es, "mi_so%d" % i, [128, D]) for i in range(2)]
                sob = [k.buf("mi_so%d" % i) for i in range(2)]
                gg = [TT(es, "mi_gg%d" % i, [128, 4, 32]) for i in range(2)]
                ggb = [k.buf("mi_gg%d" % i) for i in range(2)]
                qT = [TT(es, "mi_qT%d" % i, [128, 8, 128], BF16) for i in range(2)]
                qTb = [k.buf("mi_qT%d" % i) for i in range(2)]
                cn = 0
                for t in range(NT):
                    i = t % 2
                    tok0 = t * 128
                    mods.need(tok0 // SLOT)
                    k.dma([(xa[i][:], src[tok0:tok0 + 128, :])], w=[xab[i]])
                    nctx.part1(xa[i][:], xab[i], mods.t[4][:], mods.b[4], mods.t[3][:], mods.b[3])
                    nctx.part2(hT[i][:], hTb[i])
                    for n in range(7):
                        q = cn % 4
                        cn += 1
                        w_ = 512 if n < 6 else 32
                        for kc in range(KC):
                            k.op(pe, lambda: nc.tensor.matmul(pq[q][:, 0:w_], hT[i][:, kc, :], Win[:, kc, n * 512:n * 512 + w_],
                                                              start=(kc == 0), stop=(kc == KC - 1)), r=[hTb[i], Winb], w=[pqb[q]])
                        SK = os.environ.get("ML_SKIP", "")
                        if n == 0 and "q" in SK: pass
                        elif n == 1 and "k" in SK: pass
                        elif n in (2, 3) and "v" in SK: pass
                        elif n in (4, 5) and "o" in SK: pass
                        elif n == 0:
                            k.op(act, lambda: nc.scalar.activation(out=qkb_[i][:, 0:512], in_=pq[q][:], func=AF.Copy, scale=0.125),
                                 r=[pqb[q]], w=[qkbb[i]])
                        elif n == 1:
                            k.op(dve, lambda: nc.vector.tensor_copy(out=kf[i][:], in_=pq[q][:]), r=[pqb[q]], w=[kfb[i]])
                            k.op(act, lambda: nc.scalar.copy(out=qkb_[i][:, 512:1024], in_=kf[i][:]), r=[kfb[i]], w=[qkbb[i]])
                            k.dma([(Ktm[tok0:tok0 + 128, :], kf[i][:])], r=[kfb[i]])
                        elif n in (2, 3):
                            k.op(dve, lambda: nc.vector.tensor_copy(out=vt[i][:, (n - 2) * 512:(n - 1) * 512], in_=pq[q][:]), r=[pqb[q]], w=[vtb[i]])
                            if n == 3:
                                k.dma([(Vm[tok0:tok0 + 128, :], vt[i][:])], r=[vtb[i]])
                        elif n in (4, 5):
                            k.op(act, lambda: nc.scalar.activation(out=so[i][:, (n - 4) * 512:(n - 3) * 512], in_=pq[q][:], func=AF.Sigmoid),
                                 r=[pqb[q]], w=[sob[i]])
                            if n == 5:
                                k.dma([(SOd[tok0:tok0 + 128, :], so[i][:])], r=[sob[i]])
                        elif "g" not in SK:
                            G = gg[i]
                            k.op(dve, lambda: nc.vector.tensor_tensor(out=G[:, 0, :], in0=pq[q][:, 0:32], in1=bg[:], op=ALU.add),
                                 r=[pqb[q], bgb], w=[ggb[i]])
                            k.op(act, lambda: nc.scalar.activation(out=G[:, 1, :], in_=G[:, 0, :], func=AF.Tanh, scale=1.0 / 15.0), r=[ggb[i]], w=[ggb[i]])
                            k.op(dve, lambda: nc.vector.tensor_scalar(out=G[:, 0, :], in0=G[:, 1, :], scalar1=15.0, scalar2=None, op0=ALU.mult),
                                 r=[ggb[i]], w=[ggb[i]])
                            k.op(act, lambda: nc.scalar.activation(out=G[:, 1, :], in_=G[:, 0, :], func=AF.Exp, scale=-1.0), r=[ggb[i]], w=[ggb[i]])
                            k.op(act, lambda: nc.scalar.activation(out=G[:, 2, :], in_=G[:, 1, :], func=AF.Ln, bias=1.0), r=[ggb[i]], w=[ggb[i]])
                            g4 = G[:, 0, :].rearrange("p (a b h) -> p a b h", a=2, b=2)
                            l4 = G[:, 2, :].rearrange("p (a b h) -> p a b h", a=2, b=2)
                            k.op(dve, lambda: nc.vector.tensor_scalar(out=g4[:, :, 1, :], in0=l4[:, :, 1, :], scalar1=-1.0, scalar2=None, op0=ALU.mult),
                                 r=[ggb[i]], w=[ggb[i]])
                            k.dma([(GTd[tok0:tok0 + 128, :], G[:, 0, :])], r=[ggb[i]])
                    if "t" in SK:
                        continue
                    for jj in range(8):
                        k.op(pe, lambda: nc.tensor.transpose(ptr[:, jj, :], qkb_[i][:, jj * 128:(jj + 1) * 128], identb[:]), r=[qkbb[i]], w=[ptrb])
                    k.op(act, lambda: nc.scalar.copy(out=qT[i][:], in_=ptr[:]), r=[ptrb], w=[qTb[i]])
                    k.dma([(MQK[:, :, tok0:tok0 + 128].rearrange("j p t -> p j t"), qT[i][:])], r=[qTb[i]])
            return phase_done()

        def phase_ml_scan(l, j, direction):
            fwd = direction == 0
            TRI = TRIi if fwd else TRIr
            CAP = CAPi if fwd else CAPr
            es = contextlib.ExitStack()
            with es:
                keep = TT(es, "s_keep", [128, NT]); keepb_ = k.buf("s_keep")
                k.dma([(keep[:], (keepf_d if fwd else keepb_d)[:, :])], w=[keepb_])
                nwr = TT(es, "s_nwr", [128, D]); nwrb = k.buf("s_nwr")
                k.dma([(nwr[:], ml_nw[j:j + 1, :].partition_broadcast(128))], w=[nwrb])
                C = TT(es, "s_C", [64, 8, 129]); Cb_ = k.buf("s_C")
                Cbf = TT(es, "s_Cbf", [64, 8, 129], BF16); Cbfb = k.buf("s_Cbf")
                k.op(dve, lambda: nc.vector.memset(C[:], 0.0), w=[Cb_])
                k.op(pool, lambda: nc.gpsimd.memset(Cbf[:], 0.0), w=[Cbfb])
                NB_ = 2
                mk = lambda nm, shape, dt=F32: ([TT(es, "s_%s%d" % (nm, i), shape, dt) for i in range(NB_)],
                                                [k.buf("s_%s%d" % (nm, i)) for i in range(NB_)])
                QT, QTb = mk("QT", [64, 8, 128], BF16)
                KTt, KTb = mk("KT", [64, 8, 128], BF16)
                Kt, Ktb = mk("Kt", [128, 8, 64])
                Va, Vab = mk("Va", [128, 8, 129], BF16)
                Gt, Gtb = mk("Gt", [128, 32])
                for i in range(NB_):
                    k.op(pool, lambda: nc.gpsimd.memset(Va[i][:, :, 128:129], 1.0), w=[Vab[i]])
                gs, gsb = mk("gs", [128, 6, 8])
                bT, bTb = mk("bT", [8, 128])
                arg, argb = mk("arg", [128, 8, 128])
                eb, ebb = mk("eb", [64, 8, 128])
                QTs, QTsb = mk("QTs", [64, 8, 128], BF16)
                ST, STb = mk("ST", [128, 8, 128], BF16)
                Kw, Kwb = mk("Kw", [128, 8, 64], BF16)
                hd, hdb = mk("hd", [128, 8, 128])
                dn, dnb = mk("dn", [128, 2, 8])
                hfl, hflb = mk("hfl", [128, D])
                sot, sotb = mk("sot", [128, D])
                sq2, sq2b = mk("sq2", [128, 8, 128])
                sm2, sm2b = mk("sm2", [128, 3, 8])
                hg, hgb = mk("hg", [128, D], BF16)
                psm = PP(es, "s_psm", [128, 512]); psmb = k.buf("s_psm")
                pbc = [PP(es, "s_pbc%d" % i, [128, 4, 128]) for i in range(2)]
                pbcb = [k.buf("s_pbc%d" % i) for i in range(2)]
                pss = [PP(es, "s_pss%d" % i, [128, 4, 128]) for i in range(2)]
                pssb = [k.buf("s_pss%d" % i) for i in range(2)]
                hsplit = [(0, 3), (3, 6), (6, 8)]
                pov = [PP(es, "s_po%d" % i, [128, 3, 129]) for i in range(3)]
                pob = [k.buf("s_po%d" % i) for i in range(3)]
                order = list(range(NT)) if fwd else list(range(NT - 1, -1, -1))
                gb = 0 if fwd else 16
                for n_, c in enumerate(order):
                    i = n_ % 2
                    tok0 = c * 128
                    k.dma([(QT[i][:], MQK[0:4, :, tok0:tok0 + 128].rearrange("j (hh d) t -> d (j hh) t", hh=2))], w=[QTb[i]])
                    k.dma([(KTt[i][:], MQK[4:8, :, tok0:tok0 + 128].rearrange("j (hh d) t -> d (j hh) t", hh=2))], w=[KTb[i]])
                    k.dma([(Kt[i][:].rearrange("p h d -> p (h d)"), Ktm[tok0:tok0 + 128, :])], w=[Ktb[i]])
                    k.dma([(Va[i][:, :, 0:128], Vm[tok0:tok0 + 128, :].rearrange("p (h d) -> p h d", h=8))], w=[Vab[i]])
                    k.dma([(Gt[i][:], GTd[tok0:tok0 + 128, :])], w=[Gtb[i]])
                    if not fwd:
                        k.dma([(hfl[i][:], Hf[tok0:tok0 + 128, :])], w=[hflb[i]])
                        k.dma([(sot[i][:], SOd[tok0:tok0 + 128, :])], w=[sotb[i]])
                    ig = Gt[i][:, gb:gb + 8]
                    lf = Gt[i][:, gb + 8:gb + 16]
                    S_ = gs[i]
                    k.op(pe, lambda: nc.tensor.matmul(psm[:, 0:8], TRI[:], lf, start=True, stop=True), r=[Gtb[i], cb], w=[psmb])
                    k.op(pe, lambda: nc.tensor.matmul(psm[:, 8:16], ONESf[:], lf, start=True, stop=True), r=[Gtb[i], cb], w=[psmb])
                    k.op(pe, lambda: nc.tensor.matmul(psm[0:8, 128:256], lf, TRI[:], start=True, stop=True), r=[Gtb[i], cb], w=[psmb])
                    k.op(dve, lambda: nc.vector.tensor_copy(out=S_[:, 0, :], in_=psm[:, 0:8]), r=[psmb], w=[gsb[i]])
                    k.op(dve, lambda: nc.vector.tensor_copy(out=S_[:, 4, :], in_=psm[:, 8:16]), r=[psmb], w=[gsb[i]])
                    k.op(dve, lambda: nc.vector.tensor_copy(out=bT[i][:], in_=psm[0:8, 128:256]), r=[psmb], w=[bTb[i]])
                    k.op(dve, lambda: nc.vector.tensor_tensor(out=S_[:, 1, :], in0=ig, in1=S_[:, 0, :], op=ALU.subtract), r=[Gtb[i]], w=[gsb[i]])
                    k.op(dve, lambda: nc.vector.tensor_tensor(out=S_[:, 2, :], in0=S_[:, 1, :], in1=S_[:, 4, :], op=ALU.add), w=[gsb[i]])
                    k.op(act, lambda: nc.scalar.activation(out=S_[:, 2, :], in_=S_[:, 2, :], func=AF.Exp), w=[gsb[i]])
                    k.op(act, lambda: nc.scalar.activation(out=S_[:, 3, :], in_=S_[:, 4, :], func=AF.Exp), w=[gsb[i]])
                    for h in range(8):
                        k.op(pe, lambda: nc.tensor.matmul(pbc[h // 4][:, h % 4, :], Sel[0:8, h, :], bT[i][:], start=True, stop=True),
                             r=[bTb[i], cb], w=[pbcb[h // 4]])
                    for hh in range(2):
                        for h4 in range(4):
                            h = 4 * hh + h4
                            k.op(dve, lambda: nc.vector.scalar_tensor_tensor(out=arg[i][:, h, :], in0=pbc[hh][:, h4, :], scalar=S_[:, 1, h:h + 1],
                                                                             in1=CAP[:], op0=ALU.add, op1=ALU.min),
                                 r=[pbcb[hh], gsb[i], cb], w=[argb[i]])
                        k.op(dve, lambda: nc.vector.tensor_copy(out=eb[i][:, 4 * hh:4 * hh + 4, :], in_=pbc[hh][0:64]),
                             r=[pbcb[hh]], w=[ebb[i]])
                        k.op(act, lambda: nc.scalar.activation(out=eb[i][:, 4 * hh:4 * hh + 4, :], in_=eb[i][:, 4 * hh:4 * hh + 4, :], func=AF.Exp),
                             w=[ebb[i]])
                    k.op(act, lambda: nc.scalar.activation(out=arg[i][:], in_=arg[i][:], func=AF.Exp), w=[argb[i]])
                    k.op(dve, lambda: nc.vector.tensor_tensor(out=QTs[i][:], in0=QT[i][:], in1=eb[i][:], op=ALU.mult), r=[QTb[i], ebb[i]], w=[QTsb[i]])
                    for h in range(8):
                        k.op(pe, lambda: nc.tensor.matmul(pss[h // 4][:, h % 4, :], KTt[i][:, h, :], QT[i][:, h, :], start=True, stop=True),
                             r=[KTb[i], QTb[i]], w=[pssb[h // 4]])
                    for hh in range(2):
                        k.op(dve, lambda: nc.vector.tensor_tensor(out=ST[i][:, 4 * hh:4 * hh + 4, :], in0=pss[hh][:], in1=arg[i][:, 4 * hh:4 * hh + 4, :],
                                                                  op=ALU.mult), r=[pssb[hh], argb[i]], w=[STb[i]])
                    k.op(dve, lambda: nc.vector.tensor_tensor(out=Kw[i][:], in0=Kt[i][:], in1=bc(S_[:, 2, :].unsqueeze(2), [128, 8, 64]), op=ALU.mult),
                         r=[Ktb[i], gsb[i]], w=[Kwb[i]])
                    for h in range(8):
                        pi_ = 0 if h < 3 else (1 if h < 6 else 2)
                        hl = h - hsplit[pi_][0]
                        k.op(pe, lambda: nc.tensor.matmul(pov[pi_][:, hl, :], ST[i][:, h, :], Va[i][:, h, :], start=True, stop=False),
                             r=[STb[i], Vab[i]], w=[pob[pi_]])
                        k.op(pe, lambda: nc.tensor.matmul(pov[pi_][:, hl, :], QTs[i][:, h, :], Cbf[:, h, :], start=False, stop=True),
                             r=[QTsb[i], Cbfb], w=[pob[pi_]])
                    for pi_, (h0, h1) in enumerate(hsplit):
                        nh = h1 - h0
                        k.op(dve, lambda: nc.vector.tensor_copy(out=dn[i][:, 1, h0:h1], in_=pov[pi_][:, 0:nh, 128]), r=[pob[pi_]], w=[dnb[i]])
                    k.op(dve, lambda: nc.vector.tensor_tensor(out=dn[i][:, 0, :], in0=dn[i][:, 1, :], in1=dn[i][:, 1, :], op=ALU.mult), w=[dnb[i]])
                    k.op(dve, lambda: nc.vector.tensor_scalar(out=dn[i][:, 0, :], in0=dn[i][:, 0, :], scalar1=1.0, scalar2=None, op0=ALU.max), w=[dnb[i]])
                    k.op(pool, lambda: nc.gpsimd.tensor_tensor(out=dn[i][:, 1, :], in0=dn[i][:, 0, :], in1=nhalf[:, 0:8], op=ALU.pow), w=[dnb[i]])
                    for pi_, (h0, h1) in enumerate(hsplit):
                        nh = h1 - h0
                        k.op(dve, lambda: nc.vector.tensor_tensor(out=hd[i][:, h0:h1, :], in0=pov[pi_][:, 0:nh, 0:128],
                                                                  in1=bc(dn[i][:, 1, h0:h1].unsqueeze(2), [128, nh, 128]), op=ALU.mult),
                             r=[pob[pi_], dnb[i]], w=[hdb[i]])
                    hdf = hd[i][:].rearrange("p h d -> p (h d)")
                    if fwd:
                        k.dma([(Hf[tok0:tok0 + 128, :], hdf)], r=[hdb[i]])
                    else:
                        k.op(dve, lambda: nc.vector.tensor_tensor(out=hdf, in0=hdf, in1=hfl[i][:], op=ALU.add), r=[hflb[i]], w=[hdb[i]])
                        k.op(act, lambda: nc.scalar.activation(out=sq2[i][:], in_=hd[i][:], func=AF.Square), r=[hdb[i]], w=[sq2b[i]])
                        k.op(dve, lambda: nc.vector.tensor_reduce(out=sm2[i][:, 0, :], in_=sq2[i][:], axis=AX.X, op=ALU.add), r=[sq2b[i]], w=[sm2b[i]])
                        k.op(dve, lambda: nc.vector.tensor_scalar(out=sm2[i][:, 1, :], in0=sm2[i][:, 0, :], scalar1=1.0 / 128, scalar2=EPS,
                                                                  op0=ALU.mult, op1=ALU.add), w=[sm2b[i]])
                        k.op(pool, lambda: nc.gpsimd.tensor_tensor(out=sm2[i][:, 2, :], in0=sm2[i][:, 1, :], in1=nhalf[:, 0:8], op=ALU.pow), w=[sm2b[i]])
                        k.op(dve, lambda: nc.vector.tensor_tensor(out=hd[i][:], in0=hd[i][:], in1=bc(sm2[i][:, 2, :].unsqueeze(2), [128, 8, 128]),
                                                                  op=ALU.mult), r=[sm2b[i]], w=[hdb[i]])
                        k.op(dve, lambda: nc.vector.tensor_tensor(out=hdf, in0=hdf, in1=nwr[:], op=ALU.mult), r=[nwrb], w=[hdb[i]])
                        k.op(dve, lambda: nc.vector.tensor_tensor(out=hg[i][:], in0=hdf, in1=sot[i][:], op=ALU.mult), r=[hdb[i], sotb[i]], w=[hgb[i]])
                        k.dma([(HGd[tok0:tok0 + 128, :], hg[i][:])], r=[hgb[i]])
                    for h in range(8):
                        pi_ = 0 if h < 3 else (1 if h < 6 else 2)
                        hl = h - hsplit[pi_][0]
                        k.op(pe, lambda: nc.tensor.matmul(pov[pi_][0:64, hl, :], Kw[i][:, h, :], Va[i][:, h, :], start=True, stop=True),
                             r=[Kwb[i], Vab[i]], w=[pob[pi_]])
                    k.op(dve, lambda: nc.vector.tensor_tensor(out=C[:], in0=C[:], in1=bc(S_[0:64, 3, :].unsqueeze(2), [64, 8, 129]), op=ALU.mult),
                         r=[gsb[i]], w=[Cb_])
                    for pi_, (h0, h1) in enumerate(hsplit):
                        nh = h1 - h0
                        k.op(dve, lambda: nc.vector.tensor_tensor(out=C[:, h0:h1, :], in0=C[:, h0:h1, :], in1=pov[pi_][0:64, 0:nh, :], op=ALU.add),
                             r=[pob[pi_]], w=[Cb_])
                    k.op(dve, lambda: nc.vector.tensor_scalar(out=C[:], in0=C[:], scalar1=keep[0:64, c:c + 1], scalar2=None, op0=ALU.mult),
                         r=[keepb_], w=[Cb_])
                    k.op(act, lambda: nc.scalar.copy(out=Cbf[:], in_=C[:]), r=[Cb_], w=[Cbfb])
            return phase_done()

        def phase_ml_out(l, j, dst):
            src = state["src"]
            es = contextlib.ExitStack()
            with es:
                Wo = TT(es, "mo_Wo", [128, KC, D], BF16); Wob = k.buf("mo_Wo")
                es2 = contextlib.ExitStack()
                with es2:
                    pieces = [(Wo[:, kc, :], ml_w_out[j, kc * 128:(kc + 1) * 128, :]) for kc in range(KC)]
                    load_weight(es2, None, Wob, pieces)
                    k.barrier()
                mods = ModTiles(es, l, [5], "mo")
                ep = Epilogue(es)
                hgt = [TT(es, "mo_hg%d" % i, [128, D], BF16) for i in range(2)]
                hgtb = [k.buf("mo_hg%d" % i) for i in range(2)]
                hT = [TT(es, "mo_hT%d" % i, [128, KC, 128], BF16) for i in range(2)]
                hTb = [k.buf("mo_hT%d" % i) for i in range(2)]
                pT = [PP(es, "mo_pT%d" % i, [128, 8, 128], BF16) for i in range(2)]
                pTb = [k.buf("mo_pT%d" % i) for i in range(2)]
                py = [[PP(es, "mo_py%d_%d" % (i, h), [128, 512]) for h in range(2)] for i in range(2)]
                pyb = [[k.buf("mo_py%d_%d" % (i, h)) for h in range(2)] for i in range(2)]
                for t in range(NT):
                    tok0 = t * 128
                    i = t % 2
                    mods.need(tok0 // SLOT)
                    k.dma([(hgt[i][:], HGd[tok0:tok0 + 128, :])], w=[hgtb[i]])
                    xi = ep.load(src, tok0)
                    for kc in range(KC):
                        k.op(pe, lambda: nc.tensor.transpose(pT[i][:, kc, :], hgt[i][:, kc * 128:(kc + 1) * 128], identb[:]), r=[hgtb[i]], w=[pTb[i]])
                    k.op(act, lambda: nc.scalar.copy(out=hT[i][:], in_=pT[i][:]), r=[pTb[i]], w=[hTb[i]])
                    for h in range(2):
                        for kc in range(KC):
                            k.op(pe, lambda: nc.tensor.matmul(py[i][h][:], hT[i][:, kc, :], Wo[:, kc, h * 512:(h + 1) * 512],
                                                              start=(kc == 0), stop=(kc == KC - 1)), r=[hTb[i], Wob], w=[pyb[i][h]])
                    ep.apply(xi, py[i], pyb[i], mods.t[5], mods.b[5], dst, tok0)
            state["src"] = dst
            return phase_done()

        def program():
            if phase_mod():
                return
            for l in range(NL):
                last = (l == NL - 1)
                if phase_ffn(l, 0, xs):
                    return
                kind, j = l % 3, l // 3
                if kind == 0:
                    if phase_ml_in(l, j): return
                    if phase_ml_scan(l, j, 0): return
                    if phase_ml_scan(l, j, 1): return
                    if phase_ml_out(l, j, xs): return
                else:
                    if phase_att_in(l, kind - 1): return
                    if phase_att(l, kind - 1): return
                    if phase_att_out(l, kind - 1, xs): return
                if phase_ffn(l, 1, y_out if last else xs):
                    return
        program()
        if state["src"] is not y_out:
            es = contextlib.ExitStack()
            with es:
                tb = [TT(es, "cp%d" % i, [128, D]) for i in range(2)]
                tbb = [k.buf("cp%d" % i) for i in range(2)]
                for t in range(NT):
                    k.dma([(tb[t % 2][:], state["src"][t * 128:(t + 1) * 128, :])], w=[tbb[t % 2]])
                    k.dma([(y_out[t * 128:(t + 1) * 128, :], tb[t % 2][:])], r=[tbb[t % 2]])
            k.barrier()
    return nc


def rope_np(pos, dim):
    inv = (np.float32(10000.0) ** (-np.arange(0, dim, 2, dtype=np.float32) / np.float32(dim))).astype(np.float32)
    ang = pos.astype(np.float32)[:, None] * inv[None, :]
    ang = np.concatenate([ang, ang], axis=-1)
    return np.cos(ang).astype(np.float32), np.sin(ang).astype(np.float32)


def core_tables(T, S):
    NT = T // 128
    bps = S // 128
    pos = np.arange(T) % S
    c, s = rope_np(pos, 64)
    s_sw = np.concatenate([-s[:, :32], s[:, 32:]], axis=1)
    rc, rs = rope_np(pos // 64, 32)
    cc, cs_ = rope_np(pos % 64, 32)
    c_ax = np.concatenate([rc, cc], axis=1)
    s_ax = np.concatenate([-rs[:, :16], rs[:, 16:], -cs_[:, :16], cs_[:, 16:]], axis=1)
    blk = np.arange(NT)
    keepf = ((blk + 1) % bps != 0).astype(np.float32)
    keepb = (blk % bps != 0).astype(np.float32)
    swab = np.zeros((NT, 2), np.float32)
    swab[blk % bps == 0, 0] = NEG
    swab[(blk + 1) % bps == 0, 1] = NEG
    slot_seq = (np.arange(8) * (T // 8)) // S
    amask = np.where(slot_seq[:, None] == slot_seq[None, :], 0.0, NEG).astype(np.float32)
    rep = lambda a: np.ascontiguousarray(np.broadcast_to(a.reshape(1, -1), (128, a.size))).astype(np.float32)
    return {
        "rope_swa_c": c, "rope_swa_s": s_sw.astype(np.float32), "rope_ax_c": c_ax.astype(np.float32), "rope_ax_s": s_ax.astype(np.float32),
        "keepf": rep(keepf), "keepb": rep(keepb), "swab": rep(swab), "amask": rep(amask),
    }


WNAMES = ["ffn_w13", "ffn_w2", "ada_w", "ada_b", "norm_w", "mlstm_w_in", "mlstm_b_gate", "mlstm_norm_w", "mlstm_w_out",
          "swa_w_in", "swa_q_norm", "swa_k_norm", "swa_sink", "swa_w_out", "axial_w_in", "axial_q_norm", "axial_k_norm", "axial_w_out"]


def run_streams(streams, weights, T, NL=4, stop=None, ncores=8):
    nc = build(T, NL, stop)
    w = {n: np.ascontiguousarray(np.asarray(weights[n], dtype=np.float32)) for n in WNAMES}
    in_maps = []
    for c in range(ncores):
        x, c8, S = streams[c] if c < len(streams) else streams[-1]
        m = {"x": np.ascontiguousarray(x, dtype=np.float32), "c8": np.ascontiguousarray(c8, dtype=np.float32)}
        m.update(core_tables(T, S))
        m.update(w)
        in_maps.append(m)
    if os.environ.get("KTRACE"):
        res = run_bass_kernel_spmd(nc, in_maps, core_ids=list(range(ncores)), trace=True)
        print("EXEC_TIME_NS", res.exec_time_ns)
    else:
        res = run_bass_kernel_spmd(nc, in_maps, core_ids=list(range(ncores)))
    return [res.results[c]["y"] for c in range(len(streams))]


def kernel(x_prompt, x_sample, c_prompt, c_sample, **weights):
    x_prompt = np.asarray(x_prompt, dtype=np.float32)
    x_sample = np.asarray(x_sample, dtype=np.float32)
    c_prompt = np.asarray(c_prompt, dtype=np.float32)
    c_sample = np.asarray(c_sample, dtype=np.float32)
    T = 16384
    streams = []
    for b in range(2):
        streams.append((x_prompt[b], np.ascontiguousarray(np.broadcast_to(c_prompt[b:b + 1], (8, D))), 16384))
    for j in range(4):
        streams.append((x_sample[8 * j:8 * j + 8].reshape(T, D), c_sample[8 * j:8 * j + 8], 2048))
    ys = run_streams(streams, weights, T)
    y_prompt = np.stack([ys[0], ys[1]], axis=0).astype(np.float32)
    y_sample = np.concatenate([ys[2 + j].reshape(8, 2048, D) for j in range(4)], axis=0).astype(np.float32)
    return (y_prompt, y_sample)
```

```python
import bisect
import os
import contextlib
import numpy as np
import concourse.bass as bass
import concourse.mybir as mybir
from concourse.bass_utils import run_bass_kernel_spmd

F32 = mybir.dt.float32
BF16 = mybir.dt.bfloat16
AF = mybir.ActivationFunctionType
ALU = mybir.AluOpType
AX = mybir.AxisListType

D = 1024
DFF = 2816
NFC = 22
KC = 8
EPS = 1e-6
NEG = -30000.0
NDS = 56


class Eng:
    def __init__(s, name, h, sem):
        s.name, s.h, s.sem = name, h, sem
        s.cnt = 0
        s.idx = 0
        s.last = None
        s.sig_idx = []
        s.sig_cnt = []
        s.seen = {}


class DSem:
    def __init__(s, h, key):
        s.h, s.key, s.count = h, key, 0


class Buf:
    def __init__(s, name):
        s.name = name
        s.w = None
        s.r = {}
        s.dsem = None


class K:
    def __init__(s, nc, es):
        s.nc = nc
        mk = lambda n: es.enter_context(nc.semaphore(n))
        s.pe = Eng("pe", nc.tensor, mk("s_pe"))
        s.act = Eng("act", nc.scalar, mk("s_act"))
        s.dve = Eng("dve", nc.vector, mk("s_dve"))
        s.pool = Eng("pool", nc.gpsimd, mk("s_pool"))
        s.sp = Eng("sp", nc.sync, mk("s_sp"))
        s.engs = [s.pe, s.act, s.dve, s.pool, s.sp]
        s.dsems = [DSem(mk("d%d" % i), "d%d" % i) for i in range(NDS)]
        s.free = list(s.dsems)
        s.pbufs = []
        s.used = []
        s.pe_eager = True

    def buf(s, name):
        b = Buf(name)
        s.pbufs.append(b)
        return b

    def _wait(s, eng, ev):
        if ev[0] == "e":
            e2, idx = ev[1], ev[2]
            if e2 is eng and eng is s.pe:
                return
            i = bisect.bisect_left(e2.sig_idx, idx)
            if i < len(e2.sig_idx):
                c = e2.sig_cnt[i]
            else:
                assert e2.idx >= idx and e2.last is not None
                e2.last.then_inc(e2.sem, 1)
                e2.cnt += 1
                e2.sig_idx.append(e2.idx)
                e2.sig_cnt.append(e2.cnt)
                c = e2.cnt
            if eng.seen.get(e2.name, 0) >= c:
                return
            eng.h.wait_ge(e2.sem, c)
            eng.seen[e2.name] = c
        else:
            ds, val = ev[1], ev[2]
            if eng.seen.get(ds.key, 0) >= val:
                return
            eng.h.wait_ge(ds.h, val)
            eng.seen[ds.key] = val

    def _deps(s, eng, r, w):
        for b in r:
            if b.w is not None:
                s._wait(eng, b.w)
        for b in w:
            if b.w is not None:
                s._wait(eng, b.w)
            for ev in b.r.values():
                s._wait(eng, ev)

    def op(s, eng, fn, r=(), w=(), sig=None):
        s._deps(eng, r, w)
        ins = fn()
        eng.idx += 1
        eng.last = ins
        if sig is None:
            sig = (eng is not s.pe) or s.pe_eager
        if sig:
            ins.then_inc(eng.sem, 1)
            eng.cnt += 1
            eng.sig_idx.append(eng.idx)
            eng.sig_cnt.append(eng.cnt)
        ev = ("e", eng, eng.idx)
        for b in r:
            b.r[eng.name] = ev
        for b in w:
            b.w = ev
            b.r = {}
        return ins

    def dma(s, pairs, r=(), w=(), q=None):
        q = q or s.sp
        s._deps(q, r, w)
        owner = w[0] if len(w) else r[0]
        if owner.dsem is None:
            owner.dsem = s.free.pop()
            s.used.append(owner.dsem)
        ds = owner.dsem
        for (o, i) in pairs:
            q.h.dma_start(out=o, in_=i).then_inc(ds.h, 16)
            ds.count += 16
        ev = ("d", ds, ds.count)
        for b in r:
            b.r["dma_" + ds.key] = ev
        for b in w:
            b.w = ev
            b.r = {}

    def barrier(s):
        for e in s.engs:
            for e2 in s.engs:
                if e2 is not e and e2.idx > 0 and e2 is not s.sp:
                    s._wait(e, ("e", e2, e2.idx))
            for ds in s.used:
                if ds.count > 0:
                    s._wait(e, ("d", ds, ds.count))
        s.free = list(s.dsems)
        s.used = []
        s.pbufs = []


def bc(ap, shape):
    return ap.to_broadcast(list(shape))


def build(T, NL=4, stop=None):
    NT = T // 128
    SLOT = T // 8
    BPG = SLOT // 128
    GT = 256
    NG = T // GT
    TPG = GT // 128
    nc = bass.Bass("TRN2", target_bir_lowering=False)
    dt_in = lambda name, shape, dt=F32: nc.dram_tensor(name, list(shape), dt, kind="ExternalInput").ap()
    dt_sc = lambda name, shape, dt=F32: nc.dram_tensor(name, list(shape), dt, kind="Internal").ap()
    x_in = dt_in("x", [T, D])
    c8 = dt_in("c8", [8, D])
    ffn_w13 = dt_in("ffn_w13", [4, 2, D, 2 * DFF])
    ffn_w2 = dt_in("ffn_w2", [4, 2, DFF, D])
    ada_w = dt_in("ada_w", [4, D, 9 * D])
    ada_b = dt_in("ada_b", [4, 9 * D])
    norm_w = dt_in("norm_w", [4, 3, D])
    ml_w_in = dt_in("mlstm_w_in", [2, D, 3104])
    ml_bg = dt_in("mlstm_b_gate", [2, 32])
    ml_nw = dt_in("mlstm_norm_w", [2, D])
    ml_w_out = dt_in("mlstm_w_out", [2, D, D])
    at_w_in = [dt_in("swa_w_in", [1, D, 1536]), dt_in("axial_w_in", [1, D, 1536])]
    at_qn = [dt_in("swa_q_norm", [1, 64]), dt_in("axial_q_norm", [1, 64])]
    at_kn = [dt_in("swa_k_norm", [1, 64]), dt_in("axial_k_norm", [1, 64])]
    swa_sink = dt_in("swa_sink", [1, 16])
    at_w_out = [dt_in("swa_w_out", [1, D, D]), dt_in("axial_w_out", [1, D, D])]
    ropec = [dt_in("rope_swa_c", [T, 64]), dt_in("rope_ax_c", [T, 64])]
    ropes = [dt_in("rope_swa_s", [T, 64]), dt_in("rope_ax_s", [T, 64])]
    keepf_d = dt_in("keepf", [128, NT])
    keepb_d = dt_in("keepb", [128, NT])
    swab_d = dt_in("swab", [128, NT * 2])
    amask_d = dt_in("amask", [128, 64])
    y_out = nc.dram_tensor("y", [T, D], F32, kind="ExternalOutput").ap()
    xs = dt_sc("xs", [T, D])
    MR = dt_sc("MR", [NL, 9, 8, 128, D])
    QTK = dt_sc("QTK", [10, 128, T], BF16)
    Vd = dt_sc("Vd", [4, 128, NT, 64], BF16)
    OTd = dt_sc("OTd", [16, 64, T], BF16)
    MQK = dt_sc("MQK", [8, 128, T], BF16)
    Ktm = dt_sc("Ktm", [T, 512])
    Vm = dt_sc("Vm", [T, D], BF16)
    SOd = dt_sc("SOd", [T, D])
    GTd = dt_sc("GTd", [T, 32])
    Hf = dt_sc("Hf", [T, D])
    HGd = dt_sc("HGd", [T, D], BF16)

    top = contextlib.ExitStack()
    with top:
        k = K(nc, top)
        pe, act, dve, pool = k.pe, k.act, k.dve, k.pool
        uid = [0]

        def TT(es, name, shape, dt=F32):
            uid[0] += 1
            return es.enter_context(nc.sbuf_tensor("%s_u%d" % (name, uid[0]), list(shape), dt))

        def PP(es, name, shape, dt=F32):
            uid[0] += 1
            return es.enter_context(nc.psum_tensor("%s_u%d" % (name, uid[0]), list(shape), dt))

        identb = TT(top, "identb", [128, 128], BF16)
        identf = TT(top, "identf", [128, 128])
        TRIi = TT(top, "TRIi", [128, 128])
        TRIr = TT(top, "TRIr", [128, 128])
        ONESf = TT(top, "ONESf", [128, 128])
        Sel = TT(top, "Sel", [8, 8, 128])
        E65 = TT(top, "E65", [65, 64])
        nhalf = TT(top, "nhalf", [128, 32])
        cb = k.buf("consts")

        def mkmask(t, pattern, cmul, cmp, base=0):
            k.op(pool, lambda: nc.gpsimd.memset(t, 1.0), w=[cb])
            k.op(pool, lambda: nc.gpsimd.affine_select(out=t, in_=t, pattern=pattern, compare_op=cmp, fill=0.0,
                                                       base=base, channel_multiplier=cmul), w=[cb])
        mkmask(identf[:], [[-1, 128]], 1, ALU.is_equal)
        mkmask(TRIi[:], [[1, 128]], -1, ALU.is_ge)
        mkmask(TRIr[:], [[-1, 128]], 1, ALU.is_ge)
        mkmask(Sel[:], [[-1, 8], [0, 128]], 1, ALU.is_equal)
        mkmask(E65[:], [[0, 64]], 1, ALU.is_equal, base=-64)
        CAPi = TT(top, "CAPi", [128, 128])
        CAPr = TT(top, "CAPr", [128, 128])
        k.op(pool, lambda: nc.gpsimd.memset(ONESf[:], 1.0), w=[cb])
        k.op(pool, lambda: nc.gpsimd.memset(nhalf[:], -0.5), w=[cb])
        k.op(dve, lambda: nc.vector.tensor_copy(out=identb[:], in_=identf[:]), r=[cb], w=[cb])
        k.op(dve, lambda: nc.vector.tensor_scalar(out=CAPi[:], in0=TRIi[:], scalar1=10016.0, scalar2=-10000.0, op0=ALU.mult, op1=ALU.add), w=[cb])
        k.op(dve, lambda: nc.vector.tensor_scalar(out=CAPr[:], in0=TRIr[:], scalar1=10016.0, scalar2=-10000.0, op0=ALU.mult, op1=ALU.add), w=[cb])
        k.barrier()

        state = {"src": x_in, "nph": 0}

        def phase_done():
            k.barrier()
            state["nph"] += 1
            return stop is not None and state["nph"] >= stop

        def load_weight(es, dst, dstbuf, pieces):
            nmax = max(p[1].shape[-1] for p in pieces)
            stg = [TT(es, "wstg%d" % i, [128, nmax]) for i in range(3)]
            sb = [k.buf("wstg%d" % i) for i in range(3)]
            cv = [dve, pool, act]
            for i, (d_ap, s_ap) in enumerate(pieces):
                P, n = s_ap.shape[0], s_ap.shape[-1]
                j = i % 3
                k.dma([(stg[j][0:P, 0:n], s_ap)], w=[sb[j]])
                e = cv[i % 3]
                if e is act:
                    k.op(act, lambda: nc.scalar.copy(out=d_ap, in_=stg[j][0:P, 0:n]), r=[sb[j]], w=[dstbuf])
                else:
                    k.op(e, lambda: e.h.tensor_copy(out=d_ap, in_=stg[j][0:P, 0:n]), r=[sb[j]], w=[dstbuf])

        class NormCtx:
            def __init__(s, es, nb=2):
                s.hb = [TT(es, "n_hb%d" % i, [128, D], BF16) for i in range(nb)]
                s.hbb = [k.buf("n_hb%d" % i) for i in range(nb)]
                s.sm = [TT(es, "n_sm%d" % i, [128, 4]) for i in range(nb)]
                s.smb = [k.buf("n_sm%d" % i) for i in range(nb)]
                s.pT = [PP(es, "n_pT%d" % i, [128, 8, 128], BF16) for i in range(2)]
                s.pTb = [k.buf("n_pT%d" % i) for i in range(2)]
                s.n = 0

            def part1(s, xa, xab, A, Ab, B, Bb):
                i = s.n % len(s.hb)
                s.cur = i
                hb, hbb, sm, smb = s.hb[i], s.hbb[i], s.sm[i], s.smb[i]
                k.op(act, lambda: nc.scalar.activation(out=hb[:], in_=xa, func=AF.Square, accum_out=sm[:, 0:1]),
                     r=[xab], w=[hbb, smb])
                k.op(dve, lambda: nc.vector.tensor_scalar(out=sm[:, 1:2], in0=sm[:, 0:1], scalar1=1.0 / D, scalar2=EPS,
                                                          op0=ALU.mult, op1=ALU.add), r=[smb], w=[smb])
                k.op(pool, lambda: nc.gpsimd.tensor_tensor(out=sm[:, 2:3], in0=sm[:, 1:2], in1=nhalf[:, 0:1], op=ALU.pow),
                     r=[smb], w=[smb])
                k.op(dve, lambda: nc.vector.scalar_tensor_tensor(out=xa, in0=xa, scalar=sm[:, 2:3], in1=A,
                                                                 op0=ALU.mult, op1=ALU.mult), r=[smb, Ab, xab], w=[xab])
                k.op(pool, lambda: nc.gpsimd.tensor_tensor(out=hb[:], in0=xa, in1=B, op=ALU.add), r=[xab, Bb], w=[hbb])

            def part2(s, hT, hTb):
                i = s.cur
                j = s.n % 2
                s.n += 1
                for kc in range(KC):
                    k.op(pe, lambda: nc.tensor.transpose(s.pT[j][:, kc, :], s.hb[i][:, kc * 128:(kc + 1) * 128], identb[:]),
                         r=[s.hbb[i]], w=[s.pTb[j]])
                k.op(act, lambda: nc.scalar.copy(out=hT, in_=s.pT[j][:]), r=[s.pTb[j]], w=[hTb])

        class ModTiles:
            def __init__(s, es, l, ms, pfx):
                s.l, s.ms = l, ms
                s.t = {m: TT(es, "%s_mod%d" % (pfx, m), [128, D]) for m in ms}
                s.b = {m: k.buf("%s_mod%d" % (pfx, m)) for m in ms}
                s.slot = -1

            def need(s, slot):
                if slot != s.slot:
                    s.slot = slot
                    for m in s.ms:
                        k.dma([(s.t[m][:], MR[s.l, m, slot])], w=[s.b[m]])

        class Epilogue:
            def __init__(s, es, nb=2):
                s.xb = [TT(es, "e_xb%d" % i, [128, D]) for i in range(nb)]
                s.xbb = [k.buf("e_xb%d" % i) for i in range(nb)]
                s.n = 0

            def load(s, src, tok0):
                i = s.n % len(s.xb)
                k.dma([(s.xb[i][:], src[tok0:tok0 + 128, :])], w=[s.xbb[i]])
                return i

            def apply(s, i, py, pyb, G, Gb, dst, tok0):
                for h in range(2):
                    k.op(dve, lambda: nc.vector.tensor_tensor(out=py[h][:], in0=py[h][:], in1=G[:, h * 512:(h + 1) * 512],
                                                              op=ALU.mult), r=[Gb], w=[pyb[h]])
                    k.op(dve, lambda: nc.vector.tensor_tensor(out=s.xb[i][:, h * 512:(h + 1) * 512],
                                                              in0=s.xb[i][:, h * 512:(h + 1) * 512], in1=py[h][:], op=ALU.add),
                         r=[pyb[h]], w=[s.xbb[i]])
                k.dma([(dst[tok0:tok0 + 128, :], s.xb[i][:])], r=[s.xbb[i]])
                s.n += 1

        def phase_mod():
            es = contextlib.ExitStack()
            with es:
                c8t = TT(es, "c8t", [8, D]); c8b = k.buf("c8t")
                c8s = TT(es, "c8s", [8, D], BF16)
                csT = TT(es, "csT", [128, KC, 8], BF16); csTb = k.buf("csT")
                csrep = TT(es, "csrep", [128, 8, KC, 128], BF16); csrb = k.buf("csrep")
                pcs = PP(es, "pcs", [128, KC, 8], BF16); pcsb = k.buf("pcs")
                k.dma([(c8t[:], c8[:, :])], w=[c8b])
                k.op(act, lambda: nc.scalar.activation(out=c8s[:], in_=c8t[:], func=AF.Silu), r=[c8b], w=[c8b])
                for kc in range(KC):
                    k.op(pe, lambda: nc.tensor.transpose(pcs[:, kc, :], c8s[0:8, kc * 128:(kc + 1) * 128], identb[0:8, 0:8]),
                         r=[c8b], w=[pcsb])
                k.op(dve, lambda: nc.vector.tensor_copy(out=csT[:], in_=pcs[:]), r=[pcsb], w=[csTb])
                for sl in range(8):
                    k.op(dve, lambda: nc.vector.tensor_copy(out=csrep[:, sl], in_=bc(csT[:, :, sl:sl + 1], [128, KC, 128])),
                         r=[csTb], w=[csrb])
                stg = [TT(es, "m_stg%d" % i, [128, KC, 512]) for i in range(2)]
                stgb = [k.buf("m_stg%d" % i) for i in range(2)]
                wst = [TT(es, "m_wst%d" % i, [128, KC, 512], BF16) for i in range(2)]
                wstb = [k.buf("m_wst%d" % i) for i in range(2)]
                adb = [TT(es, "m_adb%d" % i, [128, 512]) for i in range(2)]
                adbb = [k.buf("m_adb%d" % i) for i in range(2)]
                nwr = TT(es, "m_nwr", [128, 3, D]); nwrb = k.buf("m_nwr")
                mo = [TT(es, "m_mo%d" % i, [128, 512]) for i in range(3)]
                mob = [k.buf("m_mo%d" % i) for i in range(3)]
                pm = [PP(es, "m_pm%d" % i, [128, 512]) for i in range(3)]
                pmb = [k.buf("m_pm%d" % i) for i in range(3)]
                n = 0
                cgi = 0
                for l in range(NL):
                    k.dma([(nwr[:, j, :], norm_w[l, j:j + 1, :].partition_broadcast(128)) for j in range(3)], w=[nwrb])
                    for m in range(9):
                        for half in range(2):
                            c0 = m * D + half * 512
                            j = cgi % 2
                            cgi += 1
                            k.dma([(stg[j][:], ada_w[l, :, c0:c0 + 512].rearrange("(kc p) n -> p kc n", p=128))], w=[stgb[j]])
                            k.dma([(adb[j][:], ada_b[l:l + 1, c0:c0 + 512].partition_broadcast(128))], w=[adbb[j]])
                            k.op(pool, lambda: nc.gpsimd.tensor_copy(out=wst[j][:], in_=stg[j][:]), r=[stgb[j]], w=[wstb[j]])
                            for sl in range(8):
                                q = n % 3
                                n += 1
                                for kc in range(KC):
                                    k.op(pe, lambda: nc.tensor.matmul(pm[q][:], csrep[:, sl, kc, :], wst[j][:, kc, :],
                                                                      start=(kc == 0), stop=(kc == KC - 1)),
                                         r=[csrb, wstb[j]], w=[pmb[q]])
                                k.op(dve, lambda: nc.vector.tensor_tensor(out=mo[q][:], in0=pm[q][:], in1=adb[j][:], op=ALU.add),
                                     r=[pmb[q], adbb[j]], w=[mob[q]])
                                if m in (1, 4, 7):
                                    k.op(dve, lambda: nc.vector.scalar_tensor_tensor(
                                        out=mo[q][:], in0=mo[q][:], scalar=1.0, in1=nwr[:, m // 3, half * 512:(half + 1) * 512],
                                        op0=ALU.add, op1=ALU.mult), r=[nwrb], w=[mob[q]])
                                elif m in (2, 8):
                                    k.op(dve, lambda: nc.vector.tensor_scalar(out=mo[q][:], in0=mo[q][:], scalar1=0.5, scalar2=None,
                                                                              op0=ALU.mult), w=[mob[q]])
                                k.dma([(MR[l, m, sl, :, half * 512:(half + 1) * 512], mo[q][:])], r=[mob[q]])
            return phase_done()

        def phase_ffn(l, which, dst):
            src = state["src"]
            k.pe_eager = False
            mi = 0 if which == 0 else 6
            es = contextlib.ExitStack()
            with es:
                W13 = TT(es, "W13", [128, KC, 2 * DFF], BF16); W13b = k.buf("W13")
                W2 = TT(es, "W2", [128, NFC, D], BF16); W2b = k.buf("W2")
                es2 = contextlib.ExitStack()
                with es2:
                    pieces = []
                    for kc in range(KC):
                        for c in range(4):
                            pieces.append((W13[:, kc, c * 1408:(c + 1) * 1408],
                                           ffn_w13[l, which, kc * 128:(kc + 1) * 128, c * 1408:(c + 1) * 1408]))
                    for fc in range(NFC):
                        pieces.append((W2[:, fc, :], ffn_w2[l, which, fc * 128:(fc + 1) * 128, :]))
                    wb = k.buf("Wall")
                    load_weight(es2, None, wb, pieces)
                    k.barrier()
                xa = [TT(es, "f_xa%d" % i, [128, D]) for i in range(2)]
                xab = [k.buf("f_xa%d" % i) for i in range(2)]
                nctx = NormCtx(es)
                mods = ModTiles(es, l, [mi, mi + 1], "f")
                modg = ModTiles(es, l, [mi + 2], "fg")
                hT = [TT(es, "f_hT%d" % i, [128, KC, GT], BF16) for i in range(2)]
                hTb = [k.buf("f_hT%d" % i) for i in range(2)]
                sg = [TT(es, "f_sg%d" % i, [128, GT]) for i in range(2)]
                sgb = [k.buf("f_sg%d" % i) for i in range(2)]
                uT = TT(es, "f_uT", [128, NFC, GT], BF16); uTb = k.buf("f_uT")
                ep = Epilogue(es)
                pg = [PP(es, "f_pg%d" % i, [128, GT]) for i in range(2)]
                pgb = [k.buf("f_pg%d" % i) for i in range(2)]
                pu = [PP(es, "f_pu%d" % i, [128, GT]) for i in range(2)]
                pub = [k.buf("f_pu%d" % i) for i in range(2)]
                py = [PP(es, "f_py%d" % i, [128, 512]) for i in range(2)]
                pyb = [k.buf("f_py%d" % i) for i in range(2)]
                cnt = {"xa": 0}

                def norm1(g):
                    mods.need((g * GT) // SLOT)
                    pend = []
                    for tt in range(TPG):
                        i = cnt["xa"] % 2
                        cnt["xa"] += 1
                        tok0 = g * GT + tt * 128
                        k.dma([(xa[i][:], src[tok0:tok0 + 128, :])], w=[xab[i]])
                        nctx.part1(xa[i][:], xab[i], mods.t[mi + 1][:], mods.b[mi + 1], mods.t[mi][:], mods.b[mi])
                        nctx.part2(hT[g % 2][:, :, tt * 128:(tt + 1) * 128], hTb[g % 2])

                norm1(0)
                for g in range(NG):
                    h_ = hT[g % 2]
                    for fc in range(NFC):
                        q = fc % 2
                        for kc in range(KC):
                            k.op(pe, lambda: nc.tensor.matmul(pg[q][:], W13[:, kc, fc * 128:(fc + 1) * 128], h_[:, kc, :],
                                                              start=(kc == 0), stop=(kc == KC - 1)), r=[hTb[g % 2]], w=[pgb[q]], sig=(kc == KC - 1))
                        for kc in range(KC):
                            k.op(pe, lambda: nc.tensor.matmul(pu[q][:], W13[:, kc, DFF + fc * 128:DFF + (fc + 1) * 128], h_[:, kc, :],
                                                              start=(kc == 0), stop=(kc == KC - 1)), r=[hTb[g % 2]], w=[pub[q]], sig=(kc == KC - 1))
                        k.op(act, lambda: nc.scalar.activation(out=sg[q][:], in_=pg[q][:], func=AF.Silu), r=[pgb[q]], w=[sgb[q]])
                        k.op(dve, lambda: nc.vector.tensor_tensor(out=uT[:, fc, :], in0=sg[q][:], in1=pu[q][:], op=ALU.mult),
                             r=[sgb[q], pub[q]], w=[uTb])
                        if fc == 10 and g + 1 < NG:
                            norm1(g + 1)
                    modg.need((g * GT) // SLOT)
                    for tt in range(TPG):
                        tok0 = g * GT + tt * 128
                        xi = ep.load(src, tok0)
                        for h in range(2):
                            for fc in range(NFC):
                                k.op(pe, lambda: nc.tensor.matmul(py[h][:], uT[:, fc, tt * 128:(tt + 1) * 128],
                                                                  W2[:, fc, h * 512:(h + 1) * 512],
                                                                  start=(fc == 0), stop=(fc == NFC - 1)), r=[uTb], w=[pyb[h]], sig=(fc == NFC - 1))
                        ep.apply(xi, py, pyb, modg.t[mi + 2], modg.b[mi + 2], dst, tok0)
            state["src"] = dst
            k.pe_eager = True
            return phase_done()

        def phase_att_in(l, kind):
            src = state["src"]
            es = contextlib.ExitStack()
            with es:
                Win = TT(es, "a_Win", [128, KC, 1536], BF16); Winb = k.buf("a_Win")
                es2 = contextlib.ExitStack()
                with es2:
                    pieces = [(Win[:, kc, :], at_w_in[kind][0, kc * 128:(kc + 1) * 128, :]) for kc in range(KC)]
                    load_weight(es2, None, Winb, pieces)
                    k.barrier()
                nwr = TT(es, "a_nwr", [128, 20, 64]); nwrb = k.buf("a_nwr")
                nws = TT(es, "a_nws", [128, 2, 64])
                k.dma([(nws[:, 0, :], at_qn[kind][0:1, :].partition_broadcast(128)),
                       (nws[:, 1, :], at_kn[kind][0:1, :].partition_broadcast(128))], w=[nwrb])
                k.op(dve, lambda: nc.vector.tensor_scalar(out=nwr[:, 0:16, :], in0=bc(nws[:, 0:1, :], [128, 16, 64]), scalar1=0.125,
                                                          scalar2=None, op0=ALU.mult), r=[nwrb], w=[nwrb])
                k.op(dve, lambda: nc.vector.tensor_copy(out=nwr[:, 16:20, :], in_=bc(nws[:, 1:2, :], [128, 4, 64])), r=[nwrb], w=[nwrb])
                xa = [TT(es, "a_xa%d" % i, [128, D]) for i in range(2)]
                xab = [k.buf("a_xa%d" % i) for i in range(2)]
                nctx = NormCtx(es)
                mods = ModTiles(es, l, [3, 4], "a")
                hT = [TT(es, "a_hT%d" % i, [128, KC, 128], BF16) for i in range(2)]
                hTb = [k.buf("a_hT%d" % i) for i in range(2)]
                pq = [PP(es, "a_pq%d" % i, [128, 512]) for i in range(3)]
                pqb = [k.buf("a_pq%d" % i) for i in range(3)]
                ptq = PP(es, "a_ptq", [128, 8, 128], BF16); ptqb = k.buf("a_ptq")
                ptk = PP(es, "a_ptk", [128, 2, 128], BF16); ptkb = k.buf("a_ptk")
                NB_ = 2
                qk = [TT(es, "a_qk%d" % i, [128, 20, 64]) for i in range(NB_)]
                qkb = [k.buf("a_qk%d" % i) for i in range(NB_)]
                sq = [TT(es, "a_sq%d" % i, [128, 20, 64]) for i in range(NB_)]
                sqb = [k.buf("a_sq%d" % i) for i in range(NB_)]
                t2 = [TT(es, "a_t2%d" % i, [128, 20, 64]) for i in range(NB_)]
                t2b = [k.buf("a_t2%d" % i) for i in range(NB_)]
                qr = [TT(es, "a_qr%d" % i, [128, 1280], BF16) for i in range(NB_)]
                qrb = [k.buf("a_qr%d" % i) for i in range(NB_)]
                vb = [TT(es, "a_vb%d" % i, [128, 4, 64], BF16) for i in range(NB_)]
                vbb = [k.buf("a_vb%d" % i) for i in range(NB_)]
                sm = [TT(es, "a_sm%d" % i, [128, 3, 20]) for i in range(NB_)]
                smb = [k.buf("a_sm%d" % i) for i in range(NB_)]
                cs = [TT(es, "a_cs%d" % i, [128, 2, 64]) for i in range(NB_)]
                csb = [k.buf("a_cs%d" % i) for i in range(NB_)]
                qT = [TT(es, "a_qT%d" % i, [128, 10, 128], BF16) for i in range(NB_)]
                qTb = [k.buf("a_qT%d" % i) for i in range(NB_)]
                hbk = 32 if kind == 0 else 16
                nbk = 64 // (2 * hbk)
                def stageA(t):
                    i = t % 2
                    tok0 = t * 128
                    mods.need(tok0 // SLOT)
                    k.dma([(xa[i][:], src[tok0:tok0 + 128, :])], w=[xab[i]])
                    k.dma([(cs[i][:, 0, :], ropec[kind][tok0:tok0 + 128, :]), (cs[i][:, 1, :], ropes[kind][tok0:tok0 + 128, :])], w=[csb[i]])
                    nctx.part1(xa[i][:], xab[i], mods.t[4][:], mods.b[4], mods.t[3][:], mods.b[3])
                    nctx.part2(hT[i][:], hTb[i])

                stageA(0)
                for t in range(NT):
                    i = t % 2
                    tok0 = t * 128
                    if t + 1 < NT:
                        stageA(t + 1)
                    for n in range(3):
                        for kc in range(KC):
                            k.op(pe, lambda: nc.tensor.matmul(pq[n][:], hT[i][:, kc, :], Win[:, kc, n * 512:(n + 1) * 512],
                                                              start=(kc == 0), stop=(kc == KC - 1)), r=[hTb[i], Winb], w=[pqb[n]])
                    qkf = qk[i][:].rearrange("p h d -> p (h d)")
                    k.op(act, lambda: nc.scalar.copy(out=qkf[:, 0:512], in_=pq[0][:]), r=[pqb[0]], w=[qkb[i]])
                    k.op(act, lambda: nc.scalar.copy(out=qkf[:, 512:1024], in_=pq[1][:]), r=[pqb[1]], w=[qkb[i]])
                    k.op(act, lambda: nc.scalar.copy(out=qkf[:, 1024:1280], in_=pq[2][:, 0:256]), r=[pqb[2]], w=[qkb[i]])
                    k.op(act, lambda: nc.scalar.copy(out=vb[i][:].rearrange("p h d -> p (h d)"), in_=pq[2][:, 256:512]),
                         r=[pqb[2]], w=[vbb[i]])
                    k.dma([(Vd[:, :, t, :].rearrange("g p d -> p g d"), vb[i][:])], r=[vbb[i]])
                    k.op(act, lambda: nc.scalar.activation(out=sq[i][:], in_=qk[i][:], func=AF.Square), r=[qkb[i]], w=[sqb[i]])
                    k.op(dve, lambda: nc.vector.tensor_reduce(out=sm[i][:, 0, :], in_=sq[i][:], axis=AX.X, op=ALU.add), r=[sqb[i]], w=[smb[i]])
                    k.op(dve, lambda: nc.vector.tensor_scalar(out=sm[i][:, 1, :], in0=sm[i][:, 0, :], scalar1=1.0 / 64, scalar2=EPS,
                                                              op0=ALU.mult, op1=ALU.add), r=[smb[i]], w=[smb[i]])
                    k.op(pool, lambda: nc.gpsimd.tensor_tensor(out=sm[i][:, 2, :], in0=sm[i][:, 1, :], in1=nhalf[:, 0:20], op=ALU.pow),
                         r=[smb[i]], w=[smb[i]])
                    k.op(dve, lambda: nc.vector.tensor_tensor(out=qk[i][:], in0=qk[i][:], in1=bc(sm[i][:, 2, :].unsqueeze(2), [128, 20, 64]),
                                                              op=ALU.mult), r=[smb[i]], w=[qkb[i]])
                    k.op(pool, lambda: nc.gpsimd.tensor_tensor(out=qk[i][:], in0=qk[i][:], in1=nwr[:], op=ALU.mult), r=[nwrb], w=[qkb[i]])
                    k.op(dve, lambda: nc.vector.tensor_tensor(out=sq[i][:], in0=qk[i][:], in1=bc(cs[i][:, 0:1, :], [128, 20, 64]), op=ALU.mult),
                         r=[qkb[i], csb[i]], w=[sqb[i]])
                    q5 = qk[i][:].rearrange("p h (b two e) -> p h b two e", two=2, e=hbk)
                    t5 = t2[i][:].rearrange("p h (b two e) -> p h b two e", two=2, e=hbk)
                    s5 = cs[i][:, 1:2, :].rearrange("p o (b two e) -> p o b two e", two=2, e=hbk)
                    for half in range(2):
                        k.op(pool, lambda: nc.gpsimd.tensor_tensor(out=t5[:, :, :, half, :], in0=q5[:, :, :, 1 - half, :],
                                                                   in1=bc(s5[:, :, :, half, :], [128, 20, nbk, hbk]), op=ALU.mult),
                             r=[qkb[i], csb[i]], w=[t2b[i]])
                    k.op(dve, lambda: nc.vector.tensor_tensor(out=qr[i][:], in0=sq[i][:].rearrange("p h d -> p (h d)"),
                                                              in1=t2[i][:].rearrange("p h d -> p (h d)"), op=ALU.add),
                         r=[sqb[i], t2b[i]], w=[qrb[i]])
                    for j in range(8):
                        k.op(pe, lambda: nc.tensor.transpose(ptq[:, j, :], qr[i][:, j * 128:(j + 1) * 128], identb[:]), r=[qrb[i]], w=[ptqb])
                    for j in range(2):
                        k.op(pe, lambda: nc.tensor.transpose(ptk[:, j, :], qr[i][:, 1024 + j * 128:1024 + (j + 1) * 128], identb[:]),
                             r=[qrb[i]], w=[ptkb])
                    k.op(act, lambda: nc.scalar.copy(out=qT[i][:, 0:8, :], in_=ptq[:]), r=[ptqb], w=[qTb[i]])
                    k.op(act, lambda: nc.scalar.copy(out=qT[i][:, 8:10, :], in_=ptk[:]), r=[ptkb], w=[qTb[i]])
                    k.dma([(QTK[:, :, tok0:tok0 + 128].rearrange("j p t -> p j t"), qT[i][:])], r=[qTb[i]])
            return phase_done()

        def phase_att(l, kind):
            es = contextlib.ExitStack()
            with es:
                GS = 2 if kind == 1 else 1
                ps = [PP(es, "t_ps%d" % i, [128, 1024]) for i in range(2)]
                psb = [k.buf("t_ps%d" % i) for i in range(2)]
                po = [PP(es, "t_po%d" % i, [65, 512]) for i in range(2)]
                pob = [k.buf("t_po%d" % i) for i in range(2)]
                pbt = [PP(es, "t_pb%d" % i, [64, 512]) for i in range(2)]
                pbb = [k.buf("t_pb%d" % i) for i in range(2)]
                KT = [TT(es, "t_KT%d" % i, [128, T], BF16) for i in range(2)]
                KTb = [k.buf("t_KT%d" % i) for i in range(2)]
                V = [TT(es, "t_V%d" % i, [128, NT, 65], BF16) for i in range(2)]
                Vb = [k.buf("t_V%d" % i) for i in range(2)]
                am = TT(es, "t_am", [128, 64]); amb = k.buf("t_am")
                sw = TT(es, "t_sw", [128, NT * 2])
                k.dma([(am[:], amask_d[:, :]), (sw[:], swab_d[:, :])], w=[amb])
                es_s = TT(es, "t_ess", [65, 16])
                esx = TT(es, "t_esx", [65, 16, 128]); esb = k.buf("t_esx")
                if kind == 0:
                    k.dma([(es_s[64:65, :], swa_sink[0:1, :])], w=[esb])
                    k.op(act, lambda: nc.scalar.activation(out=es_s[64:65, :], in_=es_s[64:65, :], func=AF.Exp), r=[esb], w=[esb])
                    k.op(dve, lambda: nc.vector.tensor_copy(out=esx[64:65, :, :], in_=bc(es_s[64:65, :].unsqueeze(2), [1, 16, 128])),
                         r=[esb], w=[esb])
                for i in range(2):
                    k.op(pool, lambda: nc.gpsimd.memset(V[i][:, :, 64:65], 1.0), w=[Vb[i]])
                NQ = 4
                qt = [TT(es, "t_qt%d" % i, [128, 4, 128], BF16) for i in range(NQ)]
                qtb = [k.buf("t_qt%d" % i) for i in range(NQ)]
                NP = 3
                pt = [TT(es, "t_pt%d" % i, [128, 2, 4, 128], BF16) for i in range(NP)]
                ptb = [k.buf("t_pt%d" % i) for i in range(NP)]
                R = [TT(es, "t_R%d" % i, [65, 512]) for i in range(2)]
                Rb = [k.buf("t_R%d" % i) for i in range(2)]
                rec = [TT(es, "t_rec%d" % i, [64, 512]) for i in range(2)]
                recb = [k.buf("t_rec%d" % i) for i in range(2)]
                ot = [TT(es, "t_ot%d" % i, [64, 4, 128], BF16) for i in range(2)]
                otb = [k.buf("t_ot%d" % i) for i in range(2)]

                def load_kv(g):
                    i = g % 2
                    CK = min(T, 2048)
                    ksrc = QTK[8 + g // 2, (g % 2) * 64:(g % 2) * 64 + 64, :]
                    if kind == 0:
                        k.dma([(KT[i][0:64, c0:c0 + CK], ksrc[:, c0:c0 + CK]) for c0 in range(0, T, CK)], w=[KTb[i]])
                    else:
                        ks4 = ksrc.rearrange("d (n two t) -> d n two t", two=2, t=128)
                        NBH = NT // 2
                        CB = min(NBH, 16)
                        prs = []
                        for par in range(2):
                            kd3 = KT[i][par * 64:(par + 1) * 64, 0:T // 2].rearrange("d (n t) -> d n t", t=128)
                            for b0 in range(0, NBH, CB):
                                prs.append((kd3[:, b0:b0 + CB, :], ks4[:, b0:b0 + CB, par, :]))
                        k.dma(prs, w=[KTb[i]])
                    VB = min(NT, 8)
                    k.dma([(V[i][:, b0:b0 + VB, 0:64], Vd[g, :, b0:b0 + VB, :]) for b0 in range(0, NT, VB)], w=[Vb[i]])

                items = []
                for g in range(4):
                    for i in range(NT):
                        if kind == 0:
                            js = [j for j in (i - 1, i, i + 1) if 0 <= j < NT]
                            grs = [[j] for j in js]
                        else:
                            grs = [list(range(j0, j0 + GS)) for j0 in range(0, NT, GS)]
                        for n, gr in enumerate(grs):
                            items.append((g, i, gr, n == 0, n == len(grs) - 1))
                qslot = {}
                cn = {"q": 0}

                def stage_S(n):
                    g, i, gr, first, last = items[n]
                    gi = g % 2
                    if first:
                        if i == 0 and g == 0:
                            load_kv(0)
                        if i == 1 and g + 1 < 4:
                            load_kv(g + 1)
                        qi = cn["q"] % NQ
                        cn["q"] += 1
                        qslot[(g, i)] = qi
                        qsrc = QTK[2 * g:2 * g + 2, :, i * 128:(i + 1) * 128].rearrange("j (hh d) t -> d (j hh) t", hh=2)
                        if kind == 0:
                            k.dma([(qt[qi][0:64], qsrc)], w=[qtb[qi]])
                        else:
                            k.dma([(qt[qi][0:64], qsrc), (qt[qi][64:128], qsrc)], w=[qtb[qi]])
                    qi = qslot[(g, i)]
                    si = n % 2
                    for s_, j in enumerate(gr):
                        if kind == 0:
                            lhs = KT[gi][0:64, j * 128:(j + 1) * 128]
                            rhs = qt[qi][0:64].rearrange("d h t -> d (h t)")
                        else:
                            par = j % 2
                            lhs = KT[gi][par * 64:(par + 1) * 64, (j // 2) * 128:(j // 2 + 1) * 128]
                            rhs = qt[qi][par * 64:(par + 1) * 64].rearrange("d h t -> d (h t)")
                        k.op(pe, lambda: nc.tensor.matmul(ps[si][:, s_ * 512:(s_ + 1) * 512], lhs, rhs, start=True, stop=True),
                             r=[KTb[gi], qtb[qi]], w=[psb[si]], sig=(s_ == len(gr) - 1))

                def stage_E(n):
                    g, i, gr, first, last = items[n]
                    si = n % 2
                    pi = n % NP
                    j = gr[0]
                    if kind == 1:
                        col = (i // BPG) * 8 + (j // BPG)
                        bias = am[:, col:col + 1]
                    elif j == i:
                        bias = 0.0
                    elif j < i:
                        bias = sw[:, 2 * i:2 * i + 1]
                    else:
                        bias = sw[:, 2 * i + 1:2 * i + 2]
                    ng = len(gr)
                    ptf = pt[pi][:, 0:ng].rearrange("k s h t -> k (s h t)")
                    k.op(act, lambda: nc.scalar.activation(out=ptf, in_=ps[si][:, 0:ng * 512], func=AF.Exp, bias=bias), r=[psb[si], amb], w=[ptb[pi]])
                    if kind == 0 and j != i:
                        tri = TRIr if j < i else TRIi
                        k.op(dve, lambda: nc.vector.tensor_tensor(out=pt[pi][:, 0], in0=pt[pi][:, 0], in1=bc(tri[:].unsqueeze(1), [128, 4, 128]),
                                                                  op=ALU.mult), r=[cb], w=[ptb[pi]])

                def stage_PV(n):
                    g, i, gr, first, last = items[n]
                    gi = g % 2
                    pi = n % NP
                    oi = (g * NT + i) % 2
                    for s_, j in enumerate(gr):
                        k.op(pe, lambda: nc.tensor.matmul(po[oi][:], V[gi][:, j, :], pt[pi][:, s_].rearrange("k h t -> k (h t)"),
                                                          start=(first and s_ == 0), stop=(last and s_ == len(gr) - 1)),
                             r=[Vb[gi], ptb[pi]], w=[pob[oi]])

                def epi1(n):
                    g, i, gr, first, last = items[n]
                    oi = (g * NT + i) % 2
                    k.op(dve, lambda: nc.vector.tensor_copy(out=R[oi][:], in_=po[oi][:]), r=[pob[oi]], w=[Rb[oi]])
                    if kind == 0:
                        k.op(dve, lambda: nc.vector.tensor_tensor(out=R[oi][64:65, :], in0=R[oi][64:65, :],
                                                                  in1=esx[64:65, 4 * g:4 * g + 4, :].rearrange("p h t -> p (h t)"), op=ALU.add),
                             r=[esb], w=[Rb[oi]])

                def epi2(n):
                    g, i, gr, first, last = items[n]
                    oi = (g * NT + i) % 2
                    k.op(pe, lambda: nc.tensor.matmul(pbt[oi][:], E65[:], R[oi][:], start=True, stop=True), r=[Rb[oi], cb], w=[pbb[oi]])
                    k.op(dve, lambda: nc.vector.reciprocal(out=rec[oi][:], in_=pbt[oi][:]), r=[pbb[oi]], w=[recb[oi]])
                    k.op(dve, lambda: nc.vector.tensor_tensor(out=ot[oi][:].rearrange("d h t -> d (h t)"), in0=R[oi][0:64, :], in1=rec[oi][:],
                                                              op=ALU.mult), r=[Rb[oi], recb[oi]], w=[otb[oi]])
                    k.dma([(OTd[4 * g:4 * g + 4, :, i * 128:(i + 1) * 128].rearrange("h d t -> d h t"), ot[oi][:])], r=[otb[oi]])

                NI = len(items)
                stage_S(0)
                pend = None
                for n in range(NI):
                    if n + 1 < NI:
                        stage_S(n + 1)
                    if pend is not None:
                        epi2(pend)
                        pend = None
                    stage_E(n)
                    stage_PV(n)
                    if items[n][4]:
                        epi1(n)
                        pend = n
                if pend is not None:
                    epi2(pend)
            return phase_done()

        def phase_att_out(l, kind, dst):
            src = state["src"]
            es = contextlib.ExitStack()
            with es:
                Wo = TT(es, "o_Wo", [64, 16, D], BF16); Wob = k.buf("o_Wo")
                es2 = contextlib.ExitStack()
                with es2:
                    pieces = [(Wo[:, h, :], at_w_out[kind][0, h * 64:(h + 1) * 64, :]) for h in range(16)]
                    load_weight(es2, None, Wob, pieces)
                    k.barrier()
                mods = ModTiles(es, l, [5], "o")
                ep = Epilogue(es)
                oT = [TT(es, "o_oT%d" % i, [64, 16, 128], BF16) for i in range(3)]
                oTb = [k.buf("o_oT%d" % i) for i in range(3)]
                py = [[PP(es, "o_py%d_%d" % (i, h), [128, 512]) for h in range(2)] for i in range(2)]
                pyb = [[k.buf("o_py%d_%d" % (i, h)) for h in range(2)] for i in range(2)]
                for t in range(NT):
                    tok0 = t * 128
                    i3 = t % 3
                    i2 = t % 2
                    mods.need(tok0 // SLOT)
                    k.dma([(oT[i3][:], OTd[:, :, tok0:tok0 + 128].rearrange("h d t -> d h t"))], w=[oTb[i3]])
                    xi = ep.load(src, tok0)
                    for h in range(2):
                        for hd in range(16):
                            k.op(pe, lambda: nc.tensor.matmul(py[i2][h][:], oT[i3][:, hd, :], Wo[:, hd, h * 512:(h + 1) * 512],
                                                              start=(hd == 0), stop=(hd == 15)), r=[oTb[i3], Wob], w=[pyb[i2][h]])
                    ep.apply(xi, py[i2], pyb[i2], mods.t[5], mods.b[5], dst, tok0)
            state["src"] = dst
            return phase_done()

        def phase_ml_in(l, j):
            src = state["src"]
            es = contextlib.ExitStack()
            with es:
                Win = TT(es, "m_Win", [128, KC, 3104], BF16); Winb = k.buf("m_Win")
                es2 = contextlib.ExitStack()
                with es2:
                    pieces = []
                    for kc in range(KC):
                        pieces.append((Win[:, kc, 0:1552], ml_w_in[j, kc * 128:(kc + 1) * 128, 0:1552]))
                        pieces.append((Win[:, kc, 1552:3104], ml_w_in[j, kc * 128:(kc + 1) * 128, 1552:3104]))
                    load_weight(es2, None, Winb, pieces)
                    k.barrier()
                bg = TT(es, "m_bg", [128, 32]); bgb = k.buf("m_bg")
                k.dma([(bg[:], ml_bg[j:j + 1, :].partition_broadcast(128))], w=[bgb])
                xa = [TT(es, "mi_xa%d" % i, [128, D]) for i in range(2)]
                xab = [k.buf("mi_xa%d" % i) for i in range(2)]
                nctx = NormCtx(es)
                mods = ModTiles(es, l, [3, 4], "mi")
                hT = [TT(es, "mi_hT%d" % i, [128, KC, 128], BF16) for i in range(2)]
                hTb = [k.buf("mi_hT%d" % i) for i in range(2)]
                pq = [PP(es, "mi_pq%d" % i, [128, 512]) for i in range(4)]
                pqb = [k.buf("mi_pq%d" % i) for i in range(4)]
                ptr = PP(es, "mi_ptr", [128, 8, 128], BF16); ptrb = k.buf("mi_ptr")
                qkb_ = [TT(es, "mi_qkb%d" % i, [128, 1024], BF16) for i in range(2)]
                qkbb = [k.buf("mi_qkb%d" % i) for i in range(2)]
                kf = [TT(es, "mi_kf%d" % i, [128, 512]) for i in range(2)]
                kfb = [k.buf("mi_kf%d" % i) for i in range(2)]
                vt = [TT(es, "mi_vt%d" % i, [128, D], BF16) for i in range(2)]
                vtb = [k.buf("mi_vt%d" % i) for i in range(2)]
                so = [TT(es, "mi_so%d" % i, [128, D]) for i in range(2)]
                sob = [k.buf("mi_so%d" % i) for i in range(2)]
                gg = [TT(es, "mi_gg%d" % i, [128, 4, 32]) for i in range(2)]
                ggb = [k.buf("mi_gg%d" % i) for i in range(2)]
                qT = [TT(es, "mi_qT%d" % i, [128, 8, 128], BF16) for i in range(2)]
                qTb = [k.buf("mi_qT%d" % i) for i in range(2)]
                cn = 0
                def stageA(t):
                    i = t % 2
                    tok0 = t * 128
                    mods.need(tok0 // SLOT)
                    k.dma([(xa[i][:], src[tok0:tok0 + 128, :])], w=[xab[i]])
                    nctx.part1(xa[i][:], xab[i], mods.t[4][:], mods.b[4], mods.t[3][:], mods.b[3])
                    nctx.part2(hT[i][:], hTb[i])

                stageA(0)
                for t in range(NT):
                    i = t % 2
                    tok0 = t * 128
                    if t + 1 < NT:
                        stageA(t + 1)
                    for n in range(7):
                        q = cn % 4
                        cn += 1
                        w_ = 512 if n < 6 else 32
                        for kc in range(KC):
                            k.op(pe, lambda: nc.tensor.matmul(pq[q][:, 0:w_], hT[i][:, kc, :], Win[:, kc, n * 512:n * 512 + w_],
                                                              start=(kc == 0), stop=(kc == KC - 1)), r=[hTb[i], Winb], w=[pqb[q]])
                        SK = os.environ.get("ML_SKIP", "")
                        if n == 0 and "q" in SK: pass
                        elif n == 1 and "k" in SK: pass
                        elif n in (2, 3) and "v" in SK: pass
                        elif n in (4, 5) and "o" in SK: pass
                        elif n == 0:
                            k.op(act, lambda: nc.scalar.activation(out=qkb_[i][:, 0:512], in_=pq[q][:], func=AF.Copy, scale=0.125),
                                 r=[pqb[q]], w=[qkbb[i]])
                        elif n == 1:
                            k.op(dve, lambda: nc.vector.tensor_copy(out=kf[i][:], in_=pq[q][:]), r=[pqb[q]], w=[kfb[i]])
                            k.op(act, lambda: nc.scalar.copy(out=qkb_[i][:, 512:1024], in_=kf[i][:]), r=[kfb[i]], w=[qkbb[i]])
                            k.dma([(Ktm[tok0:tok0 + 128, :], kf[i][:])], r=[kfb[i]])
                        elif n in (2, 3):
                            k.op(dve, lambda: nc.vector.tensor_copy(out=vt[i][:, (n - 2) * 512:(n - 1) * 512], in_=pq[q][:]), r=[pqb[q]], w=[vtb[i]])
                            if n == 3:
                                k.dma([(Vm[tok0:tok0 + 128, :], vt[i][:])], r=[vtb[i]])
                        elif n in (4, 5):
                            k.op(act, lambda: nc.scalar.activation(out=so[i][:, (n - 4) * 512:(n - 3) * 512], in_=pq[q][:], func=AF.Sigmoid),
                                 r=[pqb[q]], w=[sob[i]])
                            if n == 5:
                                k.dma([(SOd[tok0:tok0 + 128, :], so[i][:])], r=[sob[i]])
                        elif "g" not in SK:
                            G = gg[i]
                            k.op(dve, lambda: nc.vector.tensor_tensor(out=G[:, 0, :], in0=pq[q][:, 0:32], in1=bg[:], op=ALU.add),
                                 r=[pqb[q], bgb], w=[ggb[i]])
                            k.op(act, lambda: nc.scalar.activation(out=G[:, 1, :], in_=G[:, 0, :], func=AF.Tanh, scale=1.0 / 15.0), r=[ggb[i]], w=[ggb[i]])
                            k.op(dve, lambda: nc.vector.tensor_scalar(out=G[:, 0, :], in0=G[:, 1, :], scalar1=15.0, scalar2=None, op0=ALU.mult),
                                 r=[ggb[i]], w=[ggb[i]])
                            k.op(act, lambda: nc.scalar.activation(out=G[:, 1, :], in_=G[:, 0, :], func=AF.Exp, scale=-1.0), r=[ggb[i]], w=[ggb[i]])
                            k.op(act, lambda: nc.scalar.activation(out=G[:, 2, :], in_=G[:, 1, :], func=AF.Ln, bias=1.0), r=[ggb[i]], w=[ggb[i]])
                            g4 = G[:, 0, :].rearrange("p (a b h) -> p a b h", a=2, b=2)
                            l4 = G[:, 2, :].rearrange("p (a b h) -> p a b h", a=2, b=2)
                            k.op(dve, lambda: nc.vector.tensor_scalar(out=g4[:, :, 1, :], in0=l4[:, :, 1, :], scalar1=-1.0, scalar2=None, op0=ALU.mult),
                                 r=[ggb[i]], w=[ggb[i]])
                            k.dma([(GTd[tok0:tok0 + 128, :], G[:, 0, :])], r=[ggb[i]])
                    if "t" in SK:
                        continue
                    for jj in range(8):
                        k.op(pe, lambda: nc.tensor.transpose(ptr[:, jj, :], qkb_[i][:, jj * 128:(jj + 1) * 128], identb[:]), r=[qkbb[i]], w=[ptrb])
                    k.op(act, lambda: nc.scalar.copy(out=qT[i][:], in_=ptr[:]), r=[ptrb], w=[qTb[i]])
                    k.dma([(MQK[:, :, tok0:tok0 + 128].rearrange("j p t -> p j t"), qT[i][:])], r=[qTb[i]])
            return phase_done()

        def phase_ml_scan(l, j, direction):
            fwd = direction == 0
            TRI = TRIi if fwd else TRIr
            CAP = CAPi if fwd else CAPr
            es = contextlib.ExitStack()
            with es:
                keep = TT(es, "s_keep", [128, NT]); keepb_ = k.buf("s_keep")
                k.dma([(keep[:], (keepf_d if fwd else keepb_d)[:, :])], w=[keepb_])
                nwr = TT(es, "s_nwr", [128, D]); nwrb = k.buf("s_nwr")
                k.dma([(nwr[:], ml_nw[j:j + 1, :].partition_broadcast(128))], w=[nwrb])
                C = TT(es, "s_C", [64, 8, 129]); Cb_ = k.buf("s_C")
                Cbf = TT(es, "s_Cbf", [64, 8, 129], BF16); Cbfb = k.buf("s_Cbf")
                k.op(dve, lambda: nc.vector.memset(C[:], 0.0), w=[Cb_])
                k.op(pool, lambda: nc.gpsimd.memset(Cbf[:], 0.0), w=[Cbfb])
                NB_ = 2
                mk = lambda nm, shape, dt=F32: ([TT(es, "s_%s%d" % (nm, i), shape, dt) for i in range(NB_)],
                                                [k.buf("s_%s%d" % (nm, i)) for i in range(NB_)])
                QT, QTb = mk("QT", [64, 8, 128], BF16)
                KTt, KTb = mk("KT", [64, 8, 128], BF16)
                Kt, Ktb = mk("Kt", [128, 8, 64])
                Va, Vab = mk("Va", [128, 8, 129], BF16)
                Gt, Gtb = mk("Gt", [128, 32])
                for i in range(NB_):
                    k.op(pool, lambda: nc.gpsimd.memset(Va[i][:, :, 128:129], 1.0), w=[Vab[i]])
                gs, gsb = mk("gs", [128, 6, 8])
                bT, bTb = mk("bT", [8, 128])
                arg, argb = mk("arg", [128, 8, 128])
                eb, ebb = mk("eb", [64, 8, 128])
                QTs, QTsb = mk("QTs", [64, 8, 128], BF16)
                ST, STb = mk("ST", [128, 8, 128], BF16)
                Kw, Kwb = mk("Kw", [128, 8, 64], BF16)
                hd, hdb = mk("hd", [128, 8, 128])
                dn, dnb = mk("dn", [128, 2, 8])
                hfl, hflb = mk("hfl", [128, D])
                sot, sotb = mk("sot", [128, D])
                sq2, sq2b = mk("sq2", [128, 8, 128])
                sm2, sm2b = mk("sm2", [128, 3, 8])
                hg, hgb = mk("hg", [128, D], BF16)
                psm = PP(es, "s_psm", [128, 512]); psmb = k.buf("s_psm")
                pbc = [PP(es, "s_pbc%d" % i, [128, 4, 128]) for i in range(2)]
                pbcb = [k.buf("s_pbc%d" % i) for i in range(2)]
                pss = [PP(es, "s_pss%d" % i, [128, 4, 128]) for i in range(2)]
                pssb = [k.buf("s_pss%d" % i) for i in range(2)]
                hsplit = [(0, 3), (3, 6), (6, 8)]
                pov = [PP(es, "s_po%d" % i, [128, 3, 129]) for i in range(3)]
                pob = [k.buf("s_po%d" % i) for i in range(3)]
                order = list(range(NT)) if fwd else list(range(NT - 1, -1, -1))
                gb = 0 if fwd else 16
                for n_, c in enumerate(order):
                    i = n_ % 2
                    tok0 = c * 128
                    k.dma([(QT[i][:], MQK[0:4, :, tok0:tok0 + 128].rearrange("j (hh d) t -> d (j hh) t", hh=2))], w=[QTb[i]])
                    k.dma([(KTt[i][:], MQK[4:8, :, tok0:tok0 + 128].rearrange("j (hh d) t -> d (j hh) t", hh=2))], w=[KTb[i]])
                    k.dma([(Kt[i][:].rearrange("p h d -> p (h d)"), Ktm[tok0:tok0 + 128, :])], w=[Ktb[i]])
                    k.dma([(Va[i][:, :, 0:128], Vm[tok0:tok0 + 128, :].rearrange("p (h d) -> p h d", h=8))], w=[Vab[i]])
                    k.dma([(Gt[i][:], GTd[tok0:tok0 + 128, :])], w=[Gtb[i]])
                    if not fwd:
                        k.dma([(hfl[i][:], Hf[tok0:tok0 + 128, :])], w=[hflb[i]])
                        k.dma([(sot[i][:], SOd[tok0:tok0 + 128, :])], w=[sotb[i]])
                    ig = Gt[i][:, gb:gb + 8]
                    lf = Gt[i][:, gb + 8:gb + 16]
                    S_ = gs[i]
                    k.op(pe, lambda: nc.tensor.matmul(psm[:, 0:8], TRI[:], lf, start=True, stop=True), r=[Gtb[i], cb], w=[psmb])
                    k.op(pe, lambda: nc.tensor.matmul(psm[:, 8:16], ONESf[:], lf, start=True, stop=True), r=[Gtb[i], cb], w=[psmb])
                    k.op(pe, lambda: nc.tensor.matmul(psm[0:8, 128:256], lf, TRI[:], start=True, stop=True), r=[Gtb[i], cb], w=[psmb])
                    k.op(dve, lambda: nc.vector.tensor_copy(out=S_[:, 0, :], in_=psm[:, 0:8]), r=[psmb], w=[gsb[i]])
                    k.op(dve, lambda: nc.vector.tensor_copy(out=S_[:, 4, :], in_=psm[:, 8:16]), r=[psmb], w=[gsb[i]])
                    k.op(dve, lambda: nc.vector.tensor_copy(out=bT[i][:], in_=psm[0:8, 128:256]), r=[psmb], w=[bTb[i]])
                    k.op(dve, lambda: nc.vector.tensor_tensor(out=S_[:, 1, :], in0=ig, in1=S_[:, 0, :], op=ALU.subtract), r=[Gtb[i]], w=[gsb[i]])
                    k.op(dve, lambda: nc.vector.tensor_tensor(out=S_[:, 2, :], in0=S_[:, 1, :], in1=S_[:, 4, :], op=ALU.add), w=[gsb[i]])
                    k.op(act, lambda: nc.scalar.activation(out=S_[:, 2, :], in_=S_[:, 2, :], func=AF.Exp), w=[gsb[i]])
                    k.op(act, lambda: nc.scalar.activation(out=S_[:, 3, :], in_=S_[:, 4, :], func=AF.Exp), w=[gsb[i]])
                    for h in range(8):
                        k.op(pe, lambda: nc.tensor.matmul(pbc[h // 4][:, h % 4, :], Sel[0:8, h, :], bT[i][:], start=True, stop=True),
                             r=[bTb[i], cb], w=[pbcb[h // 4]])
                    for hh in range(2):
                        for h4 in range(4):
                            h = 4 * hh + h4
                            k.op(dve, lambda: nc.vector.scalar_tensor_tensor(out=arg[i][:, h, :], in0=pbc[hh][:, h4, :], scalar=S_[:, 1, h:h + 1],
                                                                             in1=CAP[:], op0=ALU.add, op1=ALU.min),
                                 r=[pbcb[hh], gsb[i], cb], w=[argb[i]])
                        k.op(dve, lambda: nc.vector.tensor_copy(out=eb[i][:, 4 * hh:4 * hh + 4, :], in_=pbc[hh][0:64]),
                             r=[pbcb[hh]], w=[ebb[i]])
                        k.op(act, lambda: nc.scalar.activation(out=eb[i][:, 4 * hh:4 * hh + 4, :], in_=eb[i][:, 4 * hh:4 * hh + 4, :], func=AF.Exp),
                             w=[ebb[i]])
                    k.op(act, lambda: nc.scalar.activation(out=arg[i][:], in_=arg[i][:], func=AF.Exp), w=[argb[i]])
                    k.op(dve, lambda: nc.vector.tensor_tensor(out=QTs[i][:], in0=QT[i][:], in1=eb[i][:], op=ALU.mult), r=[QTb[i], ebb[i]], w=[QTsb[i]])
                    for h in range(8):
                        k.op(pe, lambda: nc.tensor.matmul(pss[h // 4][:, h % 4, :], KTt[i][:, h, :], QT[i][:, h, :], start=True, stop=True),
                             r=[KTb[i], QTb[i]], w=[pssb[h // 4]])
                    for hh in range(2):
                        k.op(dve, lambda: nc.vector.tensor_tensor(out=ST[i][:, 4 * hh:4 * hh + 4, :], in0=pss[hh][:], in1=arg[i][:, 4 * hh:4 * hh + 4, :],
                                                                  op=ALU.mult), r=[pssb[hh], argb[i]], w=[STb[i]])
                    k.op(dve, lambda: nc.vector.tensor_tensor(out=Kw[i][:], in0=Kt[i][:], in1=bc(S_[:, 2, :].unsqueeze(2), [128, 8, 64]), op=ALU.mult),
                         r=[Ktb[i], gsb[i]], w=[Kwb[i]])
                    for h in range(8):
                        pi_ = 0 if h < 3 else (1 if h < 6 else 2)
                        hl = h - hsplit[pi_][0]
                        k.op(pe, lambda: nc.tensor.matmul(pov[pi_][:, hl, :], ST[i][:, h, :], Va[i][:, h, :], start=True, stop=False),
                             r=[STb[i], Vab[i]], w=[pob[pi_]])
                        k.op(pe, lambda: nc.tensor.matmul(pov[pi_][:, hl, :], QTs[i][:, h, :], Cbf[:, h, :], start=False, stop=True),
                             r=[QTsb[i], Cbfb], w=[pob[pi_]])
                    for pi_, (h0, h1) in enumerate(hsplit):
                        nh = h1 - h0
                        k.op(dve, lambda: nc.vector.tensor_copy(out=dn[i][:, 1, h0:h1], in_=pov[pi_][:, 0:nh, 128]), r=[pob[pi_]], w=[dnb[i]])
                    k.op(dve, lambda: nc.vector.tensor_tensor(out=dn[i][:, 0, :], in0=dn[i][:, 1, :], in1=dn[i][:, 1, :], op=ALU.mult), w=[dnb[i]])
                    k.op(dve, lambda: nc.vector.tensor_scalar(out=dn[i][:, 0, :], in0=dn[i][:, 0, :], scalar1=1.0, scalar2=None, op0=ALU.max), w=[dnb[i]])
                    k.op(pool, lambda: nc.gpsimd.tensor_tensor(out=dn[i][:, 1, :], in0=dn[i][:, 0, :], in1=nhalf[:, 0:8], op=ALU.pow), w=[dnb[i]])
                    for pi_, (h0, h1) in enumerate(hsplit):
                        nh = h1 - h0
                        k.op(dve, lambda: nc.vector.tensor_tensor(out=hd[i][:, h0:h1, :], in0=pov[pi_][:, 0:nh, 0:128],
                                                                  in1=bc(dn[i][:, 1, h0:h1].unsqueeze(2), [128, nh, 128]), op=ALU.mult),
                             r=[pob[pi_], dnb[i]], w=[hdb[i]])
                    hdf = hd[i][:].rearrange("p h d -> p (h d)")
                    if fwd:
                        k.dma([(Hf[tok0:tok0 + 128, :], hdf)], r=[hdb[i]])
                    else:
                        k.op(dve, lambda: nc.vector.tensor_tensor(out=hdf, in0=hdf, in1=hfl[i][:], op=ALU.add), r=[hflb[i]], w=[hdb[i]])
                        k.op(act, lambda: nc.scalar.activation(out=sq2[i][:], in_=hd[i][:], func=AF.Square), r=[hdb[i]], w=[sq2b[i]])
                        k.op(dve, lambda: nc.vector.tensor_reduce(out=sm2[i][:, 0, :], in_=sq2[i][:], axis=AX.X, op=ALU.add), r=[sq2b[i]], w=[sm2b[i]])
                        k.op(dve, lambda: nc.vector.tensor_scalar(out=sm2[i][:, 1, :], in0=sm2[i][:, 0, :], scalar1=1.0 / 128, scalar2=EPS,
                                                                  op0=ALU.mult, op1=ALU.add), w=[sm2b[i]])
                        k.op(pool, lambda: nc.gpsimd.tensor_tensor(out=sm2[i][:, 2, :], in0=sm2[i][:, 1, :], in1=nhalf[:, 0:8], op=ALU.pow), w=[sm2b[i]])
                        k.op(dve, lambda: nc.vector.tensor_tensor(out=hd[i][:], in0=hd[i][:], in1=bc(sm2[i][:, 2, :].unsqueeze(2), [128, 8, 128]),
                                                                  op=ALU.mult), r=[sm2b[i]], w=[hdb[i]])
                        k.op(dve, lambda: nc.vector.tensor_tensor(out=hdf, in0=hdf, in1=nwr[:], op=ALU.mult), r=[nwrb], w=[hdb[i]])
                        k.op(dve, lambda: nc.vector.tensor_tensor(out=hg[i][:], in0=hdf, in1=sot[i][:], op=ALU.mult), r=[hdb[i], sotb[i]], w=[hgb[i]])
                        k.dma([(HGd[tok0:tok0 + 128, :], hg[i][:])], r=[hgb[i]])
                    for h in range(8):
                        pi_ = 0 if h < 3 else (1 if h < 6 else 2)
                        hl = h - hsplit[pi_][0]
                        k.op(pe, lambda: nc.tensor.matmul(pov[pi_][0:64, hl, :], Kw[i][:, h, :], Va[i][:, h, :], start=True, stop=True),
                             r=[Kwb[i], Vab[i]], w=[pob[pi_]])
                    k.op(dve, lambda: nc.vector.tensor_tensor(out=C[:], in0=C[:], in1=bc(S_[0:64, 3, :].unsqueeze(2), [64, 8, 129]), op=ALU.mult),
                         r=[gsb[i]], w=[Cb_])
                    for pi_, (h0, h1) in enumerate(hsplit):
                        nh = h1 - h0
                        k.op(dve, lambda: nc.vector.tensor_tensor(out=C[:, h0:h1, :], in0=C[:, h0:h1, :], in1=pov[pi_][0:64, 0:nh, :], op=ALU.add),
                             r=[pob[pi_]], w=[Cb_])
                    k.op(dve, lambda: nc.vector.tensor_scalar(out=C[:], in0=C[:], scalar1=keep[0:64, c:c + 1], scalar2=None, op0=ALU.mult),
                         r=[keepb_], w=[Cb_])
                    k.op(act, lambda: nc.scalar.copy(out=Cbf[:], in_=C[:]), r=[Cb_], w=[Cbfb])
            return phase_done()

        def phase_ml_out(l, j, dst):
            src = state["src"]
            es = contextlib.ExitStack()
            with es:
                Wo = TT(es, "mo_Wo", [128, KC, D], BF16); Wob = k.buf("mo_Wo")
                es2 = contextlib.ExitStack()
                with es2:
                    pieces = [(Wo[:, kc, :], ml_w_out[j, kc * 128:(kc + 1) * 128, :]) for kc in range(KC)]
                    load_weight(es2, None, Wob, pieces)
                    k.barrier()
                mods = ModTiles(es, l, [5], "mo")
                ep = Epilogue(es)
                hgt = [TT(es, "mo_hg%d" % i, [128, D], BF16) for i in range(2)]
                hgtb = [k.buf("mo_hg%d" % i) for i in range(2)]
                hT = [TT(es, "mo_hT%d" % i, [128, KC, 128], BF16) for i in range(2)]
                hTb = [k.buf("mo_hT%d" % i) for i in range(2)]
                pT = [PP(es, "mo_pT%d" % i, [128, 8, 128], BF16) for i in range(2)]
                pTb = [k.buf("mo_pT%d" % i) for i in range(2)]
                py = [[PP(es, "mo_py%d_%d" % (i, h), [128, 512]) for h in range(2)] for i in range(2)]
                pyb = [[k.buf("mo_py%d_%d" % (i, h)) for h in range(2)] for i in range(2)]
                for t in range(NT):
                    tok0 = t * 128
                    i = t % 2
                    mods.need(tok0 // SLOT)
                    k.dma([(hgt[i][:], HGd[tok0:tok0 + 128, :])], w=[hgtb[i]])
                    xi = ep.load(src, tok0)
                    for kc in range(KC):
                        k.op(pe, lambda: nc.tensor.transpose(pT[i][:, kc, :], hgt[i][:, kc * 128:(kc + 1) * 128], identb[:]), r=[hgtb[i]], w=[pTb[i]])
                    k.op(act, lambda: nc.scalar.copy(out=hT[i][:], in_=pT[i][:]), r=[pTb[i]], w=[hTb[i]])
                    for h in range(2):
                        for kc in range(KC):
                            k.op(pe, lambda: nc.tensor.matmul(py[i][h][:], hT[i][:, kc, :], Wo[:, kc, h * 512:(h + 1) * 512],
                                                              start=(kc == 0), stop=(kc == KC - 1)), r=[hTb[i], Wob], w=[pyb[i][h]])
                    ep.apply(xi, py[i], pyb[i], mods.t[5], mods.b[5], dst, tok0)
            state["src"] = dst
            return phase_done()

        def program():
            if phase_mod():
                return
            for l in range(NL):
                last = (l == NL - 1)
                if phase_ffn(l, 0, xs):
                    return
                kind, j = l % 3, l // 3
                if kind == 0:
                    if phase_ml_in(l, j): return
                    if phase_ml_scan(l, j, 0): return
                    if phase_ml_scan(l, j, 1): return
                    if phase_ml_out(l, j, xs): return
                else:
                    if phase_att_in(l, kind - 1): return
                    if phase_att(l, kind - 1): return
                    if phase_att_out(l, kind - 1, xs): return
                if phase_ffn(l, 1, y_out if last else xs):
                    return
        program()
        if state["src"] is not y_out:
            es = contextlib.ExitStack()
            with es:
                tb = [TT(es, "cp%d" % i, [128, D]) for i in range(2)]
                tbb = [k.buf("cp%d" % i) for i in range(2)]
                for t in range(NT):
                    k.dma([(tb[t % 2][:], state["src"][t * 128:(t + 1) * 128, :])], w=[tbb[t % 2]])
                    k.dma([(y_out[t * 128:(t + 1) * 128, :], tb[t % 2][:])], r=[tbb[t % 2]])
            k.barrier()
    return nc


def rope_np(pos, dim):
    inv = (np.float32(10000.0) ** (-np.arange(0, dim, 2, dtype=np.float32) / np.float32(dim))).astype(np.float32)
    ang = pos.astype(np.float32)[:, None] * inv[None, :]
    ang = np.concatenate([ang, ang], axis=-1)
    return np.cos(ang).astype(np.float32), np.sin(ang).astype(np.float32)


def core_tables(T, S):
    NT = T // 128
    bps = S // 128
    pos = np.arange(T) % S
    c, s = rope_np(pos, 64)
    s_sw = np.concatenate([-s[:, :32], s[:, 32:]], axis=1)
    rc, rs = rope_np(pos // 64, 32)
    cc, cs_ = rope_np(pos % 64, 32)
    c_ax = np.concatenate([rc, cc], axis=1)
    s_ax = np.concatenate([-rs[:, :16], rs[:, 16:], -cs_[:, :16], cs_[:, 16:]], axis=1)
    blk = np.arange(NT)
    keepf = ((blk + 1) % bps != 0).astype(np.float32)
    keepb = (blk % bps != 0).astype(np.float32)
    swab = np.zeros((NT, 2), np.float32)
    swab[blk % bps == 0, 0] = NEG
    swab[(blk + 1) % bps == 0, 1] = NEG
    slot_seq = (np.arange(8) * (T // 8)) // S
    amask = np.where(slot_seq[:, None] == slot_seq[None, :], 0.0, NEG).astype(np.float32)
    rep = lambda a: np.ascontiguousarray(np.broadcast_to(a.reshape(1, -1), (128, a.size))).astype(np.float32)
    return {
        "rope_swa_c": c, "rope_swa_s": s_sw.astype(np.float32), "rope_ax_c": c_ax.astype(np.float32), "rope_ax_s": s_ax.astype(np.float32),
        "keepf": rep(keepf), "keepb": rep(keepb), "swab": rep(swab), "amask": rep(amask),
    }


WNAMES = ["ffn_w13", "ffn_w2", "ada_w", "ada_b", "norm_w", "mlstm_w_in", "mlstm_b_gate", "mlstm_norm_w", "mlstm_w_out",
          "swa_w_in", "swa_q_norm", "swa_k_norm", "swa_sink", "swa_w_out", "axial_w_in", "axial_q_norm", "axial_k_norm", "axial_w_out"]


def run_streams(streams, weights, T, NL=4, stop=None, ncores=8):
    nc = build(T, NL, stop)
    w = {n: np.ascontiguousarray(np.asarray(weights[n], dtype=np.float32)) for n in WNAMES}
    in_maps = []
    for c in range(ncores):
        x, c8, S = streams[c] if c < len(streams) else streams[-1]
        m = {"x": np.ascontiguousarray(x, dtype=np.float32), "c8": np.ascontiguousarray(c8, dtype=np.float32)}
        m.update(core_tables(T, S))
        m.update(w)
        in_maps.append(m)
    if os.environ.get("KTRACE"):
        res = run_bass_kernel_spmd(nc, in_maps, core_ids=list(range(ncores)), trace=True)
        print("EXEC_TIME_NS", res.exec_time_ns)
    else:
        res = run_bass_kernel_spmd(nc, in_maps, core_ids=list(range(ncores)))
    return [res.results[c]["y"] for c in range(len(streams))]


def kernel(x_prompt, x_sample, c_prompt, c_sample, **weights):
    x_prompt = np.asarray(x_prompt, dtype=np.float32)
    x_sample = np.asarray(x_sample, dtype=np.float32)
    c_prompt = np.asarray(c_prompt, dtype=np.float32)
    c_sample = np.asarray(c_sample, dtype=np.float32)
    T = 16384
    streams = []
    for b in range(2):
        streams.append((x_prompt[b], np.ascontiguousarray(np.broadcast_to(c_prompt[b:b + 1], (8, D))), 16384))
    for j in range(4):
        streams.append((x_sample[8 * j:8 * j + 8].reshape(T, D), c_sample[8 * j:8 * j + 8], 2048))
    ys = run_streams(streams, weights, T)
    y_prompt = np.stack([ys[0], ys[1]], axis=0).astype(np.float32)
    y_sample = np.concatenate([ys[2 + j].reshape(8, 2048, D) for j in range(4)], axis=0).astype(np.float32)
    return (y_prompt, y_sample)
```
